# Optimizing a Trainium2 kernel written in Bass

```python
import math
import jax, jax.numpy as jnp
from jax import lax
import numpy as np

D_MODEL = 2048
BATCH = 4
SEQ = 2048
DEPTH = 4
DEC_BATCH = 8
DEC_SEQ = 1
PAST_LEN = 16384
PAGE_SIZE = 128

HEAD_DIM = 128
N_A = DEPTH // 2
N_B = DEPTH - N_A
D_A = 3 * D_MODEL // 4
N_HEADS = D_A // HEAD_DIM
N_KV = 4
GROUP = N_HEADS // N_KV
MEM_HEADS = 4
MEM_W = MEM_HEADS * HEAD_DIM
D_MIX = D_A + MEM_W
N_MEM = 256
D_FF = 128 * ((8 * D_MODEL // 3 + 127) // 128)
CONV_W = 3
CMP_STRIDE = 16
CMP_LEN = 2 * CMP_STRIDE
SEL_BLOCK = 64
TOPK = 16
WINDOW = 512
WIN_QBLOCK = 128
SEL_Q_CHUNK = 32
N_BRANCH = 3
N_KV_KINDS = 4
ALPHA = (2 * DEPTH) ** 0.25
BETA = (8 * DEPTH) ** -0.25
LN_EPS = 1e-5
NEG_INF = -1e30
FORCE_SCORE = 1e9

kernel_name = 'yoco_shortconv_nsa_decoder_step'


def layer_norm(x, g, b):
    xf = x.astype(jnp.float32)
    mu = xf.mean(-1, keepdims=True)
    var = jnp.square(xf - mu).mean(-1, keepdims=True)
    return ((xf - mu) * lax.rsqrt(var + LN_EPS)).astype(x.dtype) * g + b


def masked_softmax(s, mask, axis=-1):
    s = jnp.where(mask, s.astype(jnp.float32), NEG_INF)
    return jnp.where(mask, jax.nn.softmax(s, axis=axis), 0.0)


def causal_conv(v, w, prev):
    T = v.shape[1]
    full = jnp.concatenate([prev, v], axis=1)
    out = w[0] * full[:, 0:T]
    for i in range(1, CONV_W):
        out = out + w[i] * full[:, i:i + T]
    return out, full[:, T:]


def mem_attention(qm, mem_k, mem_v):
    B, T = qm.shape[:2]
    q = qm.reshape(B, T, MEM_HEADS, HEAD_DIM) * HEAD_DIM ** -0.5
    s = jnp.einsum('bthd,bmhd->bthm', q, mem_k)
    p = jax.nn.softmax(s.astype(jnp.float32), axis=-1).astype(mem_v.dtype)
    return jnp.einsum('bthm,bmhd->bthd', p, mem_v).reshape(B, T, MEM_W)


def compress_blocks(rows, pos_emb, w):
    B, T = rows.shape[:2]
    nh = T // CMP_STRIDE
    halves = rows[:, :nh * CMP_STRIDE].reshape(B, nh, CMP_STRIDE, N_KV, HEAD_DIM)
    first = jnp.einsum('bnsgd,sde->bnge', halves[:, :-1], w[:CMP_STRIDE])
    second = jnp.einsum('bnsgd,sde->bnge', halves[:, 1:], w[CMP_STRIDE:])
    pos_term = jnp.einsum('ld,lde->e', pos_emb, w)
    return first + second + pos_term


def to_sel_blocks(rows):
    B, T = rows.shape[:2]
    n = -(-T // SEL_BLOCK)
    rows = jnp.pad(rows, ((0, 0), (0, n * SEL_BLOCK - T), (0, 0), (0, 0)))
    return rows.reshape(B, n, SEL_BLOCK, N_KV, HEAD_DIM).transpose(0, 3, 1, 2, 4)


def key_side(k_cmp_rows, v_cmp_rows, k_sel_rows, v_sel_rows, cmp_pos, w_cmp):
    return (compress_blocks(k_cmp_rows, cmp_pos[0], w_cmp[0]),
            compress_blocks(v_cmp_rows, cmp_pos[1], w_cmp[1]),
            to_sel_blocks(k_sel_rows), to_sel_blocks(v_sel_rows))


def prompt_window_bands(k_rows, v_rows):
    T = k_rows.shape[1]
    nb = T // WIN_QBLOCK
    band = jnp.arange(nb)[:, None] * WIN_QBLOCK + jnp.arange(WINDOW + WIN_QBLOCK)[None, :]
    pad = ((0, 0), (WINDOW, 0), (0, 0), (0, 0))
    wk = jnp.pad(k_rows, pad)[:, band]
    wv = jnp.pad(v_rows, pad)[:, band]
    return (wk, wv, band - WINDOW)


def selected_attention(qg, idx, valid, qpos, sel_kb, sel_vb):
    B, T = qg.shape[:2]
    qc = math.gcd(T, SEL_Q_CHUNK)
    nc = T // qc

    def chunks(a):
        return jnp.moveaxis(a.reshape((B, nc, qc) + a.shape[2:]), 1, 0)

    bi = jnp.arange(B)[:, None, None, None]
    gi = jnp.arange(N_KV)[None, None, :, None]
    offs = jnp.arange(SEL_BLOCK)

    def one(args):
        q_c, idx_c, val_c, pos_c = args
        kg = sel_kb[bi, gi, idx_c]
        vg = sel_vb[bi, gi, idx_c]
        s = jnp.einsum('bqgrd,bqgkld->bqgrkl', q_c, kg)
        kpos = idx_c[..., None] * SEL_BLOCK + offs
        mask = val_c[..., None] & (kpos <= pos_c[None, :, None, None, None])
        p = masked_softmax(s, mask[:, :, :, None], axis=(-2, -1)).astype(vg.dtype)
        return jnp.einsum('bqgrkl,bqgkld->bqgrd', p, vg)

    out = lax.map(one, (chunks(qg), chunks(idx), chunks(valid), qpos.reshape(nc, qc)))
    return jnp.moveaxis(out, 0, 1).reshape(qg.shape)


def window_attention(qg, win_k, win_v, win_kpos, qpos):
    B, T = qg.shape[:2]
    nb = win_k.shape[1]
    qb = qg.reshape(B, nb, T // nb, N_KV, GROUP, HEAD_DIM)
    qp = qpos.reshape(nb, T // nb)[:, :, None]
    kp = win_kpos[:, None, :]
    mask = (kp <= qp) & (kp > qp - WINDOW) & (kp >= 0)
    s = jnp.einsum('bnqgrd,bnkgd->bnqgrk', qb, win_k)
    p = masked_softmax(s, mask[None, :, :, None, None, :]).astype(win_v.dtype)
    return jnp.einsum('bnqgrk,bnkgd->bnqgrd', p, win_v).reshape(qg.shape)


def nsa_mixer(q, gate, cmp_k, cmp_v, sel_kb, sel_vb, win_k, win_v, win_kpos, qpos):
    B, T = q.shape[:2]
    qg = q.reshape(B, T, N_KV, GROUP, HEAD_DIM) * HEAD_DIM ** -0.5
    n_cmp = cmp_k.shape[1]
    cmp_start = jnp.arange(n_cmp) * CMP_STRIDE
    cmask = (cmp_start + CMP_LEN - 1)[None, :] <= qpos[:, None]
    s = jnp.einsum('btgrd,bngd->btgrn', qg, cmp_k)
    p_cmp = masked_softmax(s, cmask[None, :, None, None, :])
    o_cmp = jnp.einsum('btgrn,bngd->btgrd', p_cmp.astype(cmp_v.dtype), cmp_v)
    n_slc = sel_kb.shape[2]
    slc_start = jnp.arange(n_slc) * SEL_BLOCK
    overlap = ((cmp_start[:, None] <= slc_start[None, :] + SEL_BLOCK - 1)
               & (cmp_start[:, None] + CMP_LEN - 1 >= slc_start[None, :])).astype(jnp.float32)
    imp = jnp.einsum('btgrn,nj->btgj', p_cmp, overlap)
    blk = jnp.arange(n_slc)[None, :]
    cur = (qpos // SEL_BLOCK)[:, None]
    elig = slc_start[None, :] <= qpos[:, None]
    forced = (blk == 0) | (blk == cur) | (blk == cur - 1)
    score = jnp.where(forced[None, :, None, :], FORCE_SCORE,
                      jnp.where(elig[None, :, None, :], imp, -FORCE_SCORE))
    _, idx = lax.top_k(score, min(TOPK, n_slc))
    valid = jnp.take_along_axis(jnp.broadcast_to(elig[None, :, None, :], score.shape), idx, axis=-1)
    o_sel = selected_attention(qg, idx, valid, qpos, sel_kb, sel_vb)
    o_win = window_attention(qg, win_k, win_v, win_kpos, qpos)
    g = jax.nn.sigmoid(gate.astype(jnp.float32)).astype(q.dtype).reshape(B, T, N_BRANCH, N_KV, GROUP)[..., None]
    o = g[:, :, 0] * o_cmp + g[:, :, 1] * o_sel + g[:, :, 2] * o_win
    return o.reshape(B, T, N_HEADS * HEAD_DIM)


def layer_tail(x, l, y_tok, qm, mem_k, mem_v, ffn_prev, ln_g, ln_b, w_o, w_up, ffn_conv_w, w_down):
    y = jnp.concatenate([y_tok, mem_attention(qm, mem_k, mem_v)], axis=-1) @ w_o[l]
    x = layer_norm(ALPHA * x + y, ln_g[l, 0], ln_b[l, 0])
    up = x @ w_up[l]
    z, new_prev = causal_conv(up[..., :D_FF], ffn_conv_w[l], ffn_prev)
    f = (jax.nn.silu(z) * up[..., D_FF:]) @ w_down[l]
    x = layer_norm(ALPHA * x + f, ln_g[l, 1], ln_b[l, 1])
    return x, new_prev


def a_layers(x, mem_k, mem_v, conv_prev, ffn_prev, ln_g, ln_b, w_in_a, conv_a_w, w_o, w_up, ffn_conv_w, w_down):
    conv_states, ffn_states = [], []
    for l in range(N_A):
        p = x @ w_in_a[l]
        u, b_gate, c_gate, qm = (p[..., :D_A], p[..., D_A:2 * D_A], p[..., 2 * D_A:3 * D_A], p[..., 3 * D_A:])
        c, cs = causal_conv(c_gate * u, conv_a_w[l], conv_prev[l])
        x, fs = layer_tail(x, l, b_gate * c, qm, mem_k[l], mem_v[l], ffn_prev[l],
                           ln_g, ln_b, w_o, w_up, ffn_conv_w, w_down)
        conv_states.append(cs)
        ffn_states.append(fs)
    return x, conv_states, ffn_states


def b_layers(x, qpos, keys, mem_k, mem_v, ffn_prev, ln_g, ln_b, w_in_b, w_o, w_up, ffn_conv_w, w_down):
    hq = N_HEADS * HEAD_DIM
    hg = hq + N_BRANCH * N_HEADS
    ffn_states = []
    for j in range(N_B):
        l = N_A + j
        p = x @ w_in_b[j]
        y_tok = nsa_mixer(p[..., :hq], p[..., hq:hg], *keys, qpos)
        x, fs = layer_tail(x, l, y_tok, p[..., hg:], mem_k[l], mem_v[l], ffn_prev[l],
                           ln_g, ln_b, w_o, w_up, ffn_conv_w, w_down)
        ffn_states.append(fs)
    return x, ffn_states


def setup_inputs(seed: int = 0) -> dict:
    key = jax.random.key(seed)
    ks = jax.random.split(key, 24)
    f32 = jnp.float32

    def nrm(k, shape, scale=1.0):
        return jax.random.normal(k, shape, f32) * scale

    n_pages = PAST_LEN // PAGE_SIZE
    n_pool = (5 * DEC_BATCH * n_pages + 3) // 4
    wb = min(WINDOW, PAST_LEN)
    page_table = jax.random.permutation(ks[7], n_pool)[:DEC_BATCH * n_pages].reshape(DEC_BATCH, n_pages).astype(jnp.int32)
    d_in_a = 3 * D_A + MEM_W
    d_in_b = N_HEADS * HEAD_DIM + N_BRANCH * N_HEADS + MEM_W
    return {
        'x_prompt': nrm(ks[0], (BATCH, SEQ, D_MODEL)),
        'x_sample': nrm(ks[1], (DEC_BATCH, DEC_SEQ, D_MODEL)),
        'cache_kv': nrm(ks[2], (n_pool, PAGE_SIZE, N_KV_KINDS, N_KV, HEAD_DIM)),
        'cache_win': nrm(ks[3], (DEC_BATCH, wb, 2, N_KV, HEAD_DIM)),
        'cache_mem': nrm(ks[4], (DEPTH, DEC_BATCH, N_MEM, 2, MEM_HEADS, HEAD_DIM)),
        'state_conv_mix': nrm(ks[5], (N_A, DEC_BATCH, CONV_W - 1, D_A)),
        'state_conv_ffn': nrm(ks[6], (DEPTH, DEC_BATCH, CONV_W - 1, D_FF)),
        'page_table': page_table,
        'mem_prompt': nrm(ks[8], (BATCH, N_MEM, D_MODEL)),
        'ln_g': 1.0 + nrm(ks[9], (DEPTH, 2, D_MODEL), 0.02),
        'ln_b': nrm(ks[10], (DEPTH, 2, D_MODEL), 0.02),
        'w_in_a': nrm(ks[11], (N_A, D_MODEL, d_in_a), D_MODEL ** -0.5),
        'conv_a_w': nrm(ks[12], (N_A, CONV_W, D_A), CONV_W ** -0.5),
        'w_in_b': nrm(ks[13], (N_B, D_MODEL, d_in_b), D_MODEL ** -0.5),
        'w_o': nrm(ks[14], (DEPTH, D_MIX, D_MODEL), D_MIX ** -0.5 * BETA),
        'w_mem_kv': nrm(ks[15], (DEPTH, D_MODEL, 2 * MEM_W), D_MODEL ** -0.5),
        'w_up': nrm(ks[16], (DEPTH, D_MODEL, 2 * D_FF), D_MODEL ** -0.5),
        'ffn_conv_w': nrm(ks[17], (DEPTH, CONV_W, D_FF), CONV_W ** -0.5),
        'w_down': nrm(ks[18], (DEPTH, D_FF, D_MODEL), D_FF ** -0.5 * BETA),
        'w_kv_shared': nrm(ks[19], (D_MODEL, 6 * N_KV * HEAD_DIM), D_MODEL ** -0.5),
        'cmp_pos': nrm(ks[20], (2, CMP_LEN, HEAD_DIM), 0.1),
        'w_cmp': nrm(ks[21], (2, CMP_LEN, HEAD_DIM, HEAD_DIM), (CMP_LEN * HEAD_DIM) ** -0.5),
    }


def reference(x_prompt, x_sample, cache_kv, cache_win, cache_mem, state_conv_mix, state_conv_ffn, page_table,
              mem_prompt, ln_g, ln_b, w_in_a, conv_a_w, w_in_b, w_o, w_mem_kv, w_up, ffn_conv_w, w_down,
              w_kv_shared, cmp_pos, w_cmp):
    B, T = x_prompt.shape[:2]
    mem_kv = jnp.einsum('bmd,lde->lbme', mem_prompt, w_mem_kv).reshape(DEPTH, B, N_MEM, 2, MEM_HEADS, HEAD_DIM)
    zeros_mix = jnp.zeros((N_A, B, CONV_W - 1, D_A), x_prompt.dtype)
    zeros_ffn = jnp.zeros((DEPTH, B, CONV_W - 1, D_FF), x_prompt.dtype)
    h_p, conv_p, ffn_pa = a_layers(x_prompt, mem_kv[:, :, :, 0], mem_kv[:, :, :, 1], zeros_mix, zeros_ffn,
                                   ln_g, ln_b, w_in_a, conv_a_w, w_o, w_up, ffn_conv_w, w_down)
    kv_p = (h_p @ w_kv_shared).reshape(B, T, 6, N_KV, HEAD_DIM)
    keys_p = key_side(kv_p[:, :, 0], kv_p[:, :, 1], kv_p[:, :, 2], kv_p[:, :, 3], cmp_pos, w_cmp) \
        + prompt_window_bands(kv_p[:, :, 4], kv_p[:, :, 5])
    y_prompt, ffn_pb = b_layers(h_p, jnp.arange(T), keys_p, mem_kv[:, :, :, 0], mem_kv[:, :, :, 1], zeros_ffn,
                                ln_g, ln_b, w_in_b, w_o, w_up, ffn_conv_w, w_down)
    DB, DS = x_sample.shape[:2]
    past_len = page_table.shape[1] * cache_kv.shape[1]
    wb = cache_win.shape[1]
    qpos_s = past_len + jnp.arange(DS)
    h_s, conv_s, ffn_sa = a_layers(x_sample, cache_mem[:, :, :, 0], cache_mem[:, :, :, 1], state_conv_mix,
                                   state_conv_ffn, ln_g, ln_b, w_in_a, conv_a_w, w_o, w_up, ffn_conv_w, w_down)
    kv_s = (h_s @ w_kv_shared).reshape(DB, DS, 6, N_KV, HEAD_DIM)
    rows = [jnp.concatenate([cache_kv[page_table, :, i].reshape(DB, past_len, N_KV, HEAD_DIM), kv_s[:, :, i]], axis=1)
            for i in range(N_KV_KINDS)]
    win = jnp.concatenate([cache_win, kv_s[:, :, 4:]], axis=1)
    win_kpos = (past_len - wb + jnp.arange(wb + DS))[None, :]
    keys_s = key_side(rows[0], rows[1], rows[2], rows[3], cmp_pos, w_cmp) \
        + (win[:, None, :, 0], win[:, None, :, 1], win_kpos)
    y_sample, ffn_sb = b_layers(h_s, qpos_s, keys_s, cache_mem[:, :, :, 0], cache_mem[:, :, :, 1], state_conv_ffn,
                                ln_g, ln_b, w_in_b, w_o, w_up, ffn_conv_w, w_down)
    return (y_prompt, y_sample,
            kv_p[:, :, :N_KV_KINDS], kv_p[:, max(T - WINDOW, 0):, 4:], mem_kv, jnp.stack(conv_p),
            jnp.stack(ffn_pa + ffn_pb),
            kv_s[:, :, :N_KV_KINDS], win[:, -wb:], jnp.stack(conv_s), jnp.stack(ffn_sa + ffn_sb))
```

```python
import numpy as np
import ml_dtypes
from contextlib import ExitStack
import concourse.bass as bass
import concourse.mybir as mybir
from concourse.bass_utils import run_bass_kernel_spmd

F32 = mybir.dt.float32
BF16 = mybir.dt.bfloat16
I32 = mybir.dt.int32
AF = mybir.ActivationFunctionType
ALU = mybir.AluOpType
AX = mybir.AxisListType

D = 2048
NCH = 16
DA = 1536
DFF = 5504
NF = 43
NMEM = 256
W = 512
ALPHA = 8 ** 0.25
EPS = 1e-5
SCALE = 128 ** -0.5
NEG = -3.0e38

GEN = 1000000000
NDMA = 24
NGEN = 1
NSLOT = 6


class Prog:
    ENG = ['pe', 'act', 'dve', 'pool', 'sp']

    def __init__(self, nc, stack):
        self.nc = nc
        self.q = {e: [] for e in self.ENG}
        self.cnt = {e: 0 for e in self.ENG}
        self.known = {e: {} for e in self.ENG}
        self.res = {}
        self.sems = {}
        for e in ('pe', 'act', 'dve'):
            for g in range(NGEN):
                self.sems[(e, g)] = stack.enter_context(nc.semaphore(f"s_{e}_{g}"))
        self.sems[('pool', 0)] = stack.enter_context(nc.semaphore("s_pool_0"))
        for i in range(NDMA):
            self.sems[('dma', i)] = stack.enter_context(nc.semaphore(f"s_dma_{i}"))
        self.dma_tot = [0] * NDMA
        self.dma_rr = 0
        self.alias = {}

    def define(self, name, cells):
        self.alias[name] = list(cells)

    def _expand(self, names):
        out = []
        for n in names:
            a = self.alias.get(n)
            if a is None:
                out.append(n)
            else:
                out.extend(a)
        return out

    def _collect(self, eng, reads, writes):
        reads = self._expand(reads)
        writes = self._expand(writes)
        need = {}

        def add(dd):
            for k, v in dd.items():
                if need.get(k, 0) < v:
                    need[k] = v
        for r in reads:
            ent = self.res.get(r)
            if ent:
                add(ent[0])
        for r in writes:
            ent = self.res.get(r)
            if ent:
                add(ent[0])
                add(ent[1])
        waits = []
        kn = self.known[eng]
        for k, v in need.items():
            if eng == 'pe' and k[0] == 'pe':
                continue
            if kn.get(k, 0) < v:
                waits.append((k, v))
                kn[k] = v
        return waits

    def _update(self, key, val, reads, writes):
        reads = self._expand(reads)
        writes = self._expand(writes)
        for r in reads:
            ent = self.res.get(r)
            if ent is None:
                ent = self.res[r] = ({}, {})
            if ent[1].get(key, 0) < val:
                ent[1][key] = val
        for r in writes:
            self.res[r] = ({key: val}, {})

    @staticmethod
    def _excl(reads, writes):
        ps = [r for r in reads if isinstance(r, str) and (r.startswith('pf') or r.startswith('pb'))]
        if ps:
            reads = [r for r in reads if r not in ps]
            writes = list(writes) + ps
        return reads, writes

    def op(self, eng, fn, reads=(), writes=()):
        reads, writes = self._excl(reads, writes)
        waits = self._collect(eng, reads, writes)
        self.cnt[eng] += 1
        g, v = divmod(self.cnt[eng] - 1, GEN)
        key = (eng, g)
        sems = self.sems

        def emit(e):
            for k, val in waits:
                e.wait_ge(sems[k], val)
            fn(e).then_inc(sems[key], 1)
        self.q[eng].append(emit)
        self._update(key, v + 1, reads, writes)

    def dma(self, eng, out, in_, reads=(), writes=(), **kw):
        s = self.dma_rr
        self.dma_rr = (s + 1) % NDMA
        key = ('dma', s)
        waits = self._collect(eng, reads, writes)
        prev = self.dma_tot[s]
        if prev and self.known[eng].get(key, 0) < prev:
            waits.append((key, prev))
            self.known[eng][key] = prev
        self.dma_tot[s] = prev + 16
        sems = self.sems

        def emit(e):
            for k, val in waits:
                e.wait_ge(sems[k], val)
            e.dma_start(out=out, in_=in_, **kw).then_inc(sems[key], 16)
        self.q[eng].append(emit)
        self._update(key, prev + 16, reads, writes)

    def idma(self, out, in_, idx_ap, reads=(), writes=()):
        eng = 'pool'
        s = self.dma_rr
        self.dma_rr = (s + 1) % NDMA
        key = ('dma', s)
        waits = self._collect(eng, reads, writes)
        prev = self.dma_tot[s]
        if prev and self.known[eng].get(key, 0) < prev:
            waits.append((key, prev))
            self.known[eng][key] = prev
        self.dma_tot[s] = prev + 16
        sems = self.sems

        def emit(e):
            for k, val in waits:
                e.wait_ge(sems[k], val)
            e.indirect_dma_start(out=out, out_offset=None, in_=in_,
                                 in_offset=bass.IndirectOffsetOnAxis(ap=idx_ap, axis=0)).then_inc(sems[key], 16)
        self.q[eng].append(emit)
        self._update(key, prev + 16, reads, writes)

    def finish(self):
        waits = []
        for i in range(NDMA):
            if self.dma_tot[i]:
                waits.append((('dma', i), self.dma_tot[i]))
        for e in self.ENG:
            if e == 'sp' or self.cnt[e] == 0:
                continue
            g, v = divmod(self.cnt[e] - 1, GEN)
            waits.append(((e, g), v + 1))
        sems = self.sems

        def emit(e):
            for k, val in waits:
                e.wait_ge(sems[k], val)
        self.q['sp'].append(emit)

    def run_block(self):
        nc = self.nc
        q = self.q
        with nc.Block() as block:
            @block.sync
            def _(e):
                for f in q['sp']:
                    f(e)

            @block.tensor
            def _(e):
                for f in q['pe']:
                    f(e)

            @block.scalar
            def _(e):
                for f in q['act']:
                    f(e)

            @block.vector
            def _(e):
                for f in q['dve']:
                    f(e)

            @block.gpsimd
            def _(e):
                for f in q['pool']:
                    f(e)


class Builder:
    def __init__(self, NT=4, do_sample=True, n_layers=4, upto='all'):
        self.NT = NT
        self.T = NT * W
        self.do_sample = do_sample
        self.n_layers = n_layers
        self.upto = upto
        self.nc = bass.Bass("TRN2", target_bir_lowering=False)
        self.stack = ExitStack()
        self.ins = {}
        self.outs = {}
        self.scr = {}

    def din(self, name, shape, dt=F32):
        t = self.nc.dram_tensor(name, list(shape), dt, kind="ExternalInput").ap()
        self.ins[name] = t
        return t

    def dout(self, name, shape, dt=F32):
        t = self.nc.dram_tensor(name, list(shape), dt, kind="ExternalOutput").ap()
        self.outs[name] = t
        return t

    def dscr(self, name, shape, dt):
        t = self.nc.dram_tensor(name, list(shape), dt, kind="ExternalOutput").ap()
        self.scr[name] = t
        return t

    def sb(self, name, shape, dt):
        return self.stack.enter_context(self.nc.sbuf_tensor("sb_" + name, list(shape), dt))

    def ps(self, name, shape, dt):
        return self.stack.enter_context(self.nc.psum_tensor(name, list(shape), dt))

    def wload(self, src, shape):
        i = self.wslot
        self.wslot = (i + 1) % NSLOT
        n = 1
        for s in shape[1:]:
            n *= s
        assert n <= 2048
        flat = self.wring[0:shape[0], i, 0:n]
        if len(shape) == 3:
            view = flat.rearrange("p (a b) -> p a b", b=shape[2])
        else:
            view = flat
        name = ('ws', i)
        self.P.dma('pool', view, src, writes=[name])
        return view, name

    def mm(self, out, lhsT, rhs, start, stop, reads, writes):
        self.P.op('pe', lambda e: e.matmul(out, lhsT=lhsT, rhs=rhs, start=start, stop=stop), reads=reads, writes=writes)

    def tr(self, out, in_, ident, reads, writes):
        self.P.op('pe', lambda e: e.transpose(out, in_, ident), reads=reads, writes=writes)

    def act(self, out, in_, func, reads, writes, **kw):
        self.P.op('act', lambda e: e.activation(out=out, in_=in_, func=func, **kw), reads=reads, writes=writes)

    def tt(self, out, in0, in1, op, reads, writes, eng='dve'):
        self.P.op(eng, lambda e: e.tensor_tensor(out=out, in0=in0, in1=in1, op=op), reads=reads, writes=writes)

    def stt(self, out, in0, scalar, in1, op0, op1, reads, writes, eng='dve', **kw):
        self.P.op(eng, lambda e: e.scalar_tensor_tensor(out=out, in0=in0, scalar=scalar, in1=in1, op0=op0, op1=op1, **kw),
                  reads=reads, writes=writes)

    def ts(self, out, in0, s1, s2, op0, op1, reads, writes, eng='dve', **kw):
        if op1 is None:
            self.P.op(eng, lambda e: e.tensor_scalar(out=out, in0=in0, scalar1=s1, scalar2=None, op0=op0, **kw),
                      reads=reads, writes=writes)
        else:
            self.P.op(eng, lambda e: e.tensor_scalar(out=out, in0=in0, scalar1=s1, scalar2=s2, op0=op0, op1=op1, **kw),
                      reads=reads, writes=writes)

    def cp(self, out, in_, reads, writes, eng='dve'):
        if eng == 'act':
            self.act(out, in_, AF.Copy, reads, writes)
        else:
            self.P.op(eng, lambda e: e.tensor_copy(out, in_), reads=reads, writes=writes)

    def recip(self, out, in_, reads, writes):
        self.P.op('dve', lambda e: e.reciprocal(out, in_), reads=reads, writes=writes)

    def proj_fm(self, wsrc, nk_total, rhs_fn, rhs_res, ps, psname, ncols):
        k0 = 0
        while k0 < nk_total:
            nk = min(16, nk_total - k0)
            slot, sname = self.wload(wsrc(k0, nk), [128, nk, 128])
            for kk in range(nk):
                k = k0 + kk
                self.mm(ps[:, :ncols], slot[:, kk, :], rhs_fn(k), k == 0, k == nk_total - 1,
                        reads=[sname, rhs_res(k)], writes=[psname])
            k0 += nk

    @staticmethod
    def wcols(w3, c0):
        return lambda k0, nk: w3[k0 * 128:(k0 + nk) * 128, c0:c0 + 128].rearrange("(k p) n -> p k n", p=128)

    def layernorm(self, st, l, which):
        ncols, tag, xres, xT = st['ncols'], st['tag'], st['xres'], st['xT']
        pm, pq = self.pf[4], self.pf[5]
        for j in range(NCH):
            xb, xbn = self.lnb16[j % 2], ('lnb16', j % 2)
            sq, sqn = self.lnb16[2 + j % 2], ('lnb16', 2 + j % 2)
            self.cp(xb[:, :ncols], xres[:, j, :ncols], reads=[(tag, 'xres', j)], writes=[xbn], eng='dve')
            self.act(sq[:, :ncols], xres[:, j, :ncols], AF.Square, reads=[(tag, 'xres', j)], writes=[sqn])
            self.mm(pm[:, :ncols], self.ones_b[:, :], xb[:, :ncols], j == 0, j == NCH - 1,
                    reads=[xbn, 'const'], writes=['pf4'])
            self.mm(pq[:, :ncols], self.ones_b[:, :], sq[:, :ncols], j == 0, j == NCH - 1,
                    reads=[sqn, 'const'], writes=['pf5'])
        mean, rstd, tmp = self.lnmean, self.lnrstd, self.lntmp
        self.cp(mean[:, :ncols], pm[:, :ncols], reads=['pf4'], writes=['lnmean'], eng='act')
        self.tt(tmp[:, :ncols], mean[:, :ncols], mean[:, :ncols], ALU.mult, reads=['lnmean'], writes=['lntmp'])
        self.tt(tmp[:, :ncols], pq[:, :ncols], tmp[:, :ncols], ALU.subtract, reads=['pf5', 'lntmp'], writes=['lntmp'])
        self.ts(tmp[:, :ncols], tmp[:, :ncols], EPS, None, ALU.add, None, reads=['lntmp'], writes=['lntmp'])
        self.act(rstd[:, :ncols], tmp[:, :ncols], AF.Sqrt, reads=['lntmp'], writes=['lnrstd'])
        self.recip(rstd[:, :ncols], rstd[:, :ncols], reads=['lnrstd'], writes=['lnrstd'])
        g = self.lng[:, l, which, :]
        b = self.lnb[:, l, which, :]
        for j in range(NCH):
            t2 = self.zf[j % 2]
            self.tt(t2[:, :ncols], xres[:, j, :ncols], mean[:, :ncols], ALU.subtract,
                    reads=[(tag, 'xres', j), 'lnmean'], writes=[('zf', j % 2)])
            self.stt(t2[:, :ncols], t2[:, :ncols], g[:, j:j + 1], rstd[:, :ncols], ALU.mult, ALU.mult,
                     reads=[('zf', j % 2), 'lnrstd', 'const'], writes=[('zf', j % 2)])
            self.act(xres[:, j, :ncols], t2[:, :ncols], AF.Identity, bias=b[:, j:j + 1],
                     reads=[('zf', j % 2), 'const'], writes=[(tag, 'xres', j)])
            self.cp(xT[:, j, :ncols], xres[:, j, :ncols], reads=[(tag, 'xres', j)], writes=[(tag, 'xT', j)], eng='dve')

    def mem_attention(self, st):
        tag, qmT, mixT = st['tag'], st['qmT'], st['mixT']
        memKT, memV, kres = st['memKT'], st['memV'], st['memres']
        u = 0
        for h in range(4):
            for s, m in enumerate(st['subw']):
                pS, pSn = self.pf[u % 2], f'pf{u % 2}'
                E, En = self.att_e[u % 2], ('att_e', u % 2)
                PT, PTn = self.att_pt[u % 2], ('att_pt', u % 2)
                pT, pTn = self.pb[u % 2], f'pb{u % 2}'
                pO, pOn = self.pf[2 + u % 2], f'pf{2 + u % 2}'
                den, denn = self.att_den[:, u % 2, :], ('att_den', u % 2)
                mo, mon = self.att_o[u % 2], ('att_o', u % 2)
                u += 1
                self.mm(pS[:m, :256], qmT[:, h, s * 128:s * 128 + m], memKT[:, h, :], True, True,
                        reads=[(tag, 'qmT', h), kres], writes=[pSn])
                self.act(E[:m, :256], pS[:m, :256], AF.Exp, scale=SCALE, accum_out=den[:m, 0:1],
                         reads=[pSn], writes=[En, denn])
                for c in range(2):
                    self.tr(pT[:, c * 128:c * 128 + m], E[:m, c * 128:(c + 1) * 128], self.ident_b[:m, :m],
                            reads=[En, 'const'], writes=[pTn])
                self.cp(PT[:, 0:2, :m], pT[:, 0:256].rearrange("p (c q) -> p c q", q=128)[:, :, :m],
                        reads=[pTn], writes=[PTn], eng='act')
                for c in range(2):
                    self.mm(pO[:m, :128], PT[:, c, :m], memV[:, c, h * 128:(h + 1) * 128], c == 0, c == 1,
                            reads=[PTn, kres], writes=[pOn])
                self.recip(den[:m, 1:2], den[:m, 0:1], reads=[denn], writes=[denn])
                self.ts(mo[:m, :], pO[:m, :128], den[:m, 1:2], None, ALU.mult, None, reads=[pOn, denn], writes=[mon])
                self.tr(pT[:, 512:512 + m], mo[:m, :], self.ident_b[:m, :m], reads=[mon, 'const'], writes=[pTn])
                self.cp(mixT[:, 12 + h, s * 128:s * 128 + m], pT[:, 512:512 + m], reads=[pTn], writes=[(tag, 'mixT', 12 + h)],
                        eng='dve')

    def layer_tail(self, st, l):
        ncols, tag = st['ncols'], st['tag']
        xres, xT, mixT, hT = st['xres'], st['xT'], st['mixT'], st['hT']
        w_o, w_up, w_down = self.ins['w_o'], self.ins['w_up'], self.ins['w_down']
        for j in range(NCH):
            ps, psn = self.pf[j % 4], f'pf{j % 4}'
            self.proj_fm(self.wcols(w_o[l], j * 128), NCH, lambda k: mixT[:, k, :ncols], lambda k: (tag, 'mixT', k), ps, psn, ncols)
            self.stt(xres[:, j, :ncols], xres[:, j, :ncols], ALPHA, ps[:, :ncols], ALU.mult, ALU.add,
                     reads=[psn, (tag, 'xres', j)], writes=[(tag, 'xres', j)])
        if self.upto == 'wo':
            return
        self.layernorm(st, l, 0)
        if self.upto == 'ln1':
            return
        cw = self.ffncw[:, l, :, :]
        carry = st['carry_ffn'][l]
        cres = (tag, 'carry_ffn', l)
        for f in range(NF):
            b0 = (2 * f) % 4
            pz, pzn = self.pf[b0], f'pf{b0}'
            pg, pgn = self.pf[b0 + 1], f'pf{b0 + 1}'
            self.proj_fm(self.wcols(w_up[l], f * 128), NCH, lambda k: xT[:, k, :ncols], lambda k: (tag, 'xT', k), pz, pzn, ncols)
            self.proj_fm(self.wcols(w_up[l], DFF + f * 128), NCH, lambda k: xT[:, k, :ncols], lambda k: (tag, 'xT', k), pg, pgn, ncols)
            zf, zfn = self.zf[f % 2], ('zf', f % 2)
            t1, t1n = self.ct[f % 2], ('ct', f % 2)
            self.cp(zf[:, 0:2], carry[:, f, :], reads=[cres], writes=[zfn], eng='dve')
            self.cp(zf[:, 2:2 + ncols], pz[:, :ncols], reads=[pzn], writes=[zfn], eng='act')
            self.cp(carry[:, f, :], zf[:, ncols:ncols + 2], reads=[zfn], writes=[cres], eng='dve')
            self.act(t1[:, :ncols], zf[:, 0:ncols], AF.Copy, scale=cw[:, f, 0:1], reads=[zfn, 'const'], writes=[t1n])
            self.stt(t1[:, :ncols], zf[:, 1:1 + ncols], cw[:, f, 1:2], t1[:, :ncols], ALU.mult, ALU.add,
                     reads=[zfn, t1n, 'const'], writes=[t1n])
            self.stt(t1[:, :ncols], zf[:, 2:2 + ncols], cw[:, f, 2:3], t1[:, :ncols], ALU.mult, ALU.add,
                     reads=[zfn, t1n, 'const'], writes=[t1n])
            self.act(t1[:, :ncols], t1[:, :ncols], AF.Silu, reads=[t1n], writes=[t1n])
            self.tt(hT[:, f, :ncols], t1[:, :ncols], pg[:, :ncols], ALU.mult, reads=[t1n, pgn], writes=[(tag, 'hT', f)])
        if self.upto == 'ffn_up':
            return
        for j in range(NCH):
            ps, psn = self.pf[j % 4], f'pf{j % 4}'
            self.proj_fm(self.wcols(w_down[l], j * 128), NF, lambda k: hT[:, k, :ncols], lambda k: (tag, 'hT', k), ps, psn, ncols)
            self.stt(xres[:, j, :ncols], xres[:, j, :ncols], ALPHA, ps[:, :ncols], ALU.mult, ALU.add,
                     reads=[psn, (tag, 'xres', j)], writes=[(tag, 'xres', j)])
        self.layernorm(st, l, 1)

    def a_mixer(self, st, l):
        ncols, tag = st['ncols'], st['tag']
        xT, mixT, qmT = st['xT'], st['mixT'], st['qmT']
        w_in = self.ins['w_in_a'][l]
        cw = self.mixcw[:, l, :, :]
        carry = st['carry_mix'][l]
        cres = (tag, 'carry_mix', l)
        xr = lambda k: xT[:, k, :ncols]
        xn = lambda k: (tag, 'xT', k)
        for j in range(12):
            base = 3 * (j % 2)
            pu, pb_, pc = self.pf[base], self.pf[base + 1], self.pf[base + 2]
            pun, pbn, pcn = f'pf{base}', f'pf{base + 1}', f'pf{base + 2}'
            for (ps, psn, off) in ((pu, pun, 0), (pb_, pbn, DA), (pc, pcn, 2 * DA)):
                self.proj_fm(self.wcols(w_in, off + j * 128), NCH, xr, xn, ps, psn, ncols)
            us, usn = self.ct[j % 2], ('ct', j % 2)
            cuf, cufn = self.zf[j % 2], ('zf', j % 2)
            self.cp(us[:, :ncols], pu[:, :ncols], reads=[pun], writes=[usn], eng='act')
            self.cp(cuf[:, 0:2], carry[:, j, :], reads=[cres], writes=[cufn], eng='dve')
            self.tt(cuf[:, 2:2 + ncols], pc[:, :ncols], us[:, :ncols], ALU.mult, reads=[pcn, usn], writes=[cufn])
            self.cp(carry[:, j, :], cuf[:, ncols:ncols + 2], reads=[cufn], writes=[cres], eng='dve')
            self.act(us[:, :ncols], cuf[:, 0:ncols], AF.Copy, scale=cw[:, j, 0:1], reads=[cufn, 'const'], writes=[usn])
            self.stt(us[:, :ncols], cuf[:, 1:1 + ncols], cw[:, j, 1:2], us[:, :ncols], ALU.mult, ALU.add,
                     reads=[cufn, usn, 'const'], writes=[usn])
            self.stt(us[:, :ncols], cuf[:, 2:2 + ncols], cw[:, j, 2:3], us[:, :ncols], ALU.mult, ALU.add,
                     reads=[cufn, usn, 'const'], writes=[usn])
            self.tt(mixT[:, j, :ncols], us[:, :ncols], pb_[:, :ncols], ALU.mult, reads=[usn, pbn], writes=[(tag, 'mixT', j)])
        for h in range(4):
            ps, psn = self.pf[h % 2], f'pf{h % 2}'
            self.proj_fm(self.wcols(w_in, 3 * DA + h * 128), NCH, xr, xn, ps, psn, ncols)
            self.cp(qmT[:, h, :ncols], ps[:, :ncols], reads=[psn], writes=[(tag, 'qmT', h)], eng='act')

    def mem_project_all(self):
        w = self.ins['w_mem_kv']
        mpT = self.U[:, 0:16 * NMEM].rearrange("p (c t) -> p c t", t=NMEM)
        P = self.P
        mpn = [('U', i) for i in range(8)]
        stg = [self.U[:, 8192 + i * 4096: 8192 + (i + 1) * 4096].bitcast(F32) for i in range(2)]
        stgn = [[('U', 16 + i * 8 + k) for k in range(8)] for i in range(2)]
        for s in range(2):
            self.load_transpose(self.ins['memp'][s * 128:(s + 1) * 128, :], 128, stg[s], stgn[s], None, mpT, s * 128,
                                None, lambda jq: mpn)
        ob = self.smallf
        import os
        mps = int(os.environ.get('MP_STAGE', '9'))
        for l in range(4 if mps >= 2 else 0):
            for c in range(2):
                pss = [(self.pf[0], 'pf0'), (self.pf[1], 'pf1')]
                for kq in range(4):
                    slot, sname = self.wload(w[l, kq * 512:(kq + 1) * 512, c * 512:(c + 1) * 512].rearrange("(k p) n -> p k n", p=128),
                                             [128, 4, 512])
                    for s in range(2):
                        for kk in range(4):
                            k = kq * 4 + kk
                            self.mm(pss[s][0][:, :], mpT[:, k, s * 128:(s + 1) * 128], slot[:, kk, :], k == 0, k == 15,
                                    reads=[sname] + mpn, writes=[pss[s][1]])
                for s in range(2):
                    self.cp(ob[:, s, :], pss[s][0][:, :], reads=[pss[s][1]], writes=[('smallf', s)], eng='act')
                    P.dma('sp', self.outs['mem_kv'][l, s * 128:(s + 1) * 128, c * 512:(c + 1) * 512], ob[:, s, :],
                          reads=[('smallf', s)])
                    if c == 1 and mps >= 3:
                        self.cp(self.smallb[:, s, :], pss[s][0][:, :], reads=[pss[s][1]], writes=[('smallb', s)], eng='dve')
                        P.dma('sp', self.memV_d[0, l, s * 128:(s + 1) * 128, :], self.smallb[:, s, :], reads=[('smallb', s)],
                              writes=[('memV_d', 0, l)])
            for h in range(4 if mps >= 4 else 0):
                ps, psn = self.pf[2 + h % 2], f'pf{2 + h % 2}'
                self.proj_fm(self.wcols(w[l], h * 128), NCH, lambda k: mpT[:, k, :], lambda k: mpn[k // 2], ps, psn, 256)
                self.cp(self.smallb[:, 2 + h % 2, 0:256], ps[:, :256], reads=[psn], writes=[('smallb', 2 + h % 2)], eng='act')
                P.dma('sp', self.memKT_d[0, l, :, h, :], self.smallb[:, 2 + h % 2, 0:256], reads=[('smallb', 2 + h % 2)],
                      writes=[('memKT_d', 0, l)])

    def load_mem(self, st, grp, l):
        P = self.P
        P.dma('sp', st['memKT'][:], self.memKT_d[grp, l], reads=[('memKT_d', grp, l)], writes=[st['memres']])
        P.dma('sp', st['memV'][:], self.memV_d[grp, l].rearrange("(c p) n -> p c n", p=128), reads=[('memV_d', grp, l)],
              writes=[st['memres']])

    def load_transpose(self, src_rows, m, stg, stgn, dstf, dstb, col0, resf, resb):
        self.P.dma('sp', stg[:m, :], src_rows, writes=stgn)
        for jq in range(4):
            pt, ptn = self.pf[jq % 2], f'pf{jq % 2}'
            for i in range(4):
                j = jq * 4 + i
                self.tr(pt[:, i * 128:i * 128 + m], stg[:m, j * 128:(j + 1) * 128], self.ident_f[:m, :m],
                        reads=stgn + ['const'], writes=[ptn])
            src = pt[:, :].rearrange("p (c q) -> p c q", q=128)[:, :, :m]
            if dstf is not None:
                self.cp(dstf[:, jq * 4:jq * 4 + 4, col0:col0 + m], src, reads=[ptn], writes=resf(jq), eng='act')
            if dstb is not None:
                self.cp(dstb[:, jq * 4:jq * 4 + 4, col0:col0 + m], src, reads=[ptn], writes=resb(jq), eng='dve')

    def build(self):
        nc = self.nc
        NT, T = self.NT, self.T
        self.din('x', [T, D])
        self.din('memp', [NMEM, D])
        self.din('w_in_a', [2, D, 3 * DA + 512])
        self.din('w_in_b', [2, D, DA + 36 + 512])
        self.din('w_o', [4, D, D])
        self.din('w_mem_kv', [4, D, 1024])
        self.din('w_up', [4, D, 2 * DFF])
        self.din('w_down', [4, DFF, D])
        self.din('w_kv', [D, 3072])
        self.din('w_cmp', [2, 32, 128, 128])
        self.din('lng', [128, 4, 2, 16])
        self.din('lnb', [128, 4, 2, 16])
        self.din('mixcw', [128, 2, 12, 3])
        self.din('ffncw', [128, 4, NF, 3])
        self.din('ident_f', [128, 128])
        self.din('ident_b', [128, 128], BF16)
        self.din('ones_b', [128, 128], BF16)
        self.dout('y', [T, D])
        self.dout('mem_kv', [4, NMEM, 1024])
        self.dout('conv_mix', [2, 128, 12, 2])
        self.dout('conv_ffn', [4, 128, NF, 2])
        self.dout('dbg', [128, 16, W])
        self.dout('kv_rows', [T, 2048])
        self.dout('win', [512, 1024])
        self.din('cmask', [128, NT * 4, 128], BF16)
        self.din('tk_mul', [128, NT * 4, 32], BF16)
        self.din('tk_add', [128, NT * 4, 32], BF16)
        self.din('tk_elig', [128, NT * 4, 32], BF16)
        self.din('tri', [128, 4, 512], BF16)
        self.din('wprev', [128, 4, 512], BF16)
        self.din('ovl', [32, 4, 32], BF16)
        self.din('posT', [128, 2, 32], BF16)
        self.din('posrep', [128, 32, 32], BF16)
        if self.do_sample:
            self.sample_decl()
        self.ksT_d = self.dscr('ksT_d', [4, 128, T], BF16)
        self.kwT_d = self.dscr('kwT_d', [4, 128, T], BF16)
        self.vs_d = self.dscr('vs_d', [T, 512], BF16)
        self.vw_d = self.dscr('vw_d', [T, 512], BF16)
        self.memKT_d = self.dscr('memKT_d', [2, 4, 128, 4, NMEM], BF16)
        self.memV_d = self.dscr('memV_d', [2, 4, NMEM, 512], BF16)

        with self.stack:
            self.P = P = Prog(nc, self.stack)
            self.pf = [self.ps(f"pf{i}", [128, 512], F32) for i in range(6)]
            self.pb = [self.ps(f"pb{i}", [128, 1024], BF16) for i in range(2)]
            self.wring = self.sb("wring", [128, NSLOT, 2048], BF16)
            self.wslot = 0
            self.ident_f = self.sb("ident_f", [128, 128], F32)
            self.ident_b = self.sb("ident_b", [128, 128], BF16)
            self.ones_b = self.sb("ones_b", [128, 128], BF16)
            self.lnb16 = [self.sb(f"lnb16_{i}", [128, W], BF16) for i in range(4)]
            self.lng = self.sb("lng", [128, 4, 2, 16], F32)
            self.lnb = self.sb("lnb", [128, 4, 2, 16], F32)
            self.mixcw = self.sb("mixcw", [128, 2, 12, 3], F32)
            self.ffncw = self.sb("ffncw", [128, 4, NF, 3], F32)
            self.cmask = self.sb("cmask", [128, NT * 4, 128], BF16)
            self.tk_mul = self.sb("tk_mul", [128, NT * 4, 32], BF16)
            self.tk_add = self.sb("tk_add", [128, NT * 4, 32], BF16)
            self.tk_elig = self.sb("tk_elig", [128, NT * 4, 32], BF16)
            self.tri = self.sb("tri", [128, 4, 512], BF16)
            self.wprev = self.sb("wprev", [128, 4, 512], BF16)
            self.ovl = self.sb("ovl", [32, 4, 32], BF16)
            self.posT = self.sb("posT", [128, 2, 32], BF16)
            self.posrep = self.sb("posrep", [128, 32, 32], BF16)
            for nm in ('ident_f', 'ident_b', 'ones_b', 'lng', 'lnb', 'mixcw', 'ffncw', 'cmask', 'tk_mul', 'tk_add', 'tk_elig',
                       'tri', 'wprev', 'ovl', 'posT', 'posrep'):
                P.dma('sp', getattr(self, nm)[:], self.ins[nm], writes=['const'])
            self.kcbuf = self.sb("kcbuf", [128, 4, 528], BF16)
            self.vcbuf = self.sb("vcbuf", [128, 4, 528], BF16)
            self.cmptmp = self.sb("cmptmp", [128, 4, 16], BF16)
            self.cmpKT = self.sb("cmpKT", [128, 4, 128], BF16)
            self.cmpV = [self.sb(f"cmpV{i}", [32, 512], BF16) for i in range(NT)]
            self.posK = self.sb("posK", [128, 1], F32)
            self.posV = self.sb("posV", [32, 128], F32)
            self.cmp_e = self.sb("cmp_e", [128, 128], F32)
            self.cmp_p = self.sb("cmp_p", [128, 128], BF16)
            self.cmp_pt = self.sb("cmp_pt", [32, 4, 128], BF16)
            self.cmp_den = self.sb("cmp_den", [128, 2], F32)
            self.sel_den = self.sb("sel_den", [128, 8], F32)
            self.win_den = self.sb("win_den", [128, 8], F32)
            self.nsa_acc = self.sb("nsa_acc", [128, 3, 128], F32)
            self.tk_sc = self.sb("tk_sc", [128, 32], F32)
            self.tk_sc2 = self.sb("tk_sc2", [128, 32], F32)
            self.tk_m8 = self.sb("tk_m8", [128, 16], F32)
            self.selmask = self.sb("selmask", [128, 32], BF16)
            self.mdiag = self.sb("mdiag", [128, 512], BF16)
            gsig = self.sb("gsig", [128, 4, 36], F32)
            self.lnmean = self.sb("lnmean", [128, W], F32)
            self.lnrstd = self.sb("lnrstd", [128, W], F32)
            self.lntmp = self.sb("lntmp", [128, W], F32)
            self.zf = [self.sb(f"zf{i}", [128, W + 2], F32) for i in range(2)]
            self.ct = [self.sb(f"ct{i}", [128, W], F32) for i in range(2)]
            self.att_e = [self.sb(f"att_e{i}", [128, 512], BF16) for i in range(2)]
            self.att_pt = [self.sb(f"att_pt{i}", [128, 4, 128], BF16) for i in range(2)]
            self.att_o = [self.sb(f"att_o{i}", [128, 128], BF16) for i in range(2)]
            self.att_den = self.sb("att_den", [128, 2, 4], F32)
            self.smallf = self.sb("smallf", [128, 2, 512], F32)
            self.smallb = self.sb("smallb", [128, 4, 512], BF16)
            NU = 44
            self.U = self.sb("U", [128, NU * 512], BF16)
            U = self.U
            hT = U[:, 0:NF * 512].rearrange("p (c t) -> p c t", t=512)
            mixT = U[:, 0:16 * 512].rearrange("p (c t) -> p c t", t=512)
            qmT = U[:, 16 * 512:20 * 512].rearrange("p (c t) -> p c t", t=512)
            qT = U[:, 20 * 512:32 * 512].rearrange("p (c t) -> p c t", t=512)
            for f in range(NF):
                P.define(('p', 'hT', f), [('U', f)])
            for j in range(16):
                P.define(('p', 'mixT', j), [('U', j)])
            for h in range(4):
                P.define(('p', 'qmT', h), [('U', 16 + h)])
            for h in range(12):
                P.define(('p', 'qT', h), [('U', 20 + h)])
            stg = [U[:, i * 4096:(i + 1) * 4096].bitcast(F32) for i in range(2)]
            stgn = [[('U', i * 8 + k) for k in range(8)] for i in range(2)]
            self.ksg = U[:, 32 * 512:36 * 512]
            self.vsg = U[:, 36 * 512:40 * 512].rearrange("p (c d) -> p c d", d=128)
            self.kwg = U[:, 40 * 512:42 * 512]
            self.vwg = U[:, 42 * 512:44 * 512].rearrange("p (c d) -> p c d", d=128)
            P.define('ksg', [('U', i) for i in range(32, 36)])
            P.define('vsg', [('U', i) for i in range(36, 40)])
            P.define('kwg', [('U', i) for i in range(40, 42)])
            P.define('vwg', [('U', i) for i in range(42, 44)])

            xres = self.sb("xres", [128, 16, W], F32)
            xT = self.sb("xT", [128, 16, W], BF16)
            memKT = self.sb("memKT", [128, 4, NMEM], BF16)
            memV = self.sb("memV", [128, 2, 512], BF16)
            carry_mix = self.sb("carry_mix", [128, 2, 12, 2], F32)
            carry_ffn = self.sb("carry_ffn", [128, 4, NF, 2], F32)
            P.op('dve', lambda e: e.memset(carry_mix[:], 0.0), writes=[('p', 'carry_mix', 0), ('p', 'carry_mix', 1)])
            P.op('dve', lambda e: e.memset(carry_ffn[:], 0.0), writes=[('p', 'carry_ffn', l) for l in range(4)])
            st = dict(xres=xres, xT=xT, mixT=mixT, hT=hT, qmT=qmT, qT=qT, ncols=W, tag='p', subw=[128] * 4, gsig=gsig,
                      memKT=memKT, memV=memV, memres='p_mem',
                      carry_mix=[carry_mix[:, l, :, :] for l in range(2)],
                      carry_ffn=[carry_ffn[:, l, :, :] for l in range(4)])

            stages = ['const', 'memproj', 'xload', 'mixer', 'memattn', 'wo', 'ln1', 'ffn_up', 'all']
            lvl = stages.index(self.upto)
            if lvl >= 1:
                self.mem_project_all()
            if self.n_layers > 2:
                self.cmp_pos_terms()

            for it in range(NT if lvl >= 2 else 0):
                for s in range(4):
                    self.load_transpose(self.ins['x'][it * W + s * 128: it * W + (s + 1) * 128, :], 128, stg[s % 2], stgn[s % 2],
                                        xres, xT, s * 128,
                                        lambda jq: [('p', 'xres', jq * 4 + i) for i in range(4)],
                                        lambda jq: [('p', 'xT', jq * 4 + i) for i in range(4)])
                for l in range(min(2, self.n_layers) if lvl >= 3 else 0):
                    self.load_mem(st, 0, l)
                    self.a_mixer(st, l)
                    if lvl >= 4:
                        self.mem_attention(st)
                    if lvl >= 5:
                        self.layer_tail(st, l)
                if self.n_layers > 2:
                    self.kv_project(st, it)
                    self.compress(it)
                    for jj in range(self.n_layers - 2):
                        l = 2 + jj
                        self.load_mem(st, 0, l)
                        self.b_inproj(st, jj)
                        self.nsa_prompt(st, it)
                        import os
                        if os.environ.get('DBG_MIX') and it == NT - 1 and jj == 0:
                            for j in range(12):
                                self.cp(self.zf[j % 2][:, 0:W], mixT[:, j, :], reads=[('p', 'mixT', j)], writes=[('zf', j % 2)], eng='dve')
                                P.dma('sp', self.outs['dbg'][:, j, :], self.zf[j % 2][:, 0:W], reads=[('zf', j % 2)])
                        self.mem_attention(st)
                        self.layer_tail(st, l)
                for s4 in range(4):
                    sg, sgn = stg[s4 % 2], stgn[s4 % 2]
                    for jq in range(4):
                        pt, ptn = self.pf[jq % 2], f'pf{jq % 2}'
                        for i in range(4):
                            j = jq * 4 + i
                            self.tr(pt[:, i * 128:(i + 1) * 128], xres[:, j, s4 * 128:(s4 + 1) * 128], self.ident_f[:, :],
                                    reads=[('p', 'xres', j), 'const'], writes=[ptn])
                        self.cp(sg[:, jq * 512:(jq + 1) * 512], pt[:, :], reads=[ptn], writes=sgn, eng='act')
                    P.dma('sp', self.outs['y'][it * W + s4 * 128:it * W + (s4 + 1) * 128, :], sg[:, :], reads=sgn)
            import os
            if not os.environ.get('DBG_MIX'):
                P.dma('sp', self.outs['dbg'], xres[:], reads=[('p', 'xres', j) for j in range(NCH)])
            for l in range(2):
                P.dma('sp', self.outs['conv_mix'][l], carry_mix[:, l, :, :], reads=[('p', 'carry_mix', l)])
            for l in range(4):
                P.dma('sp', self.outs['conv_ffn'][l], carry_ffn[:, l, :, :], reads=[('p', 'carry_ffn', l)])
            self.st_p = st
            if self.do_sample:
                self.sample_path()
            P.finish()
            P.run_block()
        return nc

    def kv_project(self, st, it):
        P = self.P
        xT, tag = st['xT'], st['tag']
        w = self.ins['w_kv']
        t0 = it * W
        U = self.U
        kTst = U[:, 0:4 * 512].rearrange("p (g t) -> p g t", t=512)
        kTn = [('U', i) for i in range(4)]
        kvb = U[:, 4 * 512:6 * 512].rearrange("p (a t) -> p a t", t=512)
        for nm in ('kcbuf', 'vcbuf'):
            buf = getattr(self, nm)
            if it == 0:
                P.op('dve', lambda e, buf=buf: e.memset(buf[:, :, 0:16], 0.0), writes=[nm])
            else:
                self.cp(self.cmptmp[:, :, :], buf[:, :, 512:528], reads=[nm], writes=['cmptmp'], eng='dve')
                self.cp(buf[:, :, 0:16], self.cmptmp[:, :, :], reads=['cmptmp'], writes=[nm], eng='dve')
        for c in range(6):
            for kq in range(4):
                slot, sname = self.wload(w[kq * 512:(kq + 1) * 512, c * 512:(c + 1) * 512].rearrange("(k p) n -> p k n", p=128),
                                         [128, 4, 512])
                for s in range(4):
                    for kk in range(4):
                        k = kq * 4 + kk
                        self.mm(self.pf[s][:, :], xT[:, k, s * 128:(s + 1) * 128], slot[:, kk, :], k == 0, k == 15,
                                reads=[sname, (tag, 'xT', k)], writes=[f'pf{s}'])
            for s in range(4):
                kf, kfn = self.smallf[:, s % 2, :], ('smallf', s % 2)
                self.cp(kf, self.pf[s][:, :], reads=[f'pf{s}'], writes=[kfn], eng='act')
                if c < 4:
                    P.dma('sp', self.outs['kv_rows'][t0 + s * 128:t0 + (s + 1) * 128, c * 512:(c + 1) * 512], kf, reads=[kfn])
                elif it == self.NT - 1:
                    P.dma('sp', self.outs['win'][s * 128:(s + 1) * 128, (c - 4) * 512:(c - 3) * 512], kf, reads=[kfn])
                kb, kbn = kvb[:, s % 2, :], ('U', 4 + s % 2)
                self.cp(kb, kf, reads=[kfn], writes=[kbn], eng='dve')
                if c in (3, 5):
                    dst = self.vs_d if c == 3 else self.vw_d
                    P.dma('sp', dst[t0 + s * 128:t0 + (s + 1) * 128, :], kb, reads=[kbn], writes=[('vs_d' if c == 3 else 'vw_d')])
                else:
                    pT, pTn = self.pb[s % 2], f'pb{s % 2}'
                    for g in range(4):
                        self.tr(pT[:, g * 128:(g + 1) * 128], kb[:, g * 128:(g + 1) * 128], self.ident_b[:, :],
                                reads=[kbn, 'const'], writes=[pTn])
                    src = pT[:, 0:512].rearrange("p (g t) -> p g t", t=128)
                    if c == 0:
                        self.cp(self.kcbuf[:, :, 16 + s * 128:16 + (s + 1) * 128], src, reads=[pTn], writes=['kcbuf'], eng='act')
                    elif c == 1:
                        self.cp(self.vcbuf[:, :, 16 + s * 128:16 + (s + 1) * 128], src, reads=[pTn], writes=['vcbuf'], eng='act')
                    else:
                        self.cp(kTst[:, :, s * 128:(s + 1) * 128], src, reads=[pTn], writes=kTn, eng='act')
            if c in (2, 4):
                dst = self.ksT_d if c == 2 else self.kwT_d
                P.dma('sp', dst[:, :, t0:t0 + W].rearrange("g d t -> d g t"), kTst, reads=kTn,
                      writes=[('ksT_d' if c == 2 else 'kwT_d')])

    def compress(self, it, grp=0):
        wc = self.ins['w_cmp']
        for kind in range(2):
            for half in range(2):
                slot, sname = self.wload(wc[kind, half * 16:(half + 1) * 16].rearrange("l d e -> d l e"), [128, 16, 128])
                for g in range(4):
                    for ll in range(16):
                        l = half * 16 + ll
                        if kind == 0:
                            self.mm(self.pf[g][:, 0:32], slot[:, ll, :], self.kcbuf[:, g, l:l + 16 * 31 + 1:16], l == 0, l == 31,
                                    reads=[sname, 'kcbuf'], writes=[f'pf{g}'])
                        else:
                            self.mm(self.pf[g][0:32, 0:128], self.vcbuf[:, g, l:l + 16 * 31 + 1:16], slot[:, ll, :], l == 0, l == 31,
                                    reads=[sname, 'vcbuf'], writes=[f'pf{g}'])
            for g in range(4):
                if kind == 0:
                    self.act(self.cmpKT[:, g, 32 * it:32 * it + 32], self.pf[g][:, 0:32], AF.Identity, bias=self.posK[:, 0:1],
                             reads=[f'pf{g}', 'posK'], writes=['cmpKT'])
                else:
                    self.tt(self.cmpV[it][:, g * 128:(g + 1) * 128], self.pf[g][0:32, 0:128], self.posV[:, :], ALU.add,
                            reads=[f'pf{g}', 'posV'], writes=[('cmpV', it)])

    def cmp_pos_terms(self):
        wc = self.ins['w_cmp']
        for kind in range(2):
            for half in range(2):
                slot, sname = self.wload(wc[kind, half * 16:(half + 1) * 16].rearrange("l d e -> d l e"), [128, 16, 128])
                for ll in range(16):
                    l = half * 16 + ll
                    if kind == 0:
                        self.mm(self.pf[0][:, 0:1], slot[:, ll, :], self.posT[:, 0, l:l + 1], l == 0, l == 31,
                                reads=[sname, 'const'], writes=['pf0'])
                    else:
                        self.mm(self.pf[1][0:32, 0:128], self.posrep[:, l, :], slot[:, ll, :], l == 0, l == 31,
                                reads=[sname, 'const'], writes=['pf1'])
        self.cp(self.posK[:, 0:1], self.pf[0][:, 0:1], reads=['pf0'], writes=['posK'], eng='act')
        self.cp(self.posV[:, :], self.pf[1][0:32, 0:128], reads=['pf1'], writes=['posV'], eng='act')

    def b_inproj(self, st, jj):
        ncols, tag = st['ncols'], st['tag']
        xT, qT, qmT = st['xT'], st['qT'], st['qmT']
        w_in = self.ins['w_in_b'][jj]
        xr = lambda k: xT[:, k, :ncols]
        xn = lambda k: (tag, 'xT', k)
        for h in range(12):
            ps, psn = self.pf[h % 4], f'pf{h % 4}'
            self.proj_fm(self.wcols(w_in, h * 128), NCH, xr, xn, ps, psn, ncols)
            self.cp(qT[:, h, :ncols], ps[:, :ncols], reads=[psn], writes=[(tag, 'qT', h)], eng='act')
        for h in range(4):
            ps, psn = self.pf[h % 4], f'pf{h % 4}'
            self.proj_fm(self.wcols(w_in, DA + 36 + h * 128), NCH, xr, xn, ps, psn, ncols)
            self.cp(qmT[:, h, :ncols], ps[:, :ncols], reads=[psn], writes=[(tag, 'qmT', h)], eng='act')
        slot, sname = self.wload(w_in[:, DA:DA + 36].rearrange("(k p) n -> p k n", p=128), [128, 16, 36])
        gs = st['gsig']
        for s, m in enumerate(st['subw']):
            ps, psn = self.pf[4], 'pf4'
            for k in range(NCH):
                self.mm(ps[:m, 0:36], xT[:, k, s * 128:s * 128 + m], slot[:, k, :], k == 0, k == NCH - 1,
                        reads=[sname, (tag, 'xT', k)], writes=[psn])
            self.act(gs[:m, s, :], ps[:m, 0:36], AF.Sigmoid, reads=[psn], writes=[(tag, 'gsig')])

    def attn_unit(self, u, q_lhsT, q_res, m, kT, k_res, vtok, v_res, nkeys, mask_fn, mask_res, den_ap, den_res, pO, pOn, first, last):
        pS, pSn = self.pf[u % 2], f'pf{u % 2}'
        E, En = self.att_e[u % 2], ('att_e', u % 2)
        PT, PTn = self.att_pt[u % 2], ('att_pt', u % 2)
        pT, pTn = self.pb[u % 2], f'pb{u % 2}'
        self.mm(pS[:m, :nkeys], q_lhsT, kT, True, True, reads=[q_res, k_res], writes=[pSn])
        self.act(E[:m, :nkeys], pS[:m, :nkeys], AF.Exp, scale=SCALE, reads=[pSn], writes=[En])
        mk = mask_fn()
        self.stt(E[:m, :nkeys] if mk.ndim == 2 else E[:m, :nkeys].rearrange("p (b k) -> p b k", k=64),
                 E[:m, :nkeys] if mk.ndim == 2 else E[:m, :nkeys].rearrange("p (b k) -> p b k", k=64),
                 1.0, mk, ALU.mult, ALU.mult, reads=[En] + mask_res, writes=[En, den_res], accum_out=den_ap)
        nc_ = nkeys // 128
        for c in range(nc_):
            self.tr(pT[:, c * 128:c * 128 + m], E[:m, c * 128:(c + 1) * 128], self.ident_b[:m, :m], reads=[En, 'const'], writes=[pTn])
        self.cp(PT[:, 0:nc_, :m], pT[:, 0:nc_ * 128].rearrange("p (c q) -> p c q", q=128)[:, :, :m], reads=[pTn], writes=[PTn], eng='act')
        for c in range(nc_):
            self.mm(pO[:m, :128], PT[:, c, :m], vtok(c), first and c == 0, last and c == nc_ - 1, reads=[PTn, v_res], writes=[pOn])

    def nsa_prompt(self, st, it):
        P = self.P
        tag, qT, mixT, gs = st['tag'], st['qT'], st['mixT'], st['gsig']
        nk = W * (it + 1)
        ncmp = 32 * (it + 1)
        u = 0
        for g in range(4):
            P.dma('sp', self.ksg[:, 0:nk], self.ksT_d[g, :, 0:nk], reads=['ksT_d'], writes=['ksg'])
            P.dma('sp', self.vsg[:, 0:4 * (it + 1), :], self.vs_d[0:nk, g * 128:(g + 1) * 128].rearrange("(c p) d -> p c d", p=128),
                  reads=['vs_d'], writes=['vsg'])
            w0 = max(it - 1, 0) * W
            nw = nk - w0
            P.dma('sp', self.kwg[:, 0:nw], self.kwT_d[g, :, w0:nk], reads=['kwT_d'], writes=['kwg'])
            P.dma('sp', self.vwg[:, 0:nw // 128, :], self.vw_d[w0:nk, g * 128:(g + 1) * 128].rearrange("(c p) d -> p c d", p=128),
                  reads=['vw_d'], writes=['vwg'])
            for s in range(4):
                si = it * 4 + s
                qs = slice(s * 128, (s + 1) * 128)
                acc = self.nsa_acc
                for r in range(3):
                    h = 3 * g + r
                    pS, pSn = self.pf[u % 2], f'pf{u % 2}'
                    pT, pTn = self.pb[u % 2], f'pb{u % 2}'
                    pO, pOn = self.pf[2 + u % 2], f'pf{2 + u % 2}'
                    u += 1
                    Ec, Pc = self.cmp_e, self.cmp_p
                    self.mm(pS[:, :ncmp], qT[:, h, qs], self.cmpKT[:, g, 0:ncmp], True, True,
                            reads=[(tag, 'qT', h), 'cmpKT'], writes=[pSn])
                    self.act(Ec[:, :ncmp], pS[:, :ncmp], AF.Exp, scale=SCALE, reads=[pSn], writes=['cmp_e'])
                    self.stt(Ec[:, :ncmp], Ec[:, :ncmp], 1.0, self.cmask[:, si, 0:ncmp], ALU.mult, ALU.mult,
                             reads=['cmp_e', 'const'], writes=['cmp_e', 'cmp_den'], accum_out=self.cmp_den[:, 0:1])
                    self.ts(self.cmp_den[:, 1:2], self.cmp_den[:, 0:1], 1e-30, None, ALU.max, None, reads=['cmp_den'], writes=['cmp_den'])
                    self.recip(self.cmp_den[:, 1:2], self.cmp_den[:, 1:2], reads=['cmp_den'], writes=['cmp_den'])
                    self.ts(Pc[:, :ncmp], Ec[:, :ncmp], self.cmp_den[:, 1:2], None, ALU.mult, None, reads=['cmp_e', 'cmp_den'], writes=['cmp_p'])
                    for i2 in range(it + 1):
                        self.tr(pT[0:32, i2 * 128:(i2 + 1) * 128], Pc[:, 32 * i2:32 * i2 + 32], self.ident_b[:, :],
                                reads=['cmp_p', 'const'], writes=[pTn])
                    PnT = self.cmp_pt
                    self.cp(PnT[:, 0:it + 1, :], pT[0:32, 0:(it + 1) * 128].rearrange("p (c q) -> p c q", q=128), reads=[pTn], writes=['cmp_pt'], eng='act')
                    for i2 in range(it + 1):
                        self.mm(pO[:, :128], PnT[:, i2, :], self.cmpV[i2][:, g * 128:(g + 1) * 128], i2 == 0, i2 == it,
                                reads=['cmp_pt', ('cmpV', i2)], writes=[pOn])
                    for i2 in range(it + 1):
                        self.mm(self.pf[4][:, 0:32], PnT[:, i2, :], self.ovl[:, i2, :], r == 0 and i2 == 0, r == 2 and i2 == it,
                                reads=['cmp_pt', 'const'], writes=['pf4'])
                    self.ts(acc[:, r, :], pO[:, :128], gs[:, s, h:h + 1], None, ALU.mult, None, reads=[pOn, (tag, 'gsig')], writes=[('nsa_acc', r)])
                sc, sc2, m8 = self.tk_sc, self.tk_sc2, self.tk_m8
                self.tt(sc[:, :], self.pf[4][:, 0:32], self.tk_mul[:, si, :], ALU.mult, reads=['pf4', 'const'], writes=['tk_sc'])
                self.tt(sc[:, :], sc[:, :], self.tk_add[:, si, :], ALU.add, reads=['tk_sc', 'const'], writes=['tk_sc'])
                P.op('dve', lambda e: e.max(m8[:, 0:8], sc[:, :]), reads=['tk_sc'], writes=['tk_m8'])
                P.op('dve', lambda e: e.match_replace(sc2[:, :], m8[:, 0:8], sc[:, :], NEG), reads=['tk_sc', 'tk_m8'], writes=['tk_sc2'])
                P.op('dve', lambda e: e.max(m8[:, 8:16], sc2[:, :]), reads=['tk_sc2'], writes=['tk_m8'])
                self.ts(sc2[:, :], sc[:, :], m8[:, 15:16], None, ALU.is_ge, None, reads=['tk_sc', 'tk_m8'], writes=['tk_sc2'])
                self.tt(self.selmask[:, :], sc2[:, :], self.tk_elig[:, si, :], ALU.mult, reads=['tk_sc2', 'const'], writes=['selmask'])
                self.tt(self.mdiag[:, :].rearrange("p (b k) -> p b k", k=64), self.tri[:, s, :].rearrange("p (b k) -> p b k", k=64),
                        self.selmask[:, 8 * it:8 * it + 8].unsqueeze(2).to_broadcast([128, 8, 64]), ALU.mult,
                        reads=['selmask', 'const'], writes=['mdiag'])
                for r in range(3):
                    h = 3 * g + r
                    pO, pOn = self.pf[2 + r % 2], f'pf{2 + r % 2}'
                    dn = self.sel_den
                    for kg in range(it + 1):
                        if kg == it:
                            mfn = lambda: self.mdiag[:, :]
                            mres = ['mdiag']
                        else:
                            mfn = lambda kg=kg: self.selmask[:, 8 * kg:8 * kg + 8].unsqueeze(2).to_broadcast([128, 8, 64])
                            mres = ['selmask']
                        self.attn_unit(u, qT[:, h, qs], (tag, 'qT', h), 128, self.ksg[:, kg * W:(kg + 1) * W], 'ksg',
                                       lambda c, kg=kg: self.vsg[:, kg * 4 + c, :], 'vsg', W, mfn, mres,
                                       dn[:, kg:kg + 1], ('sel_den', kg), pO, pOn, kg == 0, kg == it)
                        u += 1
                    self.cp(dn[:, 4:5], dn[:, 0:1], reads=[('sel_den', 0)], writes=[('sel_den', 4)], eng='dve')
                    for k2 in range(1, it + 1):
                        self.tt(dn[:, 4:5], dn[:, 4:5], dn[:, k2:k2 + 1], ALU.add, reads=[('sel_den', 4), ('sel_den', k2)], writes=[('sel_den', 4)])
                    self.ts(dn[:, 4:5], dn[:, 4:5], 1e-30, None, ALU.max, None, reads=[('sel_den', 4)], writes=[('sel_den', 4)])
                    self.recip(dn[:, 5:6], dn[:, 4:5], reads=[('sel_den', 4)], writes=[('sel_den', 5)])
                    self.tt(dn[:, 5:6], dn[:, 5:6], gs[:, s, 12 + h:13 + h], ALU.mult, reads=[('sel_den', 5), (tag, 'gsig')], writes=[('sel_den', 5)])
                    self.stt(acc[:, r, :], pO[:, :128], dn[:, 5:6], acc[:, r, :], ALU.mult, ALU.add,
                             reads=[pOn, ('sel_den', 5), ('nsa_acc', r)], writes=[('nsa_acc', r)])
                for r in range(3):
                    h = 3 * g + r
                    pO, pOn = self.pf[2 + r % 2], f'pf{2 + r % 2}'
                    dn = self.win_den
                    ngrp = nw // W
                    for kg in range(ngrp):
                        if kg == ngrp - 1:
                            mfn = lambda: self.tri[:, s, :]
                        else:
                            mfn = lambda: self.wprev[:, s, :]
                        self.attn_unit(u, qT[:, h, qs], (tag, 'qT', h), 128, self.kwg[:, kg * W:(kg + 1) * W], 'kwg',
                                       lambda c, kg=kg: self.vwg[:, kg * 4 + c, :], 'vwg', W, mfn, ['const'],
                                       dn[:, kg:kg + 1], ('win_den', kg), pO, pOn, kg == 0, kg == ngrp - 1)
                        u += 1
                    if ngrp > 1:
                        self.tt(dn[:, 4:5], dn[:, 0:1], dn[:, 1:2], ALU.add, reads=[('win_den', 0), ('win_den', 1)], writes=[('win_den', 4)])
                    else:
                        self.cp(dn[:, 4:5], dn[:, 0:1], reads=[('win_den', 0)], writes=[('win_den', 4)], eng='dve')
                    self.recip(dn[:, 5:6], dn[:, 4:5], reads=[('win_den', 4)], writes=[('win_den', 5)])
                    self.tt(dn[:, 5:6], dn[:, 5:6], gs[:, s, 24 + h:25 + h], ALU.mult, reads=[('win_den', 5), (tag, 'gsig')], writes=[('win_den', 5)])
                    mo, mon = self.att_o[r % 2], ('att_o', r % 2)
                    self.stt(mo[:, :], pO[:, :128], dn[:, 5:6], acc[:, r, :], ALU.mult, ALU.add,
                             reads=[pOn, ('win_den', 5), ('nsa_acc', r)], writes=[mon])
                    pT, pTn = self.pb[r % 2], f'pb{r % 2}'
                    self.tr(pT[:, 512:640], mo[:, :], self.ident_b[:, :], reads=[mon, 'const'], writes=[pTn])
                    self.cp(mixT[:, h, qs], pT[:, 512:640], reads=[pTn], writes=[(tag, 'mixT', h)], eng='dve')

    def sample_decl(self):
        self.din('xs', [128, 16])
        self.din('cache_kv', [1280 * 128, 2048])
        self.din('page_tab', [1, 128], I32)
        self.din('riota', [128, 1])
        self.din('cache_win', [512, 1024])
        self.din('cache_mem', [4, NMEM, 1024])
        self.din('st_mix', [128, 2, 12, 2])
        self.din('st_ffn', [128, 4, NF, 2])
        self.din('cmask_s', [3, 1024], BF16)
        self.din('tk_s', [3, 2, 264])
        self.din('wmask_s', [3, 512], BF16)
        self.din('ovl_s', [128, 8, 264], BF16)
        self.dout('ys', [128, 16])
        self.dout('kv_rows_s', [1, 2048])
        self.dout('win_s', [512, 1024])
        self.dout('conv_mix_s', [2, 128, 12, 2])
        self.dout('conv_ffn_s', [4, 128, NF, 2])
        self.ksT_s_d = self.dscr('ksT_s_d', [4, 128, 16384], BF16)
        self.vs_s_d = self.dscr('vs_s_d', [16384, 512], BF16)
        self.kwT_s_d = self.dscr('kwT_s_d', [4, 128, 512], BF16)
        self.vw_s_d = self.dscr('vw_s_d', [512, 512], BF16)
        self.cmpV_s_d = self.dscr('cmpV_s_d', [1024, 512], BF16)

    def sample_mem_setup(self):
        P = self.P
        cm = self.ins['cache_mem']
        mb = self.U[:, 0:2 * 1024].rearrange("p (c n) -> p c n", n=1024)
        mbn = [('U', i) for i in range(4)]
        kt = self.smallb
        for l in range(4):
            P.dma('pool', mb, cm[l].rearrange("(c p) n -> p c n", p=128), writes=mbn)
            P.dma('sp', self.memV_d[1, l].rearrange("(c p) n -> p c n", p=128), mb[:, :, 512:1024], reads=mbn, writes=[('memV_d', 1, l)])
            for h in range(4):
                pT, pTn = self.pb[h % 2], f'pb{h % 2}'
                for c in range(2):
                    self.tr(pT[:, c * 128:(c + 1) * 128], mb[:, c, h * 128:(h + 1) * 128], self.ident_b[:, :], reads=mbn + ['const'], writes=[pTn])
                self.cp(kt[:, h, 0:256], pT[:, 0:256], reads=[pTn], writes=[('smallb', h)], eng='act')
            P.dma('sp', self.memKT_d[1, l], kt[:, :, 0:256], reads=[('smallb', h) for h in range(4)], writes=[('memKT_d', 1, l)])

    def sample_gather_pass(self):
        P = self.P
        U = self.U
        ckv = self.ins['cache_kv']
        ptb = self.s_ptb
        P.dma('sp', ptb[:, :], self.ins['page_tab'].partition_broadcast(128), writes=['s_ptb'])
        P.dma('sp', self.s_riota[:, :], self.ins['riota'], writes=['s_riota'])
        self.ts(self.s_idx[:, :], ptb[:, :], 128.0, self.s_riota[:, 0:1], ALU.mult, ALU.add, reads=['s_ptb', 's_riota'], writes=['s_idx'])
        gat = [U[:, i * 4096:(i + 1) * 4096].bitcast(F32) for i in range(2)]
        gatn = [[('U', i * 8 + k) for k in range(8)] for i in range(2)]
        gb = [U[:, (16 + 4 * i) * 512:(20 + 4 * i) * 512] for i in range(2)]
        gbn = [[('U', 16 + 4 * i + k) for k in range(4)] for i in range(2)]
        kTst = U[:, 24 * 512:28 * 512].rearrange("p (g t) -> p g t", t=512)
        kTn = [('U', 24 + i) for i in range(4)]
        wres = U[:, 28 * 512:44 * 512].rearrange("p (k l e) -> p k l e", k=2, l=32)
        wresn = [('U', 28 + i) for i in range(16)]
        wc = self.ins['w_cmp']
        for kind in range(2):
            for half in range(2):
                P.dma('pool', wres[:, kind, half * 16:(half + 1) * 16, :], wc[kind, half * 16:(half + 1) * 16].rearrange("l d e -> d l e"),
                      writes=wresn)
        for i in range(32):
            for nm in ('kcbuf', 'vcbuf'):
                buf = getattr(self, nm)
                if i == 0:
                    P.op('dve', lambda e, buf=buf: e.memset(buf[:, :, 0:16], 0.0), writes=[nm])
                else:
                    self.cp(self.cmptmp[:, :, :], buf[:, :, 512:528], reads=[nm], writes=['cmptmp'], eng='dve')
                    self.cp(buf[:, :, 0:16], self.cmptmp[:, :, :], reads=['cmptmp'], writes=[nm], eng='dve')
            for pgl in range(4):
                pg = 4 * i + pgl
                G, Gn = gat[pg % 2], gatn[pg % 2]
                B, Bn = gb[pg % 2], gbn[pg % 2]
                P.idma(G[:, :], ckv, self.s_idx[:, pg:pg + 1], reads=['s_idx'], writes=Gn)
                self.cp(B[:, 0:1024], G[:, 0:1024], reads=Gn, writes=Bn, eng='dve')
                self.cp(B[:, 1024:2048], G[:, 1024:2048], reads=Gn, writes=Bn, eng='act')
                P.dma('sp', self.vs_s_d[pg * 128:(pg + 1) * 128, :], B[:, 1536:2048], reads=Bn, writes=['vs_s_d'])
                for kind in range(3):
                    pT, pTn = self.pb[kind % 2], f'pb{kind % 2}'
                    for g in range(4):
                        self.tr(pT[:, g * 128:(g + 1) * 128], B[:, kind * 512 + g * 128:kind * 512 + (g + 1) * 128], self.ident_b[:, :],
                                reads=Bn + ['const'], writes=[pTn])
                    src = pT[:, 0:512].rearrange("p (g t) -> p g t", t=128)
                    if kind == 0:
                        self.cp(self.kcbuf[:, :, 16 + pgl * 128:16 + (pgl + 1) * 128], src, reads=[pTn], writes=['kcbuf'], eng='act')
                    elif kind == 1:
                        self.cp(self.vcbuf[:, :, 16 + pgl * 128:16 + (pgl + 1) * 128], src, reads=[pTn], writes=['vcbuf'], eng='dve')
                    else:
                        self.cp(kTst[:, :, pgl * 128:(pgl + 1) * 128], src, reads=[pTn], writes=kTn, eng='act')
            P.dma('sp', self.ksT_s_d[:, :, i * 512:(i + 1) * 512].rearrange("g d t -> d g t"), kTst, reads=kTn, writes=['ksT_s_d'])
            for kind in range(2):
                for g in range(4):
                    for l in range(32):
                        if kind == 0:
                            self.mm(self.pf[g][:, 0:32], wres[:, 0, l, :], self.kcbuf[:, g, l:l + 16 * 31 + 1:16], l == 0, l == 31,
                                    reads=wresn + ['kcbuf'], writes=[f'pf{g}'])
                        else:
                            self.mm(self.pf[g][0:32, 0:128], self.vcbuf[:, g, l:l + 16 * 31 + 1:16], wres[:, 1, l, :], l == 0, l == 31,
                                    reads=wresn + ['vcbuf'], writes=[f'pf{g}'])
                for g in range(4):
                    if kind == 0:
                        self.act(self.cmpKT_s[:, g, 32 * i:32 * i + 32], self.pf[g][:, 0:32], AF.Identity, bias=self.posK[:, 0:1],
                                 reads=[f'pf{g}', 'posK'], writes=['cmpKT_s'])
                    else:
                        self.tt(self.cvst[i % 2][:, g * 128:(g + 1) * 128], self.pf[g][0:32, 0:128], self.posV[:, :], ALU.add,
                                reads=[f'pf{g}', 'posV'], writes=[('cvst', i % 2)])
            P.dma('sp', self.cmpV_s_d[32 * i:32 * i + 32, :], self.cvst[i % 2][:, :], reads=[('cvst', i % 2)], writes=['cmpV_s_d'])

    def sample_win_setup(self):
        P = self.P
        cw = self.ins['cache_win']
        wb = self.U[:, 0:4 * 1024].rearrange("p (c n) -> p c n", n=1024)
        wbn = [('U', i) for i in range(8)]
        kTst = self.U[:, 8 * 512:12 * 512].rearrange("p (g t) -> p g t", t=512)
        kTn = [('U', 8 + i) for i in range(4)]
        P.dma('pool', wb, cw.rearrange("(c p) n -> p c n", p=128), writes=wbn)
        P.dma('sp', self.vw_s_d.rearrange("(c p) n -> p c n", p=128), wb[:, :, 512:1024], reads=wbn, writes=['vw_s_d'])
        for c in range(4):
            pT, pTn = self.pb[c % 2], f'pb{c % 2}'
            for g in range(4):
                self.tr(pT[:, g * 128:(g + 1) * 128], wb[:, c, g * 128:(g + 1) * 128], self.ident_b[:, :], reads=wbn + ['const'], writes=[pTn])
            self.cp(kTst[:, :, c * 128:(c + 1) * 128], pT[:, 0:512].rearrange("p (g t) -> p g t", t=128), reads=[pTn], writes=kTn, eng='act')
        P.dma('sp', self.kwT_s_d.rearrange("g d t -> d g t"), kTst, reads=kTn, writes=['kwT_s_d'])
        P.dma('sp', self.outs['win_s'][0:511, :], cw[1:512, :])

    def sample_kv(self, st):
        P = self.P
        xT, tag = st['xT'], st['tag']
        w = self.ins['w_kv']
        kvf, kvb = self.s_kvf, self.s_kvb
        for c in range(6):
            ps, psn = self.pf[c % 2], f'pf{c % 2}'
            for kq in range(4):
                slot, sname = self.wload(w[kq * 512:(kq + 1) * 512, c * 512:(c + 1) * 512].rearrange("(k p) n -> p k n", p=128), [128, 4, 512])
                for kk in range(4):
                    k = kq * 4 + kk
                    self.mm(ps[0:1, :], xT[:, k, 0:1], slot[:, kk, :], k == 0, k == 15, reads=[sname, (tag, 'xT', k)], writes=[psn])
            self.cp(kvf[0:1, c * 512:(c + 1) * 512], ps[0:1, :], reads=[psn], writes=['s_kvf'], eng='act')
        self.cp(kvb[0:1, :], kvf[0:1, :], reads=['s_kvf'], writes=['s_kvb'], eng='dve')
        P.dma('sp', self.outs['kv_rows_s'], kvf[0:1, 0:2048], reads=['s_kvf'])
        P.dma('sp', self.outs['win_s'][511:512, :], kvf[0:1, 2048:3072], reads=['s_kvf'])
        pT, pTn = self.pb[0], 'pb0'
        for i, c0 in enumerate([1024 + g * 128 for g in range(4)] + [2048 + g * 128 for g in range(4)]):
            self.tr(pT[:, 2 * i:2 * i + 1], kvb[0:1, c0:c0 + 128], self.ident_b[0:1, 0:1], reads=['s_kvb', 'const'], writes=[pTn])
        self.cp(self.s_knT[:, 0:8], pT[:, 0:16:2], reads=[pTn], writes=['s_knT'], eng='act')

    def new_key_unit(self, u, q3, q_res, knT_col, vrow, mask_ap, mask_res, den_ap, den_res, pO, pOn, first, last):
        pS, pSn = self.pf[u % 2], f'pf{u % 2}'
        pT, pTn = self.pb[u % 2], f'pb{u % 2}'
        e1 = self.s_e1
        self.mm(pS[:3, 0:1], q3, knT_col, True, True, reads=[q_res, 's_knT'], writes=[pSn])
        self.act(e1[:3, 0:1], pS[:3, 0:1], AF.Exp, scale=SCALE, reads=[pSn], writes=['s_e1'])
        if mask_ap is not None:
            self.tt(e1[:3, 0:1], e1[:3, 0:1], mask_ap, ALU.mult, reads=['s_e1'] + mask_res, writes=['s_e1'])
        self.cp(den_ap, e1[:3, 0:1], reads=['s_e1'], writes=[den_res], eng='dve')
        self.cp(e1[:3, 1:2], e1[:3, 0:1], reads=['s_e1'], writes=['s_e1b'], eng='dve')
        self.cp(self.s_e1b[:3, 0:1], e1[:3, 0:1], reads=['s_e1'], writes=['s_e1b'], eng='dve')
        self.tr(pT[0:1, 0:3], self.s_e1b[:3, 0:1], self.ident_b[:3, :3], reads=['s_e1b', 'const'], writes=[pTn])
        self.cp(self.s_pt1[0:1, 0:3], pT[0:1, 0:3], reads=[pTn], writes=['s_pt1'], eng='act')
        self.mm(pO[:3, :128], self.s_pt1[0:1, 0:3], vrow, first, last, reads=['s_pt1', 's_kvb'], writes=[pOn])

    def nsa_sample(self, st, jj):
        P = self.P
        tag, qT, mixT, xT = st['tag'], st['qT'], st['mixT'], st['xT']
        w_in = self.ins['w_in_b'][jj]
        cmpV = self.U[:, 0:8 * 512].rearrange("p (c n) -> p c n", n=512)
        cvn = [('U', i) for i in range(8)]
        P.dma('sp', cmpV, self.cmpV_s_d.rearrange("(c p) n -> p c n", p=128), reads=['cmpV_s_d'], writes=cvn)
        gslot, gname = self.wload(w_in[:, DA:DA + 36].rearrange("(k p) n -> p k n", p=128), [128, 16, 36])
        u = 0
        for g in range(4):
            q3 = qT[:, 3 * g:3 * g + 3, 0:1].rearrange("p h o -> p (h o)")
            qres = (tag, 'qT', 3 * g)
            qresl = [(tag, 'qT', 3 * g + r) for r in range(3)]
            pg_, pgn = self.pf[4], 'pf4'
            for b in range(3):
                for k in range(NCH):
                    self.mm(pg_[0:3, b:b + 1], gslot[:, k, 12 * b + 3 * g:12 * b + 3 * g + 3], xT[:, k, 0:1], k == 0, k == NCH - 1,
                            reads=[gname, (tag, 'xT', k)], writes=[pgn])
            gs = self.s_gs
            self.act(gs[0:3, 0:3], pg_[0:3, 0:3], AF.Sigmoid, reads=[pgn], writes=['s_gs'])
            Ec, Pc = self.s_ec, self.s_pc
            for kg in range(2):
                pS, pSn = self.pf[u % 2], f'pf{u % 2}'
                u += 1
                self.mm(pS[:3, :], q3, self.cmpKT_s[:, g, kg * 512:(kg + 1) * 512], True, True, reads=qresl + ['cmpKT_s'], writes=[pSn])
                self.act(Ec[:3, kg * 512:(kg + 1) * 512], pS[:3, :], AF.Exp, scale=SCALE, reads=[pSn], writes=['s_ec'])
            dn = self.s_den
            self.stt(Ec[:3, :], Ec[:3, :], 1.0, self.cmask_s[:3, :], ALU.mult, ALU.mult, reads=['s_ec', 'sconst'], writes=['s_ec', 's_den'],
                     accum_out=dn[:3, 0:1])
            self.recip(dn[:3, 1:2], dn[:3, 0:1], reads=['s_den'], writes=['s_den'])
            self.ts(Pc[:3, :], Ec[:3, :], dn[:3, 1:2], None, ALU.mult, None, reads=['s_ec', 's_den'], writes=['s_pc'])
            pT, pTn = self.pb[0], 'pb0'
            for c in range(8):
                self.tr(pT[:, c * 4:c * 4 + 3], Pc[:3, c * 128:(c + 1) * 128], self.ident_b[:3, :3], reads=['s_pc', 'const'], writes=[pTn])
            PT8 = self.s_pt8
            self.cp(PT8[:, :, 0:3], pT[:, 0:32].rearrange("p (c q) -> p c q", q=4)[:, :, 0:3], reads=[pTn], writes=['s_pt8'], eng='act')
            pO, pOn = self.pf[2], 'pf2'
            for c in range(8):
                self.mm(pO[:3, :128], PT8[:, c, 0:3], cmpV[:, c, g * 128:(g + 1) * 128], c == 0, c == 7, reads=['s_pt8'] + cvn, writes=[pOn])
            acc = self.s_acc
            self.ts(acc[:3, :], pO[:3, :128], gs[0:3, 0:1], None, ALU.mult, None, reads=[pOn, 's_gs'], writes=['s_acc'])
            psum3 = self.s_psum
            self.tt(psum3[:, :], PT8[:, :, 0], PT8[:, :, 1], ALU.add, reads=['s_pt8'], writes=['s_psum'])
            self.tt(psum3[:, :], psum3[:, :], PT8[:, :, 2], ALU.add, reads=['s_pt8', 's_psum'], writes=['s_psum'])
            prep = self.s_prep
            self.cp(prep[:, :, :], psum3[:, :].unsqueeze(2).to_broadcast([128, 8, 3]), reads=['s_psum'], writes=['s_prep'], eng='dve')
            pI, pIn = self.pf[3], 'pf3'
            for c in range(8):
                self.mm(pI[:3, 0:264], prep[:, c, :], self.ovl_s[:, c, :], c == 0, c == 7, reads=['s_prep', 'sconst'], writes=[pIn])
            sc, sc2, m8 = self.s_sc, self.s_sc2, self.s_m8
            self.tt(sc[:3, :], pI[:3, 0:264], self.tk_s[:3, 0, :], ALU.mult, reads=[pIn, 'sconst'], writes=['s_sc'])
            self.tt(sc[:3, :], sc[:3, :], self.tk_s[:3, 1, :], ALU.add, reads=['s_sc', 'sconst'], writes=['s_sc'])
            P.op('dve', lambda e: e.max(m8[:3, 0:8], sc[:3, :]), reads=['s_sc'], writes=['s_m8'])
            P.op('dve', lambda e: e.match_replace(sc2[:3, :], m8[:3, 0:8], sc[:3, :], NEG), reads=['s_sc', 's_m8'], writes=['s_sc2'])
            P.op('dve', lambda e: e.max(m8[:3, 8:16], sc2[:3, :]), reads=['s_sc2'], writes=['s_m8'])
            selm = self.s_selm
            self.ts(selm[:3, :], sc[:3, :], m8[:3, 15:16], None, ALU.is_ge, None, reads=['s_sc', 's_m8'], writes=['s_selm'])
            pO, pOn = self.pf[2], 'pf2'
            dsel = self.s_dsel
            for kk in range(8):
                P.dma('sp', self.ksg[:, :], self.ksT_s_d[g, :, kk * 2048:(kk + 1) * 2048], reads=['ksT_s_d'], writes=['ksg'])
                P.dma('sp', self.vsg[:, :, :], self.vs_s_d[kk * 2048:(kk + 1) * 2048, g * 128:(g + 1) * 128].rearrange("(c p) d -> p c d", p=128),
                      reads=['vs_s_d'], writes=['vsg'])
                for k4 in range(4):
                    kg = kk * 4 + k4
                    mfn = lambda kg=kg: selm[:3, 8 * kg:8 * kg + 8].unsqueeze(2).to_broadcast([3, 8, 64])
                    self.attn_unit(u, q3, qres, 3, self.ksg[:, k4 * W:(k4 + 1) * W], 'ksg',
                                   lambda c, k4=k4: self.vsg[:, k4 * 4 + c, :], 'vsg', W, mfn, ['s_selm'],
                                   dsel[:3, kg:kg + 1], 's_dsel', pO, pOn, kg == 0, False)
                    u += 1
            self.new_key_unit(u, q3, qres, self.s_knT[:, g:g + 1], self.s_kvb[0:1, 1536 + g * 128:1536 + (g + 1) * 128],
                              selm[:3, 256:257], ['s_selm'], dsel[:3, 32:33], 's_dsel', pO, pOn, False, True)
            u += 1
            self.act(self.s_sc2[:3, 0:33], dsel[:3, 0:33], AF.Copy, reads=['s_dsel', 's_sc2'], writes=['s_sc2', 's_dsel2'], accum_out=dsel[:3, 36:37])
            self.recip(dsel[:3, 37:38], dsel[:3, 36:37], reads=['s_dsel2'], writes=['s_dsel2'])
            self.tt(dsel[:3, 37:38], dsel[:3, 37:38], gs[0:3, 1:2], ALU.mult, reads=['s_dsel2', 's_gs'], writes=['s_dsel2'])
            self.stt(acc[:3, :], pO[:3, :128], dsel[:3, 37:38], acc[:3, :], ALU.mult, ALU.add, reads=[pOn, 's_dsel2', 's_acc'], writes=['s_acc'])
            pO, pOn = self.pf[3], 'pf3'
            dw = self.s_dwin
            P.dma('sp', self.kwg[:, 0:512], self.kwT_s_d[g, :, :], reads=['kwT_s_d'], writes=['kwg'])
            P.dma('sp', self.vwg[:, 0:4, :], self.vw_s_d[:, g * 128:(g + 1) * 128].rearrange("(c p) d -> p c d", p=128), reads=['vw_s_d'], writes=['vwg'])
            self.attn_unit(u, q3, qres, 3, self.kwg[:, 0:W], 'kwg', lambda c: self.vwg[:, c, :], 'vwg', W,
                           lambda: self.wmask_s[:3, :], ['sconst'], dw[:3, 0:1], 's_dwin', pO, pOn, True, False)
            u += 1
            self.new_key_unit(u, q3, qres, self.s_knT[:, 4 + g:5 + g], self.s_kvb[0:1, 2560 + g * 128:2560 + (g + 1) * 128],
                              None, [], dw[:3, 1:2], 's_dwin', pO, pOn, False, True)
            u += 1
            self.tt(dw[:3, 2:3], dw[:3, 0:1], dw[:3, 1:2], ALU.add, reads=['s_dwin'], writes=['s_dwin2'])
            self.recip(dw[:3, 3:4], dw[:3, 2:3], reads=['s_dwin2'], writes=['s_dwin2'])
            self.tt(dw[:3, 3:4], dw[:3, 3:4], gs[0:3, 2:3], ALU.mult, reads=['s_dwin2', 's_gs'], writes=['s_dwin2'])
            mo = self.s_mo
            self.stt(mo[:3, :], pO[:3, :128], dw[:3, 3:4], acc[:3, :], ALU.mult, ALU.add, reads=[pOn, 's_dwin2', 's_acc'], writes=['s_mo'])
            pT, pTn = self.pb[1], 'pb1'
            self.tr(pT[:, 0:3], mo[:3, :], self.ident_b[:3, :3], reads=['s_mo', 'const'], writes=[pTn])
            self.cp(mixT[:, 3 * g:3 * g + 3, 0:1].rearrange("p h o -> p (h o)"), pT[:, 0:3], reads=[pTn],
                    writes=[(tag, 'mixT', 3 * g + r) for r in range(3)], eng='dve')

    def sample_path(self):
        P = self.P
        XR = self.st_p['xres']
        XT = self.st_p['xT']
        xrn = lambda a, b: [('p', 'xres', j) for j in range(a, b)]
        xtn = lambda a, b: [('p', 'xT', j) for j in range(a, b)]
        self.s_ptb = self.sb("s_ptb", [128, 128], I32)
        self.s_riota = self.sb("s_riota", [128, 1], F32)
        self.s_idx = self.sb("s_idx", [128, 128], I32)
        self.cmpKT_s = XT[:, 0:8, :].rearrange("p (g a) t -> p g (a t)", g=4)
        P.define('cmpKT_s', xtn(0, 8))
        self.s_kvb = XT[0:1, 8:14, :].rearrange("p c t -> p (c t)")
        P.define('s_kvb', xtn(8, 14))
        self.s_pc = XT[0:3, 14:16, :].rearrange("p c t -> p (c t)")
        P.define('s_pc', xtn(14, 16))
        self.s_kvf = XR[0:1, 0:6, :].rearrange("p c t -> p (c t)")
        P.define('s_kvf', xrn(0, 6))
        self.s_ec = XR[0:3, 6:8, :].rearrange("p c t -> p (c t)")
        P.define('s_ec', xrn(6, 8))
        self.ovl_s = XR[:, 8:11, :].rearrange("p c t -> p (c t)").bitcast(BF16)[:, 0:8 * 264].rearrange("p (c j) -> p c j", j=264)
        self.tk_s = XR[0:3, 11:13, :].rearrange("p c t -> p (c t)")[:, 0:528].rearrange("p (a j) -> p a j", j=264)
        self.cmask_s = XR[0:3, 13, :].bitcast(BF16)
        self.wmask_s = XR[0:3, 14, 0:256].bitcast(BF16)
        P.define('sconst', xrn(8, 15))
        self.cvst = [self.sb(f"cvst{i}", [32, 512], BF16) for i in range(2)]
        self.s_knT = self.sb("s_knT", [128, 8], BF16)
        self.s_e1 = self.sb("s_e1", [3, 2], F32)
        self.s_e1b = self.sb("s_e1b", [3, 2], BF16)
        self.s_pt1 = self.sb("s_pt1", [1, 4], BF16)
        self.s_gs = self.sb("s_gs", [3, 4], F32)
        self.s_den = self.sb("s_den", [3, 4], F32)
        self.s_pt8 = self.sb("s_pt8", [128, 8, 4], BF16)
        self.s_acc = self.sb("s_acc", [3, 128], F32)
        self.s_psum = self.sb("s_psum", [128, 8], F32)
        self.s_prep = self.sb("s_prep", [128, 8, 3], BF16)
        self.s_sc = self.sb("s_sc", [3, 264], F32)
        self.s_sc2 = self.sb("s_sc2", [3, 264], F32)
        self.s_m8 = self.sb("s_m8", [3, 16], F32)
        self.s_selm = self.sb("s_selm", [3, 264], BF16)
        self.s_dsel = self.sb("s_dsel", [3, 40], F32)
        self.s_dwin = self.sb("s_dwin", [3, 4], F32)
        self.s_mo = self.sb("s_mo", [3, 128], BF16)
        for nm in ('cmask_s', 'tk_s', 'wmask_s', 'ovl_s'):
            P.dma('sp', getattr(self, nm), self.ins[nm], writes=['sconst'])
        xres = self.sb("xres_s", [128, 16, 1], F32)
        xT = self.sb("xT_s", [128, 16, 1], BF16)
        mixT = self.sb("mixT_s", [128, 16, 1], BF16)
        hT = self.sb("hT_s", [128, NF, 1], BF16)
        qmT = self.sb("qmT_s", [128, 4, 1], BF16)
        qT = self.sb("qT_s", [128, 12, 1], BF16)
        gsig = self.sb("gsig_s", [1, 1, 36], F32)
        carry_mix = self.sb("carry_mix_s", [128, 2, 12, 2], F32)
        carry_ffn = self.sb("carry_ffn_s", [128, 4, NF, 2], F32)
        P.dma('sp', carry_mix[:], self.ins['st_mix'], writes=[('s', 'carry_mix', 0), ('s', 'carry_mix', 1)])
        P.dma('sp', carry_ffn[:], self.ins['st_ffn'], writes=[('s', 'carry_ffn', l) for l in range(4)])
        st = dict(xres=xres, xT=xT, mixT=mixT, hT=hT, qmT=qmT, qT=qT, ncols=1, tag='s', subw=[1], gsig=gsig,
                  memKT=self.st_p['memKT'], memV=self.st_p['memV'], memres='p_mem',
                  carry_mix=[carry_mix[:, l, :, :] for l in range(2)],
                  carry_ffn=[carry_ffn[:, l, :, :] for l in range(4)])
        P.dma('sp', xres[:, :, 0:1].rearrange("p c o -> p (c o)"), self.ins['xs'], writes=[('s', 'xres', j) for j in range(NCH)])
        self.cp(xT[:, :, :], xres[:, :, :], reads=[('s', 'xres', j) for j in range(NCH)], writes=[('s', 'xT', j) for j in range(NCH)], eng='dve')
        self.sample_mem_setup()
        self.sample_win_setup()
        self.sample_gather_pass()
        for l in range(2):
            self.load_mem(st, 1, l)
            self.a_mixer(st, l)
            self.mem_attention(st)
            self.layer_tail(st, l)
        self.sample_kv(st)
        for jj in range(2):
            l = 2 + jj
            self.load_mem(st, 1, l)
            self.b_inproj(st, jj)
            self.nsa_sample(st, jj)
            self.mem_attention(st)
            self.layer_tail(st, l)
        P.dma('sp', self.outs['ys'], xres[:, :, 0:1].rearrange("p c o -> p (c o)"), reads=[('s', 'xres', j) for j in range(NCH)])
        for l in range(2):
            P.dma('sp', self.outs['conv_mix_s'][l], carry_mix[:, l, :, :], reads=[('s', 'carry_mix', l)])
        for l in range(4):
            P.dma('sp', self.outs['conv_ffn_s'][l], carry_ffn[:, l, :, :], reads=[('s', 'carry_ffn', l)])


def _bf16(a):
    return np.asarray(a, dtype=np.float32).astype(ml_dtypes.bfloat16)


def make_consts():
    c = {}
    c['ident_f'] = np.eye(128, dtype=np.float32)
    c['ident_b'] = _bf16(np.eye(128, dtype=np.float32))
    c['ones_b'] = _bf16(np.full((128, 128), 1.0 / D, dtype=np.float32))
    return c


def make_inmap(inp, b, c, NT, do_sample=False):
    T = NT * W
    m = {}
    m['x'] = np.ascontiguousarray(inp['x_prompt'][b, :T])
    m['memp'] = np.ascontiguousarray(inp['mem_prompt'][b])
    for k in ('w_in_a', 'w_in_b', 'w_o', 'w_mem_kv', 'w_up', 'w_down', 'w_cmp'):
        m[k] = np.ascontiguousarray(inp[k])
    m['w_kv'] = np.ascontiguousarray(inp['w_kv_shared'])
    m['lng'] = np.ascontiguousarray(inp['ln_g'].reshape(4, 2, 16, 128).transpose(3, 0, 1, 2))
    m['lnb'] = np.ascontiguousarray(inp['ln_b'].reshape(4, 2, 16, 128).transpose(3, 0, 1, 2))
    m['mixcw'] = np.ascontiguousarray(inp['conv_a_w'].reshape(2, 3, 12, 128).transpose(3, 0, 2, 1))
    m['ffncw'] = np.ascontiguousarray(inp['ffn_conv_w'].reshape(4, 3, NF, 128).transpose(3, 0, 2, 1))
    m.update(make_consts())
    m.update(make_masks(NT))
    m.update(make_pos(inp))
    if do_sample:
        m.update(make_sample(inp, c))
    return m


def make_masks(NT):
    c = {}
    T = NT * W
    p = np.arange(128)
    nsl = T // 64
    cm = np.zeros((128, NT * 4, 128), np.float32)
    mul = np.zeros((128, NT * 4, 32), np.float32)
    add = np.zeros((128, NT * 4, 32), np.float32)
    elig = np.zeros((128, NT * 4, 32), np.float32)
    for it in range(NT):
        for s in range(4):
            qpos = it * W + s * 128 + p
            npr = np.arange(128)
            n = npr - 1
            valid = (npr[None, :] >= 1) & (16 * n[None, :] + 31 <= qpos[:, None]) & (npr[None, :] < 32 * (it + 1))
            cm[:, it * 4 + s, :] = valid
            blk = np.arange(32)
            cur = qpos // 64
            el = (blk[None, :] * 64 <= qpos[:, None]) & (blk[None, :] < nsl)
            forced = ((blk[None, :] == 0) | (blk[None, :] == cur[:, None]) | (blk[None, :] == cur[:, None] - 1)) & (blk[None, :] < nsl)
            mul[:, it * 4 + s, :] = (el & ~forced)
            add[:, it * 4 + s, :] = np.where(forced, 1e9, np.where(el, 0.0, -1e9))
            elig[:, it * 4 + s, :] = el
    c['cmask'] = _bf16(cm)
    c['tk_mul'] = _bf16(mul)
    c['tk_add'] = _bf16(add)
    c['tk_elig'] = _bf16(elig)
    col = np.arange(512)
    tri = np.zeros((128, 4, 512), np.float32)
    for s in range(4):
        tri[:, s, :] = col[None, :] <= (128 * s + p)[:, None]
    c['tri'] = _bf16(tri)
    c['wprev'] = _bf16(1.0 - tri)
    ovl = np.zeros((32, 4, 32), np.float32)
    for i2 in range(4):
        for m in range(32):
            n = 32 * i2 + m - 1
            if n < 0:
                continue
            for j in range(32):
                if (16 * n <= 64 * j + 63) and (16 * n + 31 >= 64 * j):
                    ovl[m, i2, j] = 1.0
    c['ovl'] = _bf16(ovl)
    return c


def make_pos(inp):
    c = {}
    cp = np.asarray(inp['cmp_pos'], np.float32)
    c['posT'] = _bf16(cp.transpose(2, 0, 1))
    c['posrep'] = _bf16(np.repeat(cp[1].T[:, :, None], 32, axis=2))
    return c


def make_sample(inp, c):
    m = {}
    m['xs'] = np.ascontiguousarray(inp['x_sample'][c, 0].reshape(16, 128).T)
    m['cache_kv'] = inp['cache_kv'].reshape(1280 * 128, 2048)
    m['page_tab'] = np.ascontiguousarray(inp['page_table'][c:c + 1]).astype(np.int32)
    m['riota'] = np.arange(128, dtype=np.float32).reshape(128, 1)
    m['cache_win'] = np.ascontiguousarray(inp['cache_win'][c].reshape(512, 1024))
    m['cache_mem'] = np.ascontiguousarray(inp['cache_mem'][:, c].reshape(4, NMEM, 1024))
    m['st_mix'] = np.ascontiguousarray(inp['state_conv_mix'][:, c].reshape(2, 2, 12, 128).transpose(3, 0, 2, 1))
    m['st_ffn'] = np.ascontiguousarray(inp['state_conv_ffn'][:, c].reshape(4, 2, NF, 128).transpose(3, 0, 2, 1))
    cm = np.ones((3, 1024), np.float32)
    cm[:, 0] = 0
    m['cmask_s'] = _bf16(cm)
    tk = np.zeros((3, 2, 264), np.float32)
    j = np.arange(264)
    forced = (j == 0) | (j == 255) | (j == 256)
    real = j <= 256
    tk[:, 0, :] = (real & ~forced)
    tk[:, 1, :] = np.where(forced, 1e9, np.where(real, 0.0, -1e9))
    m['tk_s'] = tk
    wm = np.ones((3, 512), np.float32)
    wm[:, 0] = 0
    m['wmask_s'] = _bf16(wm)
    ov = np.zeros((128, 8, 264), np.float32)
    npr = (np.arange(8)[None, :] * 128 + np.arange(128)[:, None])
    n = npr - 1
    jj = np.arange(264)
    o = (16 * n[:, :, None] <= 64 * jj[None, None, :] + 63) & (16 * n[:, :, None] + 31 >= 64 * jj[None, None, :]) \
        & (n[:, :, None] >= 0) & (jj[None, None, :] <= 256)
    m['ovl_s'] = _bf16(o.astype(np.float32))
    return m


_NC_CACHE = {}


def kernel(**inputs):
    inp = {k: np.asarray(v) for k, v in inputs.items()}
    NT = 4
    if 'nc' not in _NC_CACHE:
        bld = Builder(NT=NT, n_layers=4, upto='all', do_sample=True)
        _NC_CACHE['nc'] = bld.build()
    nc = _NC_CACHE['nc']
    in_maps = [make_inmap(inp, c // 2, c, NT, do_sample=True) for c in range(8)]
    res = run_bass_kernel_spmd(nc, in_maps, core_ids=list(range(8)))
    R = res.results
    f32 = np.float32

    def fm(a, n):
        return np.asarray(a).transpose(2, 1, 0).reshape(2, n * 128)
    y_prompt = np.stack([R[2 * b]['y'] for b in range(4)]).astype(f32)
    y_sample = np.stack([R[c]['ys'].T.reshape(1, D) for c in range(8)]).astype(f32)
    kv_rows_prompt = np.stack([R[2 * b]['kv_rows'].reshape(NT * W, 4, 4, 128) for b in range(4)]).astype(f32)
    win_prompt = np.stack([R[2 * b]['win'].reshape(512, 2, 4, 128) for b in range(4)]).astype(f32)
    mem_kv = np.stack([R[2 * b]['mem_kv'].reshape(4, NMEM, 2, 4, 128) for b in range(4)], axis=1).astype(f32)
    conv_mix_p = np.stack([np.stack([fm(R[2 * b]['conv_mix'][l], 12) for b in range(4)]) for l in range(2)]).astype(f32)
    conv_ffn_p = np.stack([np.stack([fm(R[2 * b]['conv_ffn'][l], NF) for b in range(4)]) for l in range(4)]).astype(f32)
    kv_rows_s = np.stack([R[c]['kv_rows_s'].reshape(1, 4, 4, 128) for c in range(8)]).astype(f32)
    win_s = np.stack([R[c]['win_s'].reshape(512, 2, 4, 128) for c in range(8)]).astype(f32)
    conv_mix_s = np.stack([np.stack([fm(R[c]['conv_mix_s'][l], 12) for c in range(8)]) for l in range(2)]).astype(f32)
    conv_ffn_s = np.stack([np.stack([fm(R[c]['conv_ffn_s'][l], NF) for c in range(8)]) for l in range(4)]).astype(f32)
    return (y_prompt, y_sample, kv_rows_prompt, win_prompt, mem_kv, conv_mix_p, conv_ffn_p,
            kv_rows_s, win_s, conv_mix_s, conv_ffn_s)
```

```python
import numpy as np
import ml_dtypes
from contextlib import ExitStack
import concourse.bass as bass
import concourse.mybir as mybir
from concourse.bass_utils import run_bass_kernel_spmd

F32 = mybir.dt.float32
BF16 = mybir.dt.bfloat16
I32 = mybir.dt.int32
AF = mybir.ActivationFunctionType
ALU = mybir.AluOpType
AX = mybir.AxisListType

D = 2048
NCH = 16
DA = 1536
DFF = 5504
NF = 43
NMEM = 256
W = 512
ALPHA = 8 ** 0.25
EPS = 1e-5
SCALE = 128 ** -0.5
NEG = -3.0e38

GEN = 1000000000
NDMA = 24
NGEN = 1
NSLOT = 6


class Prog:
    ENG = ['pe', 'act', 'dve', 'pool', 'sp']

    def __init__(self, nc, stack):
        self.nc = nc
        self.q = {e: [] for e in self.ENG}
        self.cnt = {e: 0 for e in self.ENG}
        self.known = {e: {} for e in self.ENG}
        self.res = {}
        self.sems = {}
        for e in ('pe', 'act', 'dve'):
            for g in range(NGEN):
                self.sems[(e, g)] = stack.enter_context(nc.semaphore(f"s_{e}_{g}"))
        self.sems[('pool', 0)] = stack.enter_context(nc.semaphore("s_pool_0"))
        for i in range(NDMA):
            self.sems[('dma', i)] = stack.enter_context(nc.semaphore(f"s_dma_{i}"))
        self.dma_tot = [0] * NDMA
        self.dma_rr = 0
        self.alias = {}

    def define(self, name, cells):
        self.alias[name] = list(cells)

    def _expand(self, names):
        out = []
        for n in names:
            a = self.alias.get(n)
            if a is None:
                out.append(n)
            else:
                out.extend(a)
        return out

    def _collect(self, eng, reads, writes):
        reads = self._expand(reads)
        writes = self._expand(writes)
        need = {}

        def add(dd):
            for k, v in dd.items():
                if need.get(k, 0) < v:
                    need[k] = v
        for r in reads:
            ent = self.res.get(r)
            if ent:
                add(ent[0])
        for r in writes:
            ent = self.res.get(r)
            if ent:
                add(ent[0])
                add(ent[1])
        waits = []
        kn = self.known[eng]
        for k, v in need.items():
            if eng == 'pe' and k[0] == 'pe':
                continue
            if kn.get(k, 0) < v:
                waits.append((k, v))
                kn[k] = v
        return waits

    def _update(self, key, val, reads, writes):
        reads = self._expand(reads)
        writes = self._expand(writes)
        for r in reads:
            ent = self.res.get(r)
            if ent is None:
                ent = self.res[r] = ({}, {})
            if ent[1].get(key, 0) < val:
                ent[1][key] = val
        for r in writes:
            self.res[r] = ({key: val}, {})

    @staticmethod
    def _excl(reads, writes):
        ps = [r for r in reads if isinstance(r, str) and (r.startswith('pf') or r.startswith('pb'))]
        if ps:
            reads = [r for r in reads if r not in ps]
            writes = list(writes) + ps
        return reads, writes

    def op(self, eng, fn, reads=(), writes=()):
        reads, writes = self._excl(reads, writes)
        waits = self._collect(eng, reads, writes)
        self.cnt[eng] += 1
        g, v = divmod(self.cnt[eng] - 1, GEN)
        key = (eng, g)
        sems = self.sems

        def emit(e):
            for k, val in waits:
                e.wait_ge(sems[k], val)
            fn(e).then_inc(sems[key], 1)
        self.q[eng].append(emit)
        self._update(key, v + 1, reads, writes)

    def dma(self, eng, out, in_, reads=(), writes=(), **kw):
        s = self.dma_rr
        self.dma_rr = (s + 1) % NDMA
        key = ('dma', s)
        waits = self._collect(eng, reads, writes)
        prev = self.dma_tot[s]
        if prev and self.known[eng].get(key, 0) < prev:
            waits.append((key, prev))
            self.known[eng][key] = prev
        self.dma_tot[s] = prev + 16
        sems = self.sems

        def emit(e):
            for k, val in waits:
                e.wait_ge(sems[k], val)
            e.dma_start(out=out, in_=in_, **kw).then_inc(sems[key], 16)
        self.q[eng].append(emit)
        self._update(key, prev + 16, reads, writes)

    def idma(self, out, in_, idx_ap, reads=(), writes=()):
        eng = 'pool'
        s = self.dma_rr
        self.dma_rr = (s + 1) % NDMA
        key = ('dma', s)
        waits = self._collect(eng, reads, writes)
        prev = self.dma_tot[s]
        if prev and self.known[eng].get(key, 0) < prev:
            waits.append((key, prev))
            self.known[eng][key] = prev
        self.dma_tot[s] = prev + 16
        sems = self.sems

        def emit(e):
            for k, val in waits:
                e.wait_ge(sems[k], val)
            e.indirect_dma_start(out=out, out_offset=None, in_=in_,
                                 in_offset=bass.IndirectOffsetOnAxis(ap=idx_ap, axis=0)).then_inc(sems[key], 16)
        self.q[eng].append(emit)
        self._update(key, prev + 16, reads, writes)

    def finish(self):
        waits = []
        for i in range(NDMA):
            if self.dma_tot[i]:
                waits.append((('dma', i), self.dma_tot[i]))
        for e in self.ENG:
            if e == 'sp' or self.cnt[e] == 0:
                continue
            g, v = divmod(self.cnt[e] - 1, GEN)
            waits.append(((e, g), v + 1))
        sems = self.sems

        def emit(e):
            for k, val in waits:
                e.wait_ge(sems[k], val)
        self.q['sp'].append(emit)

    def run_block(self):
        nc = self.nc
        q = self.q
        with nc.Block() as block:
            @block.sync
            def _(e):
                for f in q['sp']:
                    f(e)

            @block.tensor
            def _(e):
                for f in q['pe']:
                    f(e)

            @block.scalar
            def _(e):
                for f in q['act']:
                    f(e)

            @block.vector
            def _(e):
                for f in q['dve']:
                    f(e)

            @block.gpsimd
            def _(e):
                for f in q['pool']:
                    f(e)


class Builder:
    def __init__(self, NT=4, do_sample=True, n_layers=4, upto='all'):
        self.NT = NT
        self.T = NT * W
        self.do_sample = do_sample
        self.n_layers = n_layers
        self.upto = upto
        self.nc = bass.Bass("TRN2", target_bir_lowering=False)
        self.stack = ExitStack()
        self.ins = {}
        self.outs = {}
        self.scr = {}

    def din(self, name, shape, dt=F32):
        t = self.nc.dram_tensor(name, list(shape), dt, kind="ExternalInput").ap()
        self.ins[name] = t
        return t

    def dout(self, name, shape, dt=F32):
        t = self.nc.dram_tensor(name, list(shape), dt, kind="ExternalOutput").ap()
        self.outs[name] = t
        return t

    def dscr(self, name, shape, dt):
        t = self.nc.dram_tensor(name, list(shape), dt, kind="ExternalOutput").ap()
        self.scr[name] = t
        return t

    def sb(self, name, shape, dt):
        return self.stack.enter_context(self.nc.sbuf_tensor("sb_" + name, list(shape), dt))

    def ps(self, name, shape, dt):
        return self.stack.enter_context(self.nc.psum_tensor(name, list(shape), dt))

    def wload(self, src, shape):
        i = self.wslot
        self.wslot = (i + 1) % NSLOT
        n = 1
        for s in shape[1:]:
            n *= s
        assert n <= 2048
        flat = self.wring[0:shape[0], i, 0:n]
        if len(shape) == 3:
            view = flat.rearrange("p (a b) -> p a b", b=shape[2])
        else:
            view = flat
        name = ('ws', i)
        self.P.dma('pool', view, src, writes=[name])
        return view, name

    def mm(self, out, lhsT, rhs, start, stop, reads, writes):
        self.P.op('pe', lambda e: e.matmul(out, lhsT=lhsT, rhs=rhs, start=start, stop=stop), reads=reads, writes=writes)

    def tr(self, out, in_, ident, reads, writes):
        self.P.op('pe', lambda e: e.transpose(out, in_, ident), reads=reads, writes=writes)

    def act(self, out, in_, func, reads, writes, **kw):
        self.P.op('act', lambda e: e.activation(out=out, in_=in_, func=func, **kw), reads=reads, writes=writes)

    def tt(self, out, in0, in1, op, reads, writes, eng='dve'):
        self.P.op(eng, lambda e: e.tensor_tensor(out=out, in0=in0, in1=in1, op=op), reads=reads, writes=writes)

    def stt(self, out, in0, scalar, in1, op0, op1, reads, writes, eng='dve', **kw):
        self.P.op(eng, lambda e: e.scalar_tensor_tensor(out=out, in0=in0, scalar=scalar, in1=in1, op0=op0, op1=op1, **kw),
                  reads=reads, writes=writes)

    def ts(self, out, in0, s1, s2, op0, op1, reads, writes, eng='dve', **kw):
        if op1 is None:
            self.P.op(eng, lambda e: e.tensor_scalar(out=out, in0=in0, scalar1=s1, scalar2=None, op0=op0, **kw),
                      reads=reads, writes=writes)
        else:
            self.P.op(eng, lambda e: e.tensor_scalar(out=out, in0=in0, scalar1=s1, scalar2=s2, op0=op0, op1=op1, **kw),
                      reads=reads, writes=writes)

    def cp(self, out, in_, reads, writes, eng='dve'):
        if eng == 'act':
            self.act(out, in_, AF.Copy, reads, writes)
        else:
            self.P.op(eng, lambda e: e.tensor_copy(out, in_), reads=reads, writes=writes)

    def recip(self, out, in_, reads, writes):
        self.P.op('dve', lambda e: e.reciprocal(out, in_), reads=reads, writes=writes)

    def proj_fm(self, wsrc, nk_total, rhs_fn, rhs_res, ps, psname, ncols):
        k0 = 0
        while k0 < nk_total:
            nk = min(16, nk_total - k0)
            slot, sname = self.wload(wsrc(k0, nk), [128, nk, 128])
            for kk in range(nk):
                k = k0 + kk
                self.mm(ps[:, :ncols], slot[:, kk, :], rhs_fn(k), k == 0, k == nk_total - 1,
                        reads=[sname, rhs_res(k)], writes=[psname])
            k0 += nk

    @staticmethod
    def wcols(w3, c0):
        return lambda k0, nk: w3[k0 * 128:(k0 + nk) * 128, c0:c0 + 128].rearrange("(k p) n -> p k n", p=128)

    def layernorm(self, st, l, which):
        ncols, tag, xres, xT = st['ncols'], st['tag'], st['xres'], st['xT']
        pm, pq = self.pf[4], self.pf[5]
        for j in range(NCH):
            xb, xbn = self.lnb16[j % 2], ('lnb16', j % 2)
            sq, sqn = self.lnb16[2 + j % 2], ('lnb16', 2 + j % 2)
            self.cp(xb[:, :ncols], xres[:, j, :ncols], reads=[(tag, 'xres', j)], writes=[xbn], eng='dve')
            self.act(sq[:, :ncols], xres[:, j, :ncols], AF.Square, reads=[(tag, 'xres', j)], writes=[sqn])
            self.mm(pm[:, :ncols], self.ones_b[:, :], xb[:, :ncols], j == 0, j == NCH - 1,
                    reads=[xbn, 'const'], writes=['pf4'])
            self.mm(pq[:, :ncols], self.ones_b[:, :], sq[:, :ncols], j == 0, j == NCH - 1,
                    reads=[sqn, 'const'], writes=['pf5'])
        mean, rstd, tmp = self.lnmean, self.lnrstd, self.lntmp
        self.cp(mean[:, :ncols], pm[:, :ncols], reads=['pf4'], writes=['lnmean'], eng='act')
        self.tt(tmp[:, :ncols], mean[:, :ncols], mean[:, :ncols], ALU.mult, reads=['lnmean'], writes=['lntmp'])
        self.tt(tmp[:, :ncols], pq[:, :ncols], tmp[:, :ncols], ALU.subtract, reads=['pf5', 'lntmp'], writes=['lntmp'])
        self.ts(tmp[:, :ncols], tmp[:, :ncols], EPS, None, ALU.add, None, reads=['lntmp'], writes=['lntmp'])
        self.act(rstd[:, :ncols], tmp[:, :ncols], AF.Sqrt, reads=['lntmp'], writes=['lnrstd'])
        self.recip(rstd[:, :ncols], rstd[:, :ncols], reads=['lnrstd'], writes=['lnrstd'])
        g = self.lng[:, l, which, :]
        b = self.lnb[:, l, which, :]
        for j in range(NCH):
            t2 = self.zf[j % 2]
            self.tt(t2[:, :ncols], xres[:, j, :ncols], mean[:, :ncols], ALU.subtract,
                    reads=[(tag, 'xres', j), 'lnmean'], writes=[('zf', j % 2)])
            self.stt(t2[:, :ncols], t2[:, :ncols], g[:, j:j + 1], rstd[:, :ncols], ALU.mult, ALU.mult,
                     reads=[('zf', j % 2), 'lnrstd', 'const'], writes=[('zf', j % 2)])
            self.act(xres[:, j, :ncols], t2[:, :ncols], AF.Identity, bias=b[:, j:j + 1],
                     reads=[('zf', j % 2), 'const'], writes=[(tag, 'xres', j)])
            self.cp(xT[:, j, :ncols], xres[:, j, :ncols], reads=[(tag, 'xres', j)], writes=[(tag, 'xT', j)], eng='dve')

    def mem_attention(self, st):
        tag, qmT, mixT = st['tag'], st['qmT'], st['mixT']
        memKT, memV, kres = st['memKT'], st['memV'], st['memres']
        u = 0
        units = []
        for h in range(4):
            for s, m in enumerate(st['subw']):
                units.append(self._mem_unit(u, h, s, m, tag, qmT, mixT, memKT, memV, kres))
                u += 1
        self.run_pipelined(units)

    def _mem_unit(self, u, h, s, m, tag, qmT, mixT, memKT, memV, kres):
        pS, pSn = self.pf[u % 2], f'pf{u % 2}'
        E, En = self.att_e[u % 2], ('att_e', u % 2)
        PT, PTn = self.att_pt[u % 2], ('att_pt', u % 2)
        pT, pTn = self.pb[u % 2], f'pb{u % 2}'
        pO, pOn = self.pf[2 + u % 2], f'pf{2 + u % 2}'
        den, denn = self.att_den[:, u % 2, :], ('att_den', u % 2)
        mo, mon = self.att_o[u % 2], ('att_o', u % 2)

        def front():
            self.mm(pS[:m, :256], qmT[:, h, s * 128:s * 128 + m], memKT[:, h, :], True, True,
                    reads=[(tag, 'qmT', h), kres], writes=[pSn])
            self.act(E[:m, :256], pS[:m, :256], AF.Exp, scale=SCALE, accum_out=den[:m, 0:1],
                     reads=[pSn], writes=[En, denn])

        def back():
            for c in range(2):
                self.tr(pT[:, c * 128:c * 128 + m], E[:m, c * 128:(c + 1) * 128], self.ident_b[:m, :m],
                        reads=[En, 'const'], writes=[pTn])
            self.cp(PT[:, 0:2, :m], pT[:, 0:256].rearrange("p (c q) -> p c q", q=128)[:, :, :m],
                    reads=[pTn], writes=[PTn], eng='act')
            for c in range(2):
                self.mm(pO[:m, :128], PT[:, c, :m], memV[:, c, h * 128:(h + 1) * 128], c == 0, c == 1,
                        reads=[PTn, kres], writes=[pOn])
            self.recip(den[:m, 1:2], den[:m, 0:1], reads=[denn], writes=[denn])
            self.ts(mo[:m, :], pO[:m, :128], den[:m, 1:2], None, ALU.mult, None, reads=[pOn, denn], writes=[mon])
            self.tr(pT[:, 512:512 + m], mo[:m, :], self.ident_b[:m, :m], reads=[mon, 'const'], writes=[pTn])
            self.cp(mixT[:, 12 + h, s * 128:s * 128 + m], pT[:, 512:512 + m], reads=[pTn], writes=[(tag, 'mixT', 12 + h)],
                    eng='dve')
        return front, back

    def layer_tail(self, st, l):
        ncols, tag = st['ncols'], st['tag']
        xres, xT, mixT, hT = st['xres'], st['xT'], st['mixT'], st['hT']
        w_o, w_up, w_down = self.ins['w_o'], self.ins['w_up'], self.ins['w_down']
        for j in range(NCH):
            ps, psn = self.pf[j % 4], f'pf{j % 4}'
            self.proj_fm(self.wcols(w_o[l], j * 128), NCH, lambda k: mixT[:, k, :ncols], lambda k: (tag, 'mixT', k), ps, psn, ncols)
            self.stt(xres[:, j, :ncols], xres[:, j, :ncols], ALPHA, ps[:, :ncols], ALU.mult, ALU.add,
                     reads=[psn, (tag, 'xres', j)], writes=[(tag, 'xres', j)])
        if self.upto == 'wo':
            return
        self.layernorm(st, l, 0)
        if self.upto == 'ln1':
            return
        cw = self.ffncw[:, l, :, :]
        carry = st['carry_ffn'][l]
        cres = (tag, 'carry_ffn', l)
        for f in range(NF):
            b0 = (2 * f) % 4
            pz, pzn = self.pf[b0], f'pf{b0}'
            pg, pgn = self.pf[b0 + 1], f'pf{b0 + 1}'
            self.proj_fm(self.wcols(w_up[l], f * 128), NCH, lambda k: xT[:, k, :ncols], lambda k: (tag, 'xT', k), pz, pzn, ncols)
            self.proj_fm(self.wcols(w_up[l], DFF + f * 128), NCH, lambda k: xT[:, k, :ncols], lambda k: (tag, 'xT', k), pg, pgn, ncols)
            zf, zfn = self.zf[f % 2], ('zf', f % 2)
            t1, t1n = self.ct[f % 2], ('ct', f % 2)
            self.cp(zf[:, 0:2], carry[:, f, :], reads=[cres], writes=[zfn], eng='dve')
            self.cp(zf[:, 2:2 + ncols], pz[:, :ncols], reads=[pzn], writes=[zfn], eng='act')
            self.cp(carry[:, f, :], zf[:, ncols:ncols + 2], reads=[zfn], writes=[cres], eng='dve')
            self.act(t1[:, :ncols], zf[:, 0:ncols], AF.Copy, scale=cw[:, f, 0:1], reads=[zfn, 'const'], writes=[t1n])
            self.stt(t1[:, :ncols], zf[:, 1:1 + ncols], cw[:, f, 1:2], t1[:, :ncols], ALU.mult, ALU.add,
                     reads=[zfn, t1n, 'const'], writes=[t1n])
            self.stt(t1[:, :ncols], zf[:, 2:2 + ncols], cw[:, f, 2:3], t1[:, :ncols], ALU.mult, ALU.add,
                     reads=[zfn, t1n, 'const'], writes=[t1n])
            self.act(t1[:, :ncols], t1[:, :ncols], AF.Silu, reads=[t1n], writes=[t1n])
            self.tt(hT[:, f, :ncols], t1[:, :ncols], pg[:, :ncols], ALU.mult, reads=[t1n, pgn], writes=[(tag, 'hT', f)])
        if self.upto == 'ffn_up':
            return
        for j in range(NCH):
            ps, psn = self.pf[j % 4], f'pf{j % 4}'
            self.proj_fm(self.wcols(w_down[l], j * 128), NF, lambda k: hT[:, k, :ncols], lambda k: (tag, 'hT', k), ps, psn, ncols)
            self.stt(xres[:, j, :ncols], xres[:, j, :ncols], ALPHA, ps[:, :ncols], ALU.mult, ALU.add,
                     reads=[psn, (tag, 'xres', j)], writes=[(tag, 'xres', j)])
        self.layernorm(st, l, 1)

    def a_mixer(self, st, l):
        ncols, tag = st['ncols'], st['tag']
        xT, mixT, qmT = st['xT'], st['mixT'], st['qmT']
        w_in = self.ins['w_in_a'][l]
        cw = self.mixcw[:, l, :, :]
        carry = st['carry_mix'][l]
        cres = (tag, 'carry_mix', l)
        xr = lambda k: xT[:, k, :ncols]
        xn = lambda k: (tag, 'xT', k)
        for j in range(12):
            base = 3 * (j % 2)
            pu, pb_, pc = self.pf[base], self.pf[base + 1], self.pf[base + 2]
            pun, pbn, pcn = f'pf{base}', f'pf{base + 1}', f'pf{base + 2}'
            for (ps, psn, off) in ((pu, pun, 0), (pb_, pbn, DA), (pc, pcn, 2 * DA)):
                self.proj_fm(self.wcols(w_in, off + j * 128), NCH, xr, xn, ps, psn, ncols)
            us, usn = self.ct[j % 2], ('ct', j % 2)
            cuf, cufn = self.zf[j % 2], ('zf', j % 2)
            self.cp(us[:, :ncols], pu[:, :ncols], reads=[pun], writes=[usn], eng='act')
            self.cp(cuf[:, 0:2], carry[:, j, :], reads=[cres], writes=[cufn], eng='dve')
            self.tt(cuf[:, 2:2 + ncols], pc[:, :ncols], us[:, :ncols], ALU.mult, reads=[pcn, usn], writes=[cufn])
            self.cp(carry[:, j, :], cuf[:, ncols:ncols + 2], reads=[cufn], writes=[cres], eng='dve')
            self.act(us[:, :ncols], cuf[:, 0:ncols], AF.Copy, scale=cw[:, j, 0:1], reads=[cufn, 'const'], writes=[usn])
            self.stt(us[:, :ncols], cuf[:, 1:1 + ncols], cw[:, j, 1:2], us[:, :ncols], ALU.mult, ALU.add,
                     reads=[cufn, usn, 'const'], writes=[usn])
            self.stt(us[:, :ncols], cuf[:, 2:2 + ncols], cw[:, j, 2:3], us[:, :ncols], ALU.mult, ALU.add,
                     reads=[cufn, usn, 'const'], writes=[usn])
            self.tt(mixT[:, j, :ncols], us[:, :ncols], pb_[:, :ncols], ALU.mult, reads=[usn, pbn], writes=[(tag, 'mixT', j)])
        for h in range(4):
            ps, psn = self.pf[h % 2], f'pf{h % 2}'
            self.proj_fm(self.wcols(w_in, 3 * DA + h * 128), NCH, xr, xn, ps, psn, ncols)
            self.cp(qmT[:, h, :ncols], ps[:, :ncols], reads=[psn], writes=[(tag, 'qmT', h)], eng='act')

    def mem_project_all(self):
        w = self.ins['w_mem_kv']
        mpT = self.U[:, 0:16 * NMEM].rearrange("p (c t) -> p c t", t=NMEM)
        P = self.P
        mpn = [('U', i) for i in range(8)]
        stg = [self.U[:, 8192 + i * 4096: 8192 + (i + 1) * 4096].bitcast(F32) for i in range(2)]
        stgn = [[('U', 16 + i * 8 + k) for k in range(8)] for i in range(2)]
        for s in range(2):
            self.load_transpose(self.ins['memp'][s * 128:(s + 1) * 128, :], 128, stg[s], stgn[s], None, mpT, s * 128,
                                None, lambda jq: mpn)
        ob = self.smallf
        import os
        mps = int(os.environ.get('MP_STAGE', '9'))
        for l in range(4 if mps >= 2 else 0):
            for c in range(2):
                pss = [(self.pf[0], 'pf0'), (self.pf[1], 'pf1')]
                for kq in range(4):
                    slot, sname = self.wload(w[l, kq * 512:(kq + 1) * 512, c * 512:(c + 1) * 512].rearrange("(k p) n -> p k n", p=128),
                                             [128, 4, 512])
                    for s in range(2):
                        for kk in range(4):
                            k = kq * 4 + kk
                            self.mm(pss[s][0][:, :], mpT[:, k, s * 128:(s + 1) * 128], slot[:, kk, :], k == 0, k == 15,
                                    reads=[sname] + mpn, writes=[pss[s][1]])
                for s in range(2):
                    self.cp(ob[:, s, :], pss[s][0][:, :], reads=[pss[s][1]], writes=[('smallf', s)], eng='act')
                    P.dma('sp', self.outs['mem_kv'][l, s * 128:(s + 1) * 128, c * 512:(c + 1) * 512], ob[:, s, :],
                          reads=[('smallf', s)])
                    if c == 1 and mps >= 3:
                        self.cp(self.smallb[:, s, :], pss[s][0][:, :], reads=[pss[s][1]], writes=[('smallb', s)], eng='dve')
                        P.dma('sp', self.memV_d[0, l, s * 128:(s + 1) * 128, :], self.smallb[:, s, :], reads=[('smallb', s)],
                              writes=[('memV_d', 0, l)])
            for h in range(4 if mps >= 4 else 0):
                ps, psn = self.pf[2 + h % 2], f'pf{2 + h % 2}'
                self.proj_fm(self.wcols(w[l], h * 128), NCH, lambda k: mpT[:, k, :], lambda k: mpn[k // 2], ps, psn, 256)
                self.cp(self.smallb[:, 2 + h % 2, 0:256], ps[:, :256], reads=[psn], writes=[('smallb', 2 + h % 2)], eng='act')
                P.dma('sp', self.memKT_d[0, l, :, h, :], self.smallb[:, 2 + h % 2, 0:256], reads=[('smallb', 2 + h % 2)],
                      writes=[('memKT_d', 0, l)])

    def load_mem(self, st, grp, l):
        P = self.P
        P.dma('sp', st['memKT'][:], self.memKT_d[grp, l], reads=[('memKT_d', grp, l)], writes=[st['memres']])
        P.dma('sp', st['memV'][:], self.memV_d[grp, l].rearrange("(c p) n -> p c n", p=128), reads=[('memV_d', grp, l)],
              writes=[st['memres']])

    def load_transpose(self, src_rows, m, stg, stgn, dstf, dstb, col0, resf, resb):
        self.P.dma('sp', stg[:m, :], src_rows, writes=stgn)
        for jq in range(4):
            pt, ptn = self.pf[jq % 2], f'pf{jq % 2}'
            for i in range(4):
                j = jq * 4 + i
                self.tr(pt[:, i * 128:i * 128 + m], stg[:m, j * 128:(j + 1) * 128], self.ident_f[:m, :m],
                        reads=stgn + ['const'], writes=[ptn])
            src = pt[:, :].rearrange("p (c q) -> p c q", q=128)[:, :, :m]
            if dstf is not None:
                self.cp(dstf[:, jq * 4:jq * 4 + 4, col0:col0 + m], src, reads=[ptn], writes=resf(jq), eng='act')
            if dstb is not None:
                self.cp(dstb[:, jq * 4:jq * 4 + 4, col0:col0 + m], src, reads=[ptn], writes=resb(jq), eng='dve')

    def build(self):
        nc = self.nc
        NT, T = self.NT, self.T
        self.din('x', [T, D])
        self.din('memp', [NMEM, D])
        self.din('w_in_a', [2, D, 3 * DA + 512])
        self.din('w_in_b', [2, D, DA + 36 + 512])
        self.din('w_o', [4, D, D])
        self.din('w_mem_kv', [4, D, 1024])
        self.din('w_up', [4, D, 2 * DFF])
        self.din('w_down', [4, DFF, D])
        self.din('w_kv', [D, 3072])
        self.din('w_cmp', [2, 32, 128, 128])
        self.din('lng', [128, 4, 2, 16])
        self.din('lnb', [128, 4, 2, 16])
        self.din('mixcw', [128, 2, 12, 3])
        self.din('ffncw', [128, 4, NF, 3])
        self.din('ident_f', [128, 128])
        self.din('ident_b', [128, 128], BF16)
        self.din('ones_b', [128, 128], BF16)
        self.dout('y', [T, D])
        self.dout('mem_kv', [4, NMEM, 1024])
        self.dout('conv_mix', [2, 128, 12, 2])
        self.dout('conv_ffn', [4, 128, NF, 2])
        self.dout('dbg', [128, 16, W])
        self.dout('kv_rows', [T, 2048])
        self.dout('win', [512, 1024])
        self.din('cmask', [128, NT * 4, 128], BF16)
        self.din('tk_mul', [128, NT * 4, 32], BF16)
        self.din('tk_add', [128, NT * 4, 32], BF16)
        self.din('tk_elig', [128, NT * 4, 32], BF16)
        self.din('tri', [128, 4, 512], BF16)
        self.din('wprev', [128, 4, 512], BF16)
        self.din('ovl', [32, 4, 32], BF16)
        self.din('posT', [128, 2, 32], BF16)
        self.din('posrep', [128, 32, 32], BF16)
        if self.do_sample:
            self.sample_decl()
        self.ksT_d = self.dscr('ksT_d', [4, 128, T], BF16)
        self.kwT_d = self.dscr('kwT_d', [4, 128, T], BF16)
        self.vs_d = self.dscr('vs_d', [T, 512], BF16)
        self.vw_d = self.dscr('vw_d', [T, 512], BF16)
        self.memKT_d = self.dscr('memKT_d', [2, 4, 128, 4, NMEM], BF16)
        self.memV_d = self.dscr('memV_d', [2, 4, NMEM, 512], BF16)

        with self.stack:
            self.P = P = Prog(nc, self.stack)
            self.pf = [self.ps(f"pf{i}", [128, 512], F32) for i in range(6)]
            self.pb = [self.ps(f"pb{i}", [128, 1024], BF16) for i in range(2)]
            self.wring = self.sb("wring", [128, NSLOT, 2048], BF16)
            self.wslot = 0
            self.ident_f = self.sb("ident_f", [128, 128], F32)
            self.ident_b = self.sb("ident_b", [128, 128], BF16)
            self.ones_b = self.sb("ones_b", [128, 128], BF16)
            self.lnb16 = [self.sb(f"lnb16_{i}", [128, W], BF16) for i in range(4)]
            self.lng = self.sb("lng", [128, 4, 2, 16], F32)
            self.lnb = self.sb("lnb", [128, 4, 2, 16], F32)
            self.mixcw = self.sb("mixcw", [128, 2, 12, 3], F32)
            self.ffncw = self.sb("ffncw", [128, 4, NF, 3], F32)
            self.cmask = self.sb("cmask", [128, NT * 4, 128], BF16)
            self.tk_mul = self.sb("tk_mul", [128, NT * 4, 32], BF16)
            self.tk_add = self.sb("tk_add", [128, NT * 4, 32], BF16)
            self.tk_elig = self.sb("tk_elig", [128, NT * 4, 32], BF16)
            self.tri = self.sb("tri", [128, 4, 512], BF16)
            self.wprev = self.sb("wprev", [128, 4, 512], BF16)
            self.ovl = self.sb("ovl", [32, 4, 32], BF16)
            self.posT = self.sb("posT", [128, 2, 32], BF16)
            self.posrep = self.sb("posrep", [128, 32, 32], BF16)
            for nm in ('ident_f', 'ident_b', 'ones_b', 'lng', 'lnb', 'mixcw', 'ffncw', 'cmask', 'tk_mul', 'tk_add', 'tk_elig',
                       'tri', 'wprev', 'ovl', 'posT', 'posrep'):
                P.dma('sp', getattr(self, nm)[:], self.ins[nm], writes=['const'])
            self.kcbuf = self.sb("kcbuf", [128, 4, 528], BF16)
            self.vcbuf = self.sb("vcbuf", [128, 4, 528], BF16)
            self.cmptmp = self.sb("cmptmp", [128, 4, 16], BF16)
            self.cmpKT = self.sb("cmpKT", [128, 4, 128], BF16)
            self.cmpV = [self.sb(f"cmpV{i}", [32, 512], BF16) for i in range(NT)]
            self.posK = self.sb("posK", [128, 1], F32)
            self.posV = self.sb("posV", [32, 128], F32)
            self.cmp_e = self.sb("cmp_e", [128, 128], F32)
            self.cmp_p = self.sb("cmp_p", [128, 128], BF16)
            self.cmp_pt = self.sb("cmp_pt", [32, 4, 128], BF16)
            self.cmp_den = self.sb("cmp_den", [128, 2], F32)
            self.sel_den = [self.sb(f"sel_den{i}", [128, 8], F32) for i in range(3)]
            self.win_den = [self.sb(f"win_den{i}", [128, 8], F32) for i in range(3)]
            self.nsa_acc = self.sb("nsa_acc", [128, 3, 128], F32)
            self.tk_sc = self.sb("tk_sc", [128, 32], F32)
            self.tk_sc2 = self.sb("tk_sc2", [128, 32], F32)
            self.tk_m8 = self.sb("tk_m8", [128, 16], F32)
            self.selmask = self.sb("selmask", [128, 32], BF16)
            self.mdiag = self.sb("mdiag", [128, 512], BF16)
            gsig = self.sb("gsig", [128, 4, 36], F32)
            self.lnmean = self.sb("lnmean", [128, W], F32)
            self.lnrstd = self.sb("lnrstd", [128, W], F32)
            self.lntmp = self.sb("lntmp", [128, W], F32)
            self.zf = [self.sb(f"zf{i}", [128, W + 2], F32) for i in range(2)]
            self.ct = [self.sb(f"ct{i}", [128, W], F32) for i in range(2)]
            self.att_e = [self.sb(f"att_e{i}", [128, 512], BF16) for i in range(2)]
            self.att_pt = [self.sb(f"att_pt{i}", [128, 4, 128], BF16) for i in range(2)]
            self.att_o = [self.sb(f"att_o{i}", [128, 128], BF16) for i in range(2)]
            self.att_den = self.sb("att_den", [128, 2, 4], F32)
            self.smallf = self.sb("smallf", [128, 2, 512], F32)
            self.smallb = self.sb("smallb", [128, 4, 512], BF16)
            NU = 44
            self.U = self.sb("U", [128, NU * 512], BF16)
            U = self.U
            hT = U[:, 0:NF * 512].rearrange("p (c t) -> p c t", t=512)
            mixT = U[:, 0:16 * 512].rearrange("p (c t) -> p c t", t=512)
            qmT = U[:, 16 * 512:20 * 512].rearrange("p (c t) -> p c t", t=512)
            qT = U[:, 20 * 512:32 * 512].rearrange("p (c t) -> p c t", t=512)
            for f in range(NF):
                P.define(('p', 'hT', f), [('U', f)])
            for j in range(16):
                P.define(('p', 'mixT', j), [('U', j)])
            for h in range(4):
                P.define(('p', 'qmT', h), [('U', 16 + h)])
            for h in range(12):
                P.define(('p', 'qT', h), [('U', 20 + h)])
            stg = [U[:, i * 4096:(i + 1) * 4096].bitcast(F32) for i in range(2)]
            stgn = [[('U', i * 8 + k) for k in range(8)] for i in range(2)]
            self.ksg = U[:, 32 * 512:36 * 512]
            self.vsg = U[:, 36 * 512:40 * 512].rearrange("p (c d) -> p c d", d=128)
            self.kwg = U[:, 40 * 512:42 * 512]
            self.vwg = U[:, 42 * 512:44 * 512].rearrange("p (c d) -> p c d", d=128)
            P.define('ksg', [('U', i) for i in range(32, 36)])
            P.define('vsg', [('U', i) for i in range(36, 40)])
            P.define('kwg', [('U', i) for i in range(40, 42)])
            P.define('vwg', [('U', i) for i in range(42, 44)])

            xres = self.sb("xres", [128, 16, W], F32)
            xT = self.sb("xT", [128, 16, W], BF16)
            memKT = self.sb("memKT", [128, 4, NMEM], BF16)
            memV = self.sb("memV", [128, 2, 512], BF16)
            carry_mix = self.sb("carry_mix", [128, 2, 12, 2], F32)
            carry_ffn = self.sb("carry_ffn", [128, 4, NF, 2], F32)
            P.op('dve', lambda e: e.memset(carry_mix[:], 0.0), writes=[('p', 'carry_mix', 0), ('p', 'carry_mix', 1)])
            P.op('dve', lambda e: e.memset(carry_ffn[:], 0.0), writes=[('p', 'carry_ffn', l) for l in range(4)])
            st = dict(xres=xres, xT=xT, mixT=mixT, hT=hT, qmT=qmT, qT=qT, ncols=W, tag='p', subw=[128] * 4, gsig=gsig,
                      memKT=memKT, memV=memV, memres='p_mem',
                      carry_mix=[carry_mix[:, l, :, :] for l in range(2)],
                      carry_ffn=[carry_ffn[:, l, :, :] for l in range(4)])

            stages = ['const', 'memproj', 'xload', 'mixer', 'memattn', 'wo', 'ln1', 'ffn_up', 'all']
            lvl = stages.index(self.upto)
            if lvl >= 1:
                self.mem_project_all()
            if self.n_layers > 2:
                self.cmp_pos_terms()

            for it in range(NT if lvl >= 2 else 0):
                for s in range(4):
                    self.load_transpose(self.ins['x'][it * W + s * 128: it * W + (s + 1) * 128, :], 128, stg[s % 2], stgn[s % 2],
                                        xres, xT, s * 128,
                                        lambda jq: [('p', 'xres', jq * 4 + i) for i in range(4)],
                                        lambda jq: [('p', 'xT', jq * 4 + i) for i in range(4)])
                for l in range(min(2, self.n_layers) if lvl >= 3 else 0):
                    self.load_mem(st, 0, l)
                    self.a_mixer(st, l)
                    if lvl >= 4:
                        self.mem_attention(st)
                    if lvl >= 5:
                        self.layer_tail(st, l)
                if self.n_layers > 2:
                    self.kv_project(st, it)
                    self.compress(it)
                    for jj in range(self.n_layers - 2):
                        l = 2 + jj
                        self.load_mem(st, 0, l)
                        self.b_inproj(st, jj)
                        self.nsa_prompt(st, it)
                        import os
                        if os.environ.get('DBG_MIX') and it == NT - 1 and jj == 0:
                            for j in range(12):
                                self.cp(self.zf[j % 2][:, 0:W], mixT[:, j, :], reads=[('p', 'mixT', j)], writes=[('zf', j % 2)], eng='dve')
                                P.dma('sp', self.outs['dbg'][:, j, :], self.zf[j % 2][:, 0:W], reads=[('zf', j % 2)])
                        self.mem_attention(st)
                        self.layer_tail(st, l)
                for s4 in range(4):
                    sg, sgn = stg[s4 % 2], stgn[s4 % 2]
                    for jq in range(4):
                        pt, ptn = self.pf[jq % 2], f'pf{jq % 2}'
                        for i in range(4):
                            j = jq * 4 + i
                            self.tr(pt[:, i * 128:(i + 1) * 128], xres[:, j, s4 * 128:(s4 + 1) * 128], self.ident_f[:, :],
                                    reads=[('p', 'xres', j), 'const'], writes=[ptn])
                        self.cp(sg[:, jq * 512:(jq + 1) * 512], pt[:, :], reads=[ptn], writes=sgn, eng='act')
                    P.dma('sp', self.outs['y'][it * W + s4 * 128:it * W + (s4 + 1) * 128, :], sg[:, :], reads=sgn)
            import os
            if not os.environ.get('DBG_MIX'):
                P.dma('sp', self.outs['dbg'], xres[:], reads=[('p', 'xres', j) for j in range(NCH)])
            for l in range(2):
                P.dma('sp', self.outs['conv_mix'][l], carry_mix[:, l, :, :], reads=[('p', 'carry_mix', l)])
            for l in range(4):
                P.dma('sp', self.outs['conv_ffn'][l], carry_ffn[:, l, :, :], reads=[('p', 'carry_ffn', l)])
            self.st_p = st
            if self.do_sample:
                self.sample_path()
            P.finish()
            P.run_block()
        return nc

    def kv_project(self, st, it):
        P = self.P
        xT, tag = st['xT'], st['tag']
        w = self.ins['w_kv']
        t0 = it * W
        U = self.U
        kTst = U[:, 0:4 * 512].rearrange("p (g t) -> p g t", t=512)
        kTn = [('U', i) for i in range(4)]
        kvb = U[:, 4 * 512:6 * 512].rearrange("p (a t) -> p a t", t=512)
        for nm in ('kcbuf', 'vcbuf'):
            buf = getattr(self, nm)
            if it == 0:
                P.op('dve', lambda e, buf=buf: e.memset(buf[:, :, 0:16], 0.0), writes=[nm])
            else:
                self.cp(self.cmptmp[:, :, :], buf[:, :, 512:528], reads=[nm], writes=['cmptmp'], eng='dve')
                self.cp(buf[:, :, 0:16], self.cmptmp[:, :, :], reads=['cmptmp'], writes=[nm], eng='dve')
        for c in range(6):
            for kq in range(4):
                slot, sname = self.wload(w[kq * 512:(kq + 1) * 512, c * 512:(c + 1) * 512].rearrange("(k p) n -> p k n", p=128),
                                         [128, 4, 512])
                for s in range(4):
                    for kk in range(4):
                        k = kq * 4 + kk
                        self.mm(self.pf[s][:, :], xT[:, k, s * 128:(s + 1) * 128], slot[:, kk, :], k == 0, k == 15,
                                reads=[sname, (tag, 'xT', k)], writes=[f'pf{s}'])
            for s in range(4):
                kf, kfn = self.smallf[:, s % 2, :], ('smallf', s % 2)
                self.cp(kf, self.pf[s][:, :], reads=[f'pf{s}'], writes=[kfn], eng='act')
                if c < 4:
                    P.dma('sp', self.outs['kv_rows'][t0 + s * 128:t0 + (s + 1) * 128, c * 512:(c + 1) * 512], kf, reads=[kfn])
                elif it == self.NT - 1:
                    P.dma('sp', self.outs['win'][s * 128:(s + 1) * 128, (c - 4) * 512:(c - 3) * 512], kf, reads=[kfn])
                kb, kbn = kvb[:, s % 2, :], ('U', 4 + s % 2)
                self.cp(kb, kf, reads=[kfn], writes=[kbn], eng='dve')
                if c in (3, 5):
                    dst = self.vs_d if c == 3 else self.vw_d
                    P.dma('sp', dst[t0 + s * 128:t0 + (s + 1) * 128, :], kb, reads=[kbn], writes=[('vs_d' if c == 3 else 'vw_d')])
                else:
                    pT, pTn = self.pb[s % 2], f'pb{s % 2}'
                    for g in range(4):
                        self.tr(pT[:, g * 128:(g + 1) * 128], kb[:, g * 128:(g + 1) * 128], self.ident_b[:, :],
                                reads=[kbn, 'const'], writes=[pTn])
                    src = pT[:, 0:512].rearrange("p (g t) -> p g t", t=128)
                    if c == 0:
                        self.cp(self.kcbuf[:, :, 16 + s * 128:16 + (s + 1) * 128], src, reads=[pTn], writes=['kcbuf'], eng='act')
                    elif c == 1:
                        self.cp(self.vcbuf[:, :, 16 + s * 128:16 + (s + 1) * 128], src, reads=[pTn], writes=['vcbuf'], eng='act')
                    else:
                        self.cp(kTst[:, :, s * 128:(s + 1) * 128], src, reads=[pTn], writes=kTn, eng='act')
            if c in (2, 4):
                dst = self.ksT_d if c == 2 else self.kwT_d
                P.dma('sp', dst[:, :, t0:t0 + W].rearrange("g d t -> d g t"), kTst, reads=kTn,
                      writes=[('ksT_d' if c == 2 else 'kwT_d')])

    def compress(self, it, grp=0):
        wc = self.ins['w_cmp']
        for kind in range(2):
            for half in range(2):
                slot, sname = self.wload(wc[kind, half * 16:(half + 1) * 16].rearrange("l d e -> d l e"), [128, 16, 128])
                for g in range(4):
                    for ll in range(16):
                        l = half * 16 + ll
                        if kind == 0:
                            self.mm(self.pf[g][:, 0:32], slot[:, ll, :], self.kcbuf[:, g, l:l + 16 * 31 + 1:16], l == 0, l == 31,
                                    reads=[sname, 'kcbuf'], writes=[f'pf{g}'])
                        else:
                            self.mm(self.pf[g][0:32, 0:128], self.vcbuf[:, g, l:l + 16 * 31 + 1:16], slot[:, ll, :], l == 0, l == 31,
                                    reads=[sname, 'vcbuf'], writes=[f'pf{g}'])
            for g in range(4):
                if kind == 0:
                    self.act(self.cmpKT[:, g, 32 * it:32 * it + 32], self.pf[g][:, 0:32], AF.Identity, bias=self.posK[:, 0:1],
                             reads=[f'pf{g}', 'posK'], writes=['cmpKT'])
                else:
                    self.tt(self.cmpV[it][:, g * 128:(g + 1) * 128], self.pf[g][0:32, 0:128], self.posV[:, :], ALU.add,
                            reads=[f'pf{g}', 'posV'], writes=[('cmpV', it)])

    def cmp_pos_terms(self):
        wc = self.ins['w_cmp']
        for kind in range(2):
            for half in range(2):
                slot, sname = self.wload(wc[kind, half * 16:(half + 1) * 16].rearrange("l d e -> d l e"), [128, 16, 128])
                for ll in range(16):
                    l = half * 16 + ll
                    if kind == 0:
                        self.mm(self.pf[0][:, 0:1], slot[:, ll, :], self.posT[:, 0, l:l + 1], l == 0, l == 31,
                                reads=[sname, 'const'], writes=['pf0'])
                    else:
                        self.mm(self.pf[1][0:32, 0:128], self.posrep[:, l, :], slot[:, ll, :], l == 0, l == 31,
                                reads=[sname, 'const'], writes=['pf1'])
        self.cp(self.posK[:, 0:1], self.pf[0][:, 0:1], reads=['pf0'], writes=['posK'], eng='act')
        self.cp(self.posV[:, :], self.pf[1][0:32, 0:128], reads=['pf1'], writes=['posV'], eng='act')

    def b_inproj(self, st, jj):
        ncols, tag = st['ncols'], st['tag']
        xT, qT, qmT = st['xT'], st['qT'], st['qmT']
        w_in = self.ins['w_in_b'][jj]
        xr = lambda k: xT[:, k, :ncols]
        xn = lambda k: (tag, 'xT', k)
        for h in range(12):
            ps, psn = self.pf[h % 4], f'pf{h % 4}'
            self.proj_fm(self.wcols(w_in, h * 128), NCH, xr, xn, ps, psn, ncols)
            self.cp(qT[:, h, :ncols], ps[:, :ncols], reads=[psn], writes=[(tag, 'qT', h)], eng='act')
        for h in range(4):
            ps, psn = self.pf[h % 4], f'pf{h % 4}'
            self.proj_fm(self.wcols(w_in, DA + 36 + h * 128), NCH, xr, xn, ps, psn, ncols)
            self.cp(qmT[:, h, :ncols], ps[:, :ncols], reads=[psn], writes=[(tag, 'qmT', h)], eng='act')
        slot, sname = self.wload(w_in[:, DA:DA + 36].rearrange("(k p) n -> p k n", p=128), [128, 16, 36])
        gs = st['gsig']
        for s, m in enumerate(st['subw']):
            ps, psn = self.pf[4], 'pf4'
            for k in range(NCH):
                self.mm(ps[:m, 0:36], xT[:, k, s * 128:s * 128 + m], slot[:, k, :], k == 0, k == NCH - 1,
                        reads=[sname, (tag, 'xT', k)], writes=[psn])
            self.act(gs[:m, s, :], ps[:m, 0:36], AF.Sigmoid, reads=[psn], writes=[(tag, 'gsig')])

    def attn_unit(self, u, q_lhsT, q_res, m, kT, k_res, vtok, v_res, nkeys, mask_fn, mask_res, den_ap, den_res, pO, pOn, first, last,
                  deferred=False, post=None):
        pS, pSn = self.pf[u % 2], f'pf{u % 2}'
        E, En = self.att_e[u % 2], ('att_e', u % 2)
        PT, PTn = self.att_pt[u % 2], ('att_pt', u % 2)
        pT, pTn = self.pb[u % 2], f'pb{u % 2}'
        mk = mask_fn()
        qr = q_res if isinstance(q_res, list) else [q_res]

        def front():
            self.mm(pS[:m, :nkeys], q_lhsT, kT, True, True, reads=qr + [k_res], writes=[pSn])
            self.act(E[:m, :nkeys], pS[:m, :nkeys], AF.Exp, scale=SCALE, reads=[pSn], writes=[En])
            Ev = E[:m, :nkeys] if mk.ndim == 2 else E[:m, :nkeys].rearrange("p (b k) -> p b k", k=64)
            self.stt(Ev, Ev, 1.0, mk, ALU.mult, ALU.mult, reads=[En] + mask_res, writes=[En, den_res], accum_out=den_ap)

        def back():
            nc_ = nkeys // 128
            for c in range(nc_):
                self.tr(pT[:, c * 128:c * 128 + m], E[:m, c * 128:(c + 1) * 128], self.ident_b[:m, :m], reads=[En, 'const'], writes=[pTn])
            self.cp(PT[:, 0:nc_, :m], pT[:, 0:nc_ * 128].rearrange("p (c q) -> p c q", q=128)[:, :, :m], reads=[pTn], writes=[PTn], eng='act')
            for c in range(nc_):
                self.mm(pO[:m, :128], PT[:, c, :m], vtok(c), first and c == 0, last and c == nc_ - 1, reads=[PTn, v_res], writes=[pOn])
            if post is not None:
                post()
        if deferred:
            return front, back
        front()
        back()

    @staticmethod
    def run_pipelined(units):
        pend = None
        for fr, bk in units:
            fr()
            if pend is not None:
                pend()
            pend = bk
        if pend is not None:
            pend()

    def nsa_prompt(self, st, it):
        P = self.P
        tag, qT, mixT, gs = st['tag'], st['qT'], st['mixT'], st['gsig']
        nk = W * (it + 1)
        ncmp = 32 * (it + 1)
        u = 0
        for g in range(4):
            P.dma('sp', self.ksg[:, 0:nk], self.ksT_d[g, :, 0:nk], reads=['ksT_d'], writes=['ksg'])
            P.dma('sp', self.vsg[:, 0:4 * (it + 1), :], self.vs_d[0:nk, g * 128:(g + 1) * 128].rearrange("(c p) d -> p c d", p=128),
                  reads=['vs_d'], writes=['vsg'])
            w0 = max(it - 1, 0) * W
            nw = nk - w0
            P.dma('sp', self.kwg[:, 0:nw], self.kwT_d[g, :, w0:nk], reads=['kwT_d'], writes=['kwg'])
            P.dma('sp', self.vwg[:, 0:nw // 128, :], self.vw_d[w0:nk, g * 128:(g + 1) * 128].rearrange("(c p) d -> p c d", p=128),
                  reads=['vw_d'], writes=['vwg'])
            for s in range(4):
                si = it * 4 + s
                qs = slice(s * 128, (s + 1) * 128)
                acc = self.nsa_acc
                for r in range(3):
                    h = 3 * g + r
                    pS, pSn = self.pf[u % 2], f'pf{u % 2}'
                    pT, pTn = self.pb[u % 2], f'pb{u % 2}'
                    pO, pOn = self.pf[2 + u % 2], f'pf{2 + u % 2}'
                    u += 1
                    Ec, Pc = self.cmp_e, self.cmp_p
                    self.mm(pS[:, :ncmp], qT[:, h, qs], self.cmpKT[:, g, 0:ncmp], True, True,
                            reads=[(tag, 'qT', h), 'cmpKT'], writes=[pSn])
                    self.act(Ec[:, :ncmp], pS[:, :ncmp], AF.Exp, scale=SCALE, reads=[pSn], writes=['cmp_e'])
                    self.stt(Ec[:, :ncmp], Ec[:, :ncmp], 1.0, self.cmask[:, si, 0:ncmp], ALU.mult, ALU.mult,
                             reads=['cmp_e', 'const'], writes=['cmp_e', 'cmp_den'], accum_out=self.cmp_den[:, 0:1])
                    self.ts(self.cmp_den[:, 1:2], self.cmp_den[:, 0:1], 1e-30, None, ALU.max, None, reads=['cmp_den'], writes=['cmp_den'])
                    self.recip(self.cmp_den[:, 1:2], self.cmp_den[:, 1:2], reads=['cmp_den'], writes=['cmp_den'])
                    self.ts(Pc[:, :ncmp], Ec[:, :ncmp], self.cmp_den[:, 1:2], None, ALU.mult, None, reads=['cmp_e', 'cmp_den'], writes=['cmp_p'])
                    for i2 in range(it + 1):
                        self.tr(pT[0:32, i2 * 128:(i2 + 1) * 128], Pc[:, 32 * i2:32 * i2 + 32], self.ident_b[:, :],
                                reads=['cmp_p', 'const'], writes=[pTn])
                    PnT = self.cmp_pt
                    self.cp(PnT[:, 0:it + 1, :], pT[0:32, 0:(it + 1) * 128].rearrange("p (c q) -> p c q", q=128), reads=[pTn], writes=['cmp_pt'], eng='act')
                    for i2 in range(it + 1):
                        self.mm(pO[:, :128], PnT[:, i2, :], self.cmpV[i2][:, g * 128:(g + 1) * 128], i2 == 0, i2 == it,
                                reads=['cmp_pt', ('cmpV', i2)], writes=[pOn])
                    for i2 in range(it + 1):
                        self.mm(self.pf[4][:, 0:32], PnT[:, i2, :], self.ovl[:, i2, :], r == 0 and i2 == 0, r == 2 and i2 == it,
                                reads=['cmp_pt', 'const'], writes=['pf4'])
                    self.ts(acc[:, r, :], pO[:, :128], gs[:, s, h:h + 1], None, ALU.mult, None, reads=[pOn, (tag, 'gsig')], writes=[('nsa_acc', r)])
                sc, sc2, m8 = self.tk_sc, self.tk_sc2, self.tk_m8
                self.tt(sc[:, :], self.pf[4][:, 0:32], self.tk_mul[:, si, :], ALU.mult, reads=['pf4', 'const'], writes=['tk_sc'])
                self.tt(sc[:, :], sc[:, :], self.tk_add[:, si, :], ALU.add, reads=['tk_sc', 'const'], writes=['tk_sc'])
                P.op('dve', lambda e: e.max(m8[:, 0:8], sc[:, :]), reads=['tk_sc'], writes=['tk_m8'])
                P.op('dve', lambda e: e.match_replace(sc2[:, :], m8[:, 0:8], sc[:, :], NEG), reads=['tk_sc', 'tk_m8'], writes=['tk_sc2'])
                P.op('dve', lambda e: e.max(m8[:, 8:16], sc2[:, :]), reads=['tk_sc2'], writes=['tk_m8'])
                self.ts(sc2[:, :], sc[:, :], m8[:, 15:16], None, ALU.is_ge, None, reads=['tk_sc', 'tk_m8'], writes=['tk_sc2'])
                self.tt(self.selmask[:, :], sc2[:, :], self.tk_elig[:, si, :], ALU.mult, reads=['tk_sc2', 'const'], writes=['selmask'])
                self.tt(self.mdiag[:, :].rearrange("p (b k) -> p b k", k=64), self.tri[:, s, :].rearrange("p (b k) -> p b k", k=64),
                        self.selmask[:, 8 * it:8 * it + 8].unsqueeze(2).to_broadcast([128, 8, 64]), ALU.mult,
                        reads=['selmask', 'const'], writes=['mdiag'])
                units = []
                for r in range(3):
                    h = 3 * g + r
                    pO, pOn = self.pf[2 + r % 2], f'pf{2 + r % 2}'
                    dn = self.sel_den[r]
                    dnn = lambda k, r=r: ('sel_den', r, k)

                    def post_sel(r=r, h=h, pO=pO, pOn=pOn, dn=dn, dnn=dnn):
                        self.cp(dn[:, 4:5], dn[:, 0:1], reads=[dnn(0)], writes=[dnn(4)], eng='dve')
                        for k2 in range(1, it + 1):
                            self.tt(dn[:, 4:5], dn[:, 4:5], dn[:, k2:k2 + 1], ALU.add, reads=[dnn(4), dnn(k2)], writes=[dnn(4)])
                        self.ts(dn[:, 4:5], dn[:, 4:5], 1e-30, None, ALU.max, None, reads=[dnn(4)], writes=[dnn(4)])
                        self.recip(dn[:, 5:6], dn[:, 4:5], reads=[dnn(4)], writes=[dnn(5)])
                        self.tt(dn[:, 5:6], dn[:, 5:6], gs[:, s, 12 + h:13 + h], ALU.mult, reads=[dnn(5), (tag, 'gsig')], writes=[dnn(5)])
                        self.stt(acc[:, r, :], pO[:, :128], dn[:, 5:6], acc[:, r, :], ALU.mult, ALU.add,
                                 reads=[pOn, dnn(5), ('nsa_acc', r)], writes=[('nsa_acc', r)])
                    for kg in range(it + 1):
                        if kg == it:
                            mfn = lambda: self.mdiag[:, :]
                            mres = ['mdiag']
                        else:
                            mfn = lambda kg=kg: self.selmask[:, 8 * kg:8 * kg + 8].unsqueeze(2).to_broadcast([128, 8, 64])
                            mres = ['selmask']
                        units.append(self.attn_unit(u, qT[:, h, qs], (tag, 'qT', h), 128, self.ksg[:, kg * W:(kg + 1) * W], 'ksg',
                                                    lambda c, kg=kg: self.vsg[:, kg * 4 + c, :], 'vsg', W, mfn, mres,
                                                    dn[:, kg:kg + 1], dnn(kg), pO, pOn, kg == 0, kg == it,
                                                    deferred=True, post=post_sel if kg == it else None))
                        u += 1
                ngrp = nw // W
                for r in range(3):
                    h = 3 * g + r
                    pO, pOn = self.pf[2 + (r + 1) % 2], f'pf{2 + (r + 1) % 2}'
                    dn = self.win_den[r]
                    dnn = lambda k, r=r: ('win_den', r, k)

                    def post_win(r=r, h=h, pO=pO, pOn=pOn, dn=dn, dnn=dnn):
                        if ngrp > 1:
                            self.tt(dn[:, 4:5], dn[:, 0:1], dn[:, 1:2], ALU.add, reads=[dnn(0), dnn(1)], writes=[dnn(4)])
                        else:
                            self.cp(dn[:, 4:5], dn[:, 0:1], reads=[dnn(0)], writes=[dnn(4)], eng='dve')
                        self.recip(dn[:, 5:6], dn[:, 4:5], reads=[dnn(4)], writes=[dnn(5)])
                        self.tt(dn[:, 5:6], dn[:, 5:6], gs[:, s, 24 + h:25 + h], ALU.mult, reads=[dnn(5), (tag, 'gsig')], writes=[dnn(5)])
                        mo, mon = self.att_o[r % 2], ('att_o', r % 2)
                        self.stt(mo[:, :], pO[:, :128], dn[:, 5:6], acc[:, r, :], ALU.mult, ALU.add,
                                 reads=[pOn, dnn(5), ('nsa_acc', r)], writes=[mon])
                        pT, pTn = self.pb[r % 2], f'pb{r % 2}'
                        self.tr(pT[:, 512:640], mo[:, :], self.ident_b[:, :], reads=[mon, 'const'], writes=[pTn])
                        self.cp(mixT[:, h, qs], pT[:, 512:640], reads=[pTn], writes=[(tag, 'mixT', h)], eng='dve')
                    for kg in range(ngrp):
                        if kg == ngrp - 1:
                            mfn = lambda: self.tri[:, s, :]
                        else:
                            mfn = lambda: self.wprev[:, s, :]
                        units.append(self.attn_unit(u, qT[:, h, qs], (tag, 'qT', h), 128, self.kwg[:, kg * W:(kg + 1) * W], 'kwg',
                                                    lambda c, kg=kg: self.vwg[:, kg * 4 + c, :], 'vwg', W, mfn, ['const'],
                                                    dn[:, kg:kg + 1], dnn(kg), pO, pOn, kg == 0, kg == ngrp - 1,
                                                    deferred=True, post=post_win if kg == ngrp - 1 else None))
                        u += 1
                self.run_pipelined(units)

    def sample_decl(self):
        self.din('xs', [128, 16])
        self.din('cache_kv', [1280 * 128, 2048])
        self.din('page_tab', [1, 128], I32)
        self.din('riota', [128, 1])
        self.din('cache_win', [512, 1024])
        self.din('cache_mem', [4, NMEM, 1024])
        self.din('st_mix', [128, 2, 12, 2])
        self.din('st_ffn', [128, 4, NF, 2])
        self.din('cmask_s', [3, 1024], BF16)
        self.din('tk_s', [3, 2, 264])
        self.din('wmask_s', [3, 512], BF16)
        self.din('ovl_s', [128, 8, 264], BF16)
        self.dout('ys', [128, 16])
        self.dout('kv_rows_s', [1, 2048])
        self.dout('win_s', [512, 1024])
        self.dout('conv_mix_s', [2, 128, 12, 2])
        self.dout('conv_ffn_s', [4, 128, NF, 2])
        self.ksT_s_d = self.dscr('ksT_s_d', [4, 128, 16384], BF16)
        self.vs_s_d = self.dscr('vs_s_d', [16384, 512], BF16)
        self.kwT_s_d = self.dscr('kwT_s_d', [4, 128, 512], BF16)
        self.vw_s_d = self.dscr('vw_s_d', [512, 512], BF16)
        self.cmpV_s_d = self.dscr('cmpV_s_d', [1024, 512], BF16)

    def sample_mem_setup(self):
        P = self.P
        cm = self.ins['cache_mem']
        mb = self.U[:, 0:2 * 1024].rearrange("p (c n) -> p c n", n=1024)
        mbn = [('U', i) for i in range(4)]
        kt = self.smallb
        for l in range(4):
            P.dma('pool', mb, cm[l].rearrange("(c p) n -> p c n", p=128), writes=mbn)
            P.dma('sp', self.memV_d[1, l].rearrange("(c p) n -> p c n", p=128), mb[:, :, 512:1024], reads=mbn, writes=[('memV_d', 1, l)])
            for h in range(4):
                pT, pTn = self.pb[h % 2], f'pb{h % 2}'
                for c in range(2):
                    self.tr(pT[:, c * 128:(c + 1) * 128], mb[:, c, h * 128:(h + 1) * 128], self.ident_b[:, :], reads=mbn + ['const'], writes=[pTn])
                self.cp(kt[:, h, 0:256], pT[:, 0:256], reads=[pTn], writes=[('smallb', h)], eng='act')
            P.dma('sp', self.memKT_d[1, l], kt[:, :, 0:256], reads=[('smallb', h) for h in range(4)], writes=[('memKT_d', 1, l)])

    def sample_gather_pass(self):
        P = self.P
        U = self.U
        ckv = self.ins['cache_kv']
        ptb = self.s_ptb
        P.dma('sp', ptb[:, :], self.ins['page_tab'].partition_broadcast(128), writes=['s_ptb'])
        P.dma('sp', self.s_riota[:, :], self.ins['riota'], writes=['s_riota'])
        self.ts(self.s_idx[:, :], ptb[:, :], 128.0, self.s_riota[:, 0:1], ALU.mult, ALU.add, reads=['s_ptb', 's_riota'], writes=['s_idx'])
        gat = [U[:, i * 4096:(i + 1) * 4096].bitcast(F32) for i in range(2)]
        gatn = [[('U', i * 8 + k) for k in range(8)] for i in range(2)]
        gb = [U[:, (16 + 4 * i) * 512:(20 + 4 * i) * 512] for i in range(2)]
        gbn = [[('U', 16 + 4 * i + k) for k in range(4)] for i in range(2)]
        kTst = U[:, 24 * 512:28 * 512].rearrange("p (g t) -> p g t", t=512)
        kTn = [('U', 24 + i) for i in range(4)]
        wres = U[:, 28 * 512:44 * 512].rearrange("p (k l e) -> p k l e", k=2, l=32)
        wresn = [('U', 28 + i) for i in range(16)]
        wc = self.ins['w_cmp']
        for kind in range(2):
            for half in range(2):
                P.dma('pool', wres[:, kind, half * 16:(half + 1) * 16, :], wc[kind, half * 16:(half + 1) * 16].rearrange("l d e -> d l e"),
                      writes=wresn)
        for i in range(32):
            for nm in ('kcbuf', 'vcbuf'):
                buf = getattr(self, nm)
                if i == 0:
                    P.op('dve', lambda e, buf=buf: e.memset(buf[:, :, 0:16], 0.0), writes=[nm])
                else:
                    self.cp(self.cmptmp[:, :, :], buf[:, :, 512:528], reads=[nm], writes=['cmptmp'], eng='dve')
                    self.cp(buf[:, :, 0:16], self.cmptmp[:, :, :], reads=['cmptmp'], writes=[nm], eng='dve')
            for pgl in range(4):
                pg = 4 * i + pgl
                G, Gn = gat[pg % 2], gatn[pg % 2]
                B, Bn = gb[pg % 2], gbn[pg % 2]
                P.idma(G[:, :], ckv, self.s_idx[:, pg:pg + 1], reads=['s_idx'], writes=Gn)
                self.cp(B[:, 0:1024], G[:, 0:1024], reads=Gn, writes=Bn, eng='dve')
                self.cp(B[:, 1024:2048], G[:, 1024:2048], reads=Gn, writes=Bn, eng='act')
                P.dma('sp', self.vs_s_d[pg * 128:(pg + 1) * 128, :], B[:, 1536:2048], reads=Bn, writes=['vs_s_d'])
                for kind in range(3):
                    pT, pTn = self.pb[kind % 2], f'pb{kind % 2}'
                    for g in range(4):
                        self.tr(pT[:, g * 128:(g + 1) * 128], B[:, kind * 512 + g * 128:kind * 512 + (g + 1) * 128], self.ident_b[:, :],
                                reads=Bn + ['const'], writes=[pTn])
                    src = pT[:, 0:512].rearrange("p (g t) -> p g t", t=128)
                    if kind == 0:
                        self.cp(self.kcbuf[:, :, 16 + pgl * 128:16 + (pgl + 1) * 128], src, reads=[pTn], writes=['kcbuf'], eng='act')
                    elif kind == 1:
                        self.cp(self.vcbuf[:, :, 16 + pgl * 128:16 + (pgl + 1) * 128], src, reads=[pTn], writes=['vcbuf'], eng='dve')
                    else:
                        self.cp(kTst[:, :, pgl * 128:(pgl + 1) * 128], src, reads=[pTn], writes=kTn, eng='act')
            P.dma('sp', self.ksT_s_d[:, :, i * 512:(i + 1) * 512].rearrange("g d t -> d g t"), kTst, reads=kTn, writes=['ksT_s_d'])
            for kind in range(2):
                for g in range(4):
                    for l in range(32):
                        if kind == 0:
                            self.mm(self.pf[g][:, 0:32], wres[:, 0, l, :], self.kcbuf[:, g, l:l + 16 * 31 + 1:16], l == 0, l == 31,
                                    reads=wresn + ['kcbuf'], writes=[f'pf{g}'])
                        else:
                            self.mm(self.pf[g][0:32, 0:128], self.vcbuf[:, g, l:l + 16 * 31 + 1:16], wres[:, 1, l, :], l == 0, l == 31,
                                    reads=wresn + ['vcbuf'], writes=[f'pf{g}'])
                for g in range(4):
                    if kind == 0:
                        self.act(self.cmpKT_s[:, g, 32 * i:32 * i + 32], self.pf[g][:, 0:32], AF.Identity, bias=self.posK[:, 0:1],
                                 reads=[f'pf{g}', 'posK'], writes=['cmpKT_s'])
                    else:
                        self.tt(self.cvst[i % 2][:, g * 128:(g + 1) * 128], self.pf[g][0:32, 0:128], self.posV[:, :], ALU.add,
                                reads=[f'pf{g}', 'posV'], writes=[('cvst', i % 2)])
            P.dma('sp', self.cmpV_s_d[32 * i:32 * i + 32, :], self.cvst[i % 2][:, :], reads=[('cvst', i % 2)], writes=['cmpV_s_d'])

    def sample_win_setup(self):
        P = self.P
        cw = self.ins['cache_win']
        wb = self.U[:, 0:4 * 1024].rearrange("p (c n) -> p c n", n=1024)
        wbn = [('U', i) for i in range(8)]
        kTst = self.U[:, 8 * 512:12 * 512].rearrange("p (g t) -> p g t", t=512)
        kTn = [('U', 8 + i) for i in range(4)]
        P.dma('pool', wb, cw.rearrange("(c p) n -> p c n", p=128), writes=wbn)
        P.dma('sp', self.vw_s_d.rearrange("(c p) n -> p c n", p=128), wb[:, :, 512:1024], reads=wbn, writes=['vw_s_d'])
        for c in range(4):
            pT, pTn = self.pb[c % 2], f'pb{c % 2}'
            for g in range(4):
                self.tr(pT[:, g * 128:(g + 1) * 128], wb[:, c, g * 128:(g + 1) * 128], self.ident_b[:, :], reads=wbn + ['const'], writes=[pTn])
            self.cp(kTst[:, :, c * 128:(c + 1) * 128], pT[:, 0:512].rearrange("p (g t) -> p g t", t=128), reads=[pTn], writes=kTn, eng='act')
        P.dma('sp', self.kwT_s_d.rearrange("g d t -> d g t"), kTst, reads=kTn, writes=['kwT_s_d'])
        P.dma('sp', self.outs['win_s'][0:511, :], cw[1:512, :])

    def sample_kv(self, st):
        P = self.P
        xT, tag = st['xT'], st['tag']
        w = self.ins['w_kv']
        kvf, kvb = self.s_kvf, self.s_kvb
        for c in range(6):
            ps, psn = self.pf[c % 2], f'pf{c % 2}'
            for kq in range(4):
                slot, sname = self.wload(w[kq * 512:(kq + 1) * 512, c * 512:(c + 1) * 512].rearrange("(k p) n -> p k n", p=128), [128, 4, 512])
                for kk in range(4):
                    k = kq * 4 + kk
                    self.mm(ps[0:1, :], xT[:, k, 0:1], slot[:, kk, :], k == 0, k == 15, reads=[sname, (tag, 'xT', k)], writes=[psn])
            self.cp(kvf[0:1, c * 512:(c + 1) * 512], ps[0:1, :], reads=[psn], writes=['s_kvf'], eng='act')
        self.cp(kvb[0:1, :], kvf[0:1, :], reads=['s_kvf'], writes=['s_kvb'], eng='dve')
        P.dma('sp', self.outs['kv_rows_s'], kvf[0:1, 0:2048], reads=['s_kvf'])
        P.dma('sp', self.outs['win_s'][511:512, :], kvf[0:1, 2048:3072], reads=['s_kvf'])
        pT, pTn = self.pb[0], 'pb0'
        for i, c0 in enumerate([1024 + g * 128 for g in range(4)] + [2048 + g * 128 for g in range(4)]):
            self.tr(pT[:, 2 * i:2 * i + 1], kvb[0:1, c0:c0 + 128], self.ident_b[0:1, 0:1], reads=['s_kvb', 'const'], writes=[pTn])
        self.cp(self.s_knT[:, 0:8], pT[:, 0:16:2], reads=[pTn], writes=['s_knT'], eng='act')

    def new_key_unit(self, u, q3, q_res, knT_col, vrow, mask_ap, mask_res, den_ap, den_res, pO, pOn, first, last):
        pS, pSn = self.pf[u % 2], f'pf{u % 2}'
        pT, pTn = self.pb[u % 2], f'pb{u % 2}'
        e1 = self.s_e1
        self.mm(pS[:3, 0:1], q3, knT_col, True, True, reads=[q_res, 's_knT'], writes=[pSn])
        self.act(e1[:3, 0:1], pS[:3, 0:1], AF.Exp, scale=SCALE, reads=[pSn], writes=['s_e1'])
        if mask_ap is not None:
            self.tt(e1[:3, 0:1], e1[:3, 0:1], mask_ap, ALU.mult, reads=['s_e1'] + mask_res, writes=['s_e1'])
        self.cp(den_ap, e1[:3, 0:1], reads=['s_e1'], writes=[den_res], eng='dve')
        self.cp(e1[:3, 1:2], e1[:3, 0:1], reads=['s_e1'], writes=['s_e1b'], eng='dve')
        self.cp(self.s_e1b[:3, 0:1], e1[:3, 0:1], reads=['s_e1'], writes=['s_e1b'], eng='dve')
        self.tr(pT[0:1, 0:3], self.s_e1b[:3, 0:1], self.ident_b[:3, :3], reads=['s_e1b', 'const'], writes=[pTn])
        self.cp(self.s_pt1[0:1, 0:3], pT[0:1, 0:3], reads=[pTn], writes=['s_pt1'], eng='act')
        self.mm(pO[:3, :128], self.s_pt1[0:1, 0:3], vrow, first, last, reads=['s_pt1', 's_kvb'], writes=[pOn])

    def nsa_sample(self, st, jj):
        P = self.P
        tag, qT, mixT, xT = st['tag'], st['qT'], st['mixT'], st['xT']
        w_in = self.ins['w_in_b'][jj]
        cmpV = self.U[:, 0:8 * 512].rearrange("p (c n) -> p c n", n=512)
        cvn = [('U', i) for i in range(8)]
        P.dma('sp', cmpV, self.cmpV_s_d.rearrange("(c p) n -> p c n", p=128), reads=['cmpV_s_d'], writes=cvn)
        gslot, gname = self.wload(w_in[:, DA:DA + 36].rearrange("(k p) n -> p k n", p=128), [128, 16, 36])
        u = 0
        for g in range(4):
            q3 = qT[:, 3 * g:3 * g + 3, 0:1].rearrange("p h o -> p (h o)")
            qres = (tag, 'qT', 3 * g)
            qresl = [(tag, 'qT', 3 * g + r) for r in range(3)]
            pg_, pgn = self.pf[4], 'pf4'
            for b in range(3):
                for k in range(NCH):
                    self.mm(pg_[0:3, b:b + 1], gslot[:, k, 12 * b + 3 * g:12 * b + 3 * g + 3], xT[:, k, 0:1], k == 0, k == NCH - 1,
                            reads=[gname, (tag, 'xT', k)], writes=[pgn])
            gs = self.s_gs
            self.act(gs[0:3, 0:3], pg_[0:3, 0:3], AF.Sigmoid, reads=[pgn], writes=['s_gs'])
            Ec, Pc = self.s_ec, self.s_pc
            for kg in range(2):
                pS, pSn = self.pf[u % 2], f'pf{u % 2}'
                u += 1
                self.mm(pS[:3, :], q3, self.cmpKT_s[:, g, kg * 512:(kg + 1) * 512], True, True, reads=qresl + ['cmpKT_s'], writes=[pSn])
                self.act(Ec[:3, kg * 512:(kg + 1) * 512], pS[:3, :], AF.Exp, scale=SCALE, reads=[pSn], writes=['s_ec'])
            dn = self.s_den
            self.stt(Ec[:3, :], Ec[:3, :], 1.0, self.cmask_s[:3, :], ALU.mult, ALU.mult, reads=['s_ec', 'sconst'], writes=['s_ec', 's_den'],
                     accum_out=dn[:3, 0:1])
            self.recip(dn[:3, 1:2], dn[:3, 0:1], reads=['s_den'], writes=['s_den'])
            self.ts(Pc[:3, :], Ec[:3, :], dn[:3, 1:2], None, ALU.mult, None, reads=['s_ec', 's_den'], writes=['s_pc'])
            pT, pTn = self.pb[0], 'pb0'
            for c in range(8):
                self.tr(pT[:, c * 4:c * 4 + 3], Pc[:3, c * 128:(c + 1) * 128], self.ident_b[:3, :3], reads=['s_pc', 'const'], writes=[pTn])
            PT8 = self.s_pt8
            self.cp(PT8[:, :, 0:3], pT[:, 0:32].rearrange("p (c q) -> p c q", q=4)[:, :, 0:3], reads=[pTn], writes=['s_pt8'], eng='act')
            pO, pOn = self.pf[2], 'pf2'
            for c in range(8):
                self.mm(pO[:3, :128], PT8[:, c, 0:3], cmpV[:, c, g * 128:(g + 1) * 128], c == 0, c == 7, reads=['s_pt8'] + cvn, writes=[pOn])
            acc = self.s_acc
            self.ts(acc[:3, :], pO[:3, :128], gs[0:3, 0:1], None, ALU.mult, None, reads=[pOn, 's_gs'], writes=['s_acc'])
            psum3 = self.s_psum
            self.tt(psum3[:, :], PT8[:, :, 0], PT8[:, :, 1], ALU.add, reads=['s_pt8'], writes=['s_psum'])
            self.tt(psum3[:, :], psum3[:, :], PT8[:, :, 2], ALU.add, reads=['s_pt8', 's_psum'], writes=['s_psum'])
            prep = self.s_prep
            self.cp(prep[:, :, :], psum3[:, :].unsqueeze(2).to_broadcast([128, 8, 3]), reads=['s_psum'], writes=['s_prep'], eng='dve')
            pI, pIn = self.pf[3], 'pf3'
            for c in range(8):
                self.mm(pI[:3, 0:264], prep[:, c, :], self.ovl_s[:, c, :], c == 0, c == 7, reads=['s_prep', 'sconst'], writes=[pIn])
            sc, sc2, m8 = self.s_sc, self.s_sc2, self.s_m8
            self.tt(sc[:3, :], pI[:3, 0:264], self.tk_s[:3, 0, :], ALU.mult, reads=[pIn, 'sconst'], writes=['s_sc'])
            self.tt(sc[:3, :], sc[:3, :], self.tk_s[:3, 1, :], ALU.add, reads=['s_sc', 'sconst'], writes=['s_sc'])
            P.op('dve', lambda e: e.max(m8[:3, 0:8], sc[:3, :]), reads=['s_sc'], writes=['s_m8'])
            P.op('dve', lambda e: e.match_replace(sc2[:3, :], m8[:3, 0:8], sc[:3, :], NEG), reads=['s_sc', 's_m8'], writes=['s_sc2'])
            P.op('dve', lambda e: e.max(m8[:3, 8:16], sc2[:3, :]), reads=['s_sc2'], writes=['s_m8'])
            selm = self.s_selm
            self.ts(selm[:3, :], sc[:3, :], m8[:3, 15:16], None, ALU.is_ge, None, reads=['s_sc', 's_m8'], writes=['s_selm'])
            pO, pOn = self.pf[2], 'pf2'
            dsel = self.s_dsel
            for kk in range(8):
                P.dma('sp', self.ksg[:, :], self.ksT_s_d[g, :, kk * 2048:(kk + 1) * 2048], reads=['ksT_s_d'], writes=['ksg'])
                P.dma('sp', self.vsg[:, :, :], self.vs_s_d[kk * 2048:(kk + 1) * 2048, g * 128:(g + 1) * 128].rearrange("(c p) d -> p c d", p=128),
                      reads=['vs_s_d'], writes=['vsg'])
                units = []
                for k4 in range(4):
                    kg = kk * 4 + k4
                    mfn = lambda kg=kg: selm[:3, 8 * kg:8 * kg + 8].unsqueeze(2).to_broadcast([3, 8, 64])
                    units.append(self.attn_unit(u, q3, qresl, 3, self.ksg[:, k4 * W:(k4 + 1) * W], 'ksg',
                                                lambda c, k4=k4: self.vsg[:, k4 * 4 + c, :], 'vsg', W, mfn, ['s_selm'],
                                                dsel[:3, kg:kg + 1], ('s_dsel', kg), pO, pOn, kg == 0, False, deferred=True))
                    u += 1
                self.run_pipelined(units)
            self.new_key_unit(u, q3, qres, self.s_knT[:, g:g + 1], self.s_kvb[0:1, 1536 + g * 128:1536 + (g + 1) * 128],
                              selm[:3, 256:257], ['s_selm'], dsel[:3, 32:33], ('s_dsel', 32), pO, pOn, False, True)
            u += 1
            self.act(self.s_sc2[:3, 0:33], dsel[:3, 0:33], AF.Copy, reads=[('s_dsel', k) for k in range(33)] + ['s_sc2'], writes=['s_sc2', 's_dsel2'], accum_out=dsel[:3, 36:37])
            self.recip(dsel[:3, 37:38], dsel[:3, 36:37], reads=['s_dsel2'], writes=['s_dsel2'])
            self.tt(dsel[:3, 37:38], dsel[:3, 37:38], gs[0:3, 1:2], ALU.mult, reads=['s_dsel2', 's_gs'], writes=['s_dsel2'])
            self.stt(acc[:3, :], pO[:3, :128], dsel[:3, 37:38], acc[:3, :], ALU.mult, ALU.add, reads=[pOn, 's_dsel2', 's_acc'], writes=['s_acc'])
            pO, pOn = self.pf[3], 'pf3'
            dw = self.s_dwin
            P.dma('sp', self.kwg[:, 0:512], self.kwT_s_d[g, :, :], reads=['kwT_s_d'], writes=['kwg'])
            P.dma('sp', self.vwg[:, 0:4, :], self.vw_s_d[:, g * 128:(g + 1) * 128].rearrange("(c p) d -> p c d", p=128), reads=['vw_s_d'], writes=['vwg'])
            self.attn_unit(u, q3, qres, 3, self.kwg[:, 0:W], 'kwg', lambda c: self.vwg[:, c, :], 'vwg', W,
                           lambda: self.wmask_s[:3, :], ['sconst'], dw[:3, 0:1], 's_dwin', pO, pOn, True, False)
            u += 1
            self.new_key_unit(u, q3, qres, self.s_knT[:, 4 + g:5 + g], self.s_kvb[0:1, 2560 + g * 128:2560 + (g + 1) * 128],
                              None, [], dw[:3, 1:2], 's_dwin', pO, pOn, False, True)
            u += 1
            self.tt(dw[:3, 2:3], dw[:3, 0:1], dw[:3, 1:2], ALU.add, reads=['s_dwin'], writes=['s_dwin2'])
            self.recip(dw[:3, 3:4], dw[:3, 2:3], reads=['s_dwin2'], writes=['s_dwin2'])
            self.tt(dw[:3, 3:4], dw[:3, 3:4], gs[0:3, 2:3], ALU.mult, reads=['s_dwin2', 's_gs'], writes=['s_dwin2'])
            mo = self.s_mo
            self.stt(mo[:3, :], pO[:3, :128], dw[:3, 3:4], acc[:3, :], ALU.mult, ALU.add, reads=[pOn, 's_dwin2', 's_acc'], writes=['s_mo'])
            pT, pTn = self.pb[1], 'pb1'
            self.tr(pT[:, 0:3], mo[:3, :], self.ident_b[:3, :3], reads=['s_mo', 'const'], writes=[pTn])
            self.cp(mixT[:, 3 * g:3 * g + 3, 0:1].rearrange("p h o -> p (h o)"), pT[:, 0:3], reads=[pTn],
                    writes=[(tag, 'mixT', 3 * g + r) for r in range(3)], eng='dve')

    def sample_path(self):
        P = self.P
        XR = self.st_p['xres']
        XT = self.st_p['xT']
        xrn = lambda a, b: [('p', 'xres', j) for j in range(a, b)]
        xtn = lambda a, b: [('p', 'xT', j) for j in range(a, b)]
        self.s_ptb = self.sb("s_ptb", [128, 128], I32)
        self.s_riota = self.sb("s_riota", [128, 1], F32)
        self.s_idx = self.sb("s_idx", [128, 128], I32)
        self.cmpKT_s = XT[:, 0:8, :].rearrange("p (g a) t -> p g (a t)", g=4)
        P.define('cmpKT_s', xtn(0, 8))
        self.s_kvb = XT[0:1, 8:14, :].rearrange("p c t -> p (c t)")
        P.define('s_kvb', xtn(8, 14))
        self.s_pc = XT[0:3, 14:16, :].rearrange("p c t -> p (c t)")
        P.define('s_pc', xtn(14, 16))
        self.s_kvf = XR[0:1, 0:6, :].rearrange("p c t -> p (c t)")
        P.define('s_kvf', xrn(0, 6))
        self.s_ec = XR[0:3, 6:8, :].rearrange("p c t -> p (c t)")
        P.define('s_ec', xrn(6, 8))
        self.ovl_s = XR[:, 8:11, :].rearrange("p c t -> p (c t)").bitcast(BF16)[:, 0:8 * 264].rearrange("p (c j) -> p c j", j=264)
        self.tk_s = XR[0:3, 11:13, :].rearrange("p c t -> p (c t)")[:, 0:528].rearrange("p (a j) -> p a j", j=264)
        self.cmask_s = XR[0:3, 13, :].bitcast(BF16)
        self.wmask_s = XR[0:3, 14, 0:256].bitcast(BF16)
        P.define('sconst', xrn(8, 15))
        self.cvst = [self.sb(f"cvst{i}", [32, 512], BF16) for i in range(2)]
        self.s_knT = self.sb("s_knT", [128, 8], BF16)
        self.s_e1 = self.sb("s_e1", [3, 2], F32)
        self.s_e1b = self.sb("s_e1b", [3, 2], BF16)
        self.s_pt1 = self.sb("s_pt1", [1, 4], BF16)
        self.s_gs = self.sb("s_gs", [3, 4], F32)
        self.s_den = self.sb("s_den", [3, 4], F32)
        self.s_pt8 = self.sb("s_pt8", [128, 8, 4], BF16)
        self.s_acc = self.sb("s_acc", [3, 128], F32)
        self.s_psum = self.sb("s_psum", [128, 8], F32)
        self.s_prep = self.sb("s_prep", [128, 8, 3], BF16)
        self.s_sc = self.sb("s_sc", [3, 264], F32)
        self.s_sc2 = self.sb("s_sc2", [3, 264], F32)
        self.s_m8 = self.sb("s_m8", [3, 16], F32)
        self.s_selm = self.sb("s_selm", [3, 264], BF16)
        self.s_dsel = self.sb("s_dsel", [3, 40], F32)
        self.s_dwin = self.sb("s_dwin", [3, 4], F32)
        self.s_mo = self.sb("s_mo", [3, 128], BF16)
        for nm in ('cmask_s', 'tk_s', 'wmask_s', 'ovl_s'):
            P.dma('sp', getattr(self, nm), self.ins[nm], writes=['sconst'])
        xres = self.sb("xres_s", [128, 16, 1], F32)
        xT = self.sb("xT_s", [128, 16, 1], BF16)
        mixT = self.sb("mixT_s", [128, 16, 1], BF16)
        hT = self.sb("hT_s", [128, NF, 1], BF16)
        qmT = self.sb("qmT_s", [128, 4, 1], BF16)
        qT = self.sb("qT_s", [128, 12, 1], BF16)
        gsig = self.sb("gsig_s", [1, 1, 36], F32)
        carry_mix = self.sb("carry_mix_s", [128, 2, 12, 2], F32)
        carry_ffn = self.sb("carry_ffn_s", [128, 4, NF, 2], F32)
        P.dma('sp', carry_mix[:], self.ins['st_mix'], writes=[('s', 'carry_mix', 0), ('s', 'carry_mix', 1)])
        P.dma('sp', carry_ffn[:], self.ins['st_ffn'], writes=[('s', 'carry_ffn', l) for l in range(4)])
        st = dict(xres=xres, xT=xT, mixT=mixT, hT=hT, qmT=qmT, qT=qT, ncols=1, tag='s', subw=[1], gsig=gsig,
                  memKT=self.st_p['memKT'], memV=self.st_p['memV'], memres='p_mem',
                  carry_mix=[carry_mix[:, l, :, :] for l in range(2)],
                  carry_ffn=[carry_ffn[:, l, :, :] for l in range(4)])
        P.dma('sp', xres[:, :, 0:1].rearrange("p c o -> p (c o)"), self.ins['xs'], writes=[('s', 'xres', j) for j in range(NCH)])
        self.cp(xT[:, :, :], xres[:, :, :], reads=[('s', 'xres', j) for j in range(NCH)], writes=[('s', 'xT', j) for j in range(NCH)], eng='dve')
        self.sample_mem_setup()
        self.sample_win_setup()
        self.sample_gather_pass()
        for l in range(2):
            self.load_mem(st, 1, l)
            self.a_mixer(st, l)
            self.mem_attention(st)
            self.layer_tail(st, l)
        self.sample_kv(st)
        for jj in range(2):
            l = 2 + jj
            self.load_mem(st, 1, l)
            self.b_inproj(st, jj)
            self.nsa_sample(st, jj)
            self.mem_attention(st)
            self.layer_tail(st, l)
        P.dma('sp', self.outs['ys'], xres[:, :, 0:1].rearrange("p c o -> p (c o)"), reads=[('s', 'xres', j) for j in range(NCH)])
        for l in range(2):
            P.dma('sp', self.outs['conv_mix_s'][l], carry_mix[:, l, :, :], reads=[('s', 'carry_mix', l)])
        for l in range(4):
            P.dma('sp', self.outs['conv_ffn_s'][l], carry_ffn[:, l, :, :], reads=[('s', 'carry_ffn', l)])


def _bf16(a):
    return np.asarray(a, dtype=np.float32).astype(ml_dtypes.bfloat16)


def make_consts():
    c = {}
    c['ident_f'] = np.eye(128, dtype=np.float32)
    c['ident_b'] = _bf16(np.eye(128, dtype=np.float32))
    c['ones_b'] = _bf16(np.full((128, 128), 1.0 / D, dtype=np.float32))
    return c


def make_inmap(inp, b, c, NT, do_sample=False):
    T = NT * W
    m = {}
    m['x'] = np.ascontiguousarray(inp['x_prompt'][b, :T])
    m['memp'] = np.ascontiguousarray(inp['mem_prompt'][b])
    for k in ('w_in_a', 'w_in_b', 'w_o', 'w_mem_kv', 'w_up', 'w_down', 'w_cmp'):
        m[k] = np.ascontiguousarray(inp[k])
    m['w_kv'] = np.ascontiguousarray(inp['w_kv_shared'])
    m['lng'] = np.ascontiguousarray(inp['ln_g'].reshape(4, 2, 16, 128).transpose(3, 0, 1, 2))
    m['lnb'] = np.ascontiguousarray(inp['ln_b'].reshape(4, 2, 16, 128).transpose(3, 0, 1, 2))
    m['mixcw'] = np.ascontiguousarray(inp['conv_a_w'].reshape(2, 3, 12, 128).transpose(3, 0, 2, 1))
    m['ffncw'] = np.ascontiguousarray(inp['ffn_conv_w'].reshape(4, 3, NF, 128).transpose(3, 0, 2, 1))
    m.update(make_consts())
    m.update(make_masks(NT))
    m.update(make_pos(inp))
    if do_sample:
        m.update(make_sample(inp, c))
    return m


def make_masks(NT):
    c = {}
    T = NT * W
    p = np.arange(128)
    nsl = T // 64
    cm = np.zeros((128, NT * 4, 128), np.float32)
    mul = np.zeros((128, NT * 4, 32), np.float32)
    add = np.zeros((128, NT * 4, 32), np.float32)
    elig = np.zeros((128, NT * 4, 32), np.float32)
    for it in range(NT):
        for s in range(4):
            qpos = it * W + s * 128 + p
            npr = np.arange(128)
            n = npr - 1
            valid = (npr[None, :] >= 1) & (16 * n[None, :] + 31 <= qpos[:, None]) & (npr[None, :] < 32 * (it + 1))
            cm[:, it * 4 + s, :] = valid
            blk = np.arange(32)
            cur = qpos // 64
            el = (blk[None, :] * 64 <= qpos[:, None]) & (blk[None, :] < nsl)
            forced = ((blk[None, :] == 0) | (blk[None, :] == cur[:, None]) | (blk[None, :] == cur[:, None] - 1)) & (blk[None, :] < nsl)
            mul[:, it * 4 + s, :] = (el & ~forced)
            add[:, it * 4 + s, :] = np.where(forced, 1e9, np.where(el, 0.0, -1e9))
            elig[:, it * 4 + s, :] = el
    c['cmask'] = _bf16(cm)
    c['tk_mul'] = _bf16(mul)
    c['tk_add'] = _bf16(add)
    c['tk_elig'] = _bf16(elig)
    col = np.arange(512)
    tri = np.zeros((128, 4, 512), np.float32)
    for s in range(4):
        tri[:, s, :] = col[None, :] <= (128 * s + p)[:, None]
    c['tri'] = _bf16(tri)
    c['wprev'] = _bf16(1.0 - tri)
    ovl = np.zeros((32, 4, 32), np.float32)
    for i2 in range(4):
        for m in range(32):
            n = 32 * i2 + m - 1
            if n < 0:
                continue
            for j in range(32):
                if (16 * n <= 64 * j + 63) and (16 * n + 31 >= 64 * j):
                    ovl[m, i2, j] = 1.0
    c['ovl'] = _bf16(ovl)
    return c


def make_pos(inp):
    c = {}
    cp = np.asarray(inp['cmp_pos'], np.float32)
    c['posT'] = _bf16(cp.transpose(2, 0, 1))
    c['posrep'] = _bf16(np.repeat(cp[1].T[:, :, None], 32, axis=2))
    return c


def make_sample(inp, c):
    m = {}
    m['xs'] = np.ascontiguousarray(inp['x_sample'][c, 0].reshape(16, 128).T)
    m['cache_kv'] = inp['cache_kv'].reshape(1280 * 128, 2048)
    m['page_tab'] = np.ascontiguousarray(inp['page_table'][c:c + 1]).astype(np.int32)
    m['riota'] = np.arange(128, dtype=np.float32).reshape(128, 1)
    m['cache_win'] = np.ascontiguousarray(inp['cache_win'][c].reshape(512, 1024))
    m['cache_mem'] = np.ascontiguousarray(inp['cache_mem'][:, c].reshape(4, NMEM, 1024))
    m['st_mix'] = np.ascontiguousarray(inp['state_conv_mix'][:, c].reshape(2, 2, 12, 128).transpose(3, 0, 2, 1))
    m['st_ffn'] = np.ascontiguousarray(inp['state_conv_ffn'][:, c].reshape(4, 2, NF, 128).transpose(3, 0, 2, 1))
    cm = np.ones((3, 1024), np.float32)
    cm[:, 0] = 0
    m['cmask_s'] = _bf16(cm)
    tk = np.zeros((3, 2, 264), np.float32)
    j = np.arange(264)
    forced = (j == 0) | (j == 255) | (j == 256)
    real = j <= 256
    tk[:, 0, :] = (real & ~forced)
    tk[:, 1, :] = np.where(forced, 1e9, np.where(real, 0.0, -1e9))
    m['tk_s'] = tk
    wm = np.ones((3, 512), np.float32)
    wm[:, 0] = 0
    m['wmask_s'] = _bf16(wm)
    ov = np.zeros((128, 8, 264), np.float32)
    npr = (np.arange(8)[None, :] * 128 + np.arange(128)[:, None])
    n = npr - 1
    jj = np.arange(264)
    o = (16 * n[:, :, None] <= 64 * jj[None, None, :] + 63) & (16 * n[:, :, None] + 31 >= 64 * jj[None, None, :]) \
        & (n[:, :, None] >= 0) & (jj[None, None, :] <= 256)
    m['ovl_s'] = _bf16(o.astype(np.float32))
    return m


_NC_CACHE = {}


def kernel(**inputs):
    inp = {k: np.asarray(v) for k, v in inputs.items()}
    NT = 4
    if 'nc' not in _NC_CACHE:
        bld = Builder(NT=NT, n_layers=4, upto='all', do_sample=True)
        _NC_CACHE['nc'] = bld.build()
    nc = _NC_CACHE['nc']
    in_maps = [make_inmap(inp, c // 2, c, NT, do_sample=True) for c in range(8)]
    res = run_bass_kernel_spmd(nc, in_maps, core_ids=list(range(8)))
    R = res.results
    f32 = np.float32

    def fm(a, n):
        return np.asarray(a).transpose(2, 1, 0).reshape(2, n * 128)
    y_prompt = np.stack([R[2 * b]['y'] for b in range(4)]).astype(f32)
    y_sample = np.stack([R[c]['ys'].T.reshape(1, D) for c in range(8)]).astype(f32)
    kv_rows_prompt = np.stack([R[2 * b]['kv_rows'].reshape(NT * W, 4, 4, 128) for b in range(4)]).astype(f32)
    win_prompt = np.stack([R[2 * b]['win'].reshape(512, 2, 4, 128) for b in range(4)]).astype(f32)
    mem_kv = np.stack([R[2 * b]['mem_kv'].reshape(4, NMEM, 2, 4, 128) for b in range(4)], axis=1).astype(f32)
    conv_mix_p = np.stack([np.stack([fm(R[2 * b]['conv_mix'][l], 12) for b in range(4)]) for l in range(2)]).astype(f32)
    conv_ffn_p = np.stack([np.stack([fm(R[2 * b]['conv_ffn'][l], NF) for b in range(4)]) for l in range(4)]).astype(f32)
    kv_rows_s = np.stack([R[c]['kv_rows_s'].reshape(1, 4, 4, 128) for c in range(8)]).astype(f32)
    win_s = np.stack([R[c]['win_s'].reshape(512, 2, 4, 128) for c in range(8)]).astype(f32)
    conv_mix_s = np.stack([np.stack([fm(R[c]['conv_mix_s'][l], 12) for c in range(8)]) for l in range(2)]).astype(f32)
    conv_ffn_s = np.stack([np.stack([fm(R[c]['conv_ffn_s'][l], NF) for c in range(8)]) for l in range(4)]).astype(f32)
    return (y_prompt, y_sample, kv_rows_prompt, win_prompt, mem_kv, conv_mix_p, conv_ffn_p,
            kv_rows_s, win_s, conv_mix_s, conv_ffn_s)
```

```python
import numpy as np
import ml_dtypes
from contextlib import ExitStack
import concourse.bass as bass
import concourse.mybir as mybir
from concourse.bass_utils import run_bass_kernel_spmd

F32 = mybir.dt.float32
BF16 = mybir.dt.bfloat16
I32 = mybir.dt.int32
AF = mybir.ActivationFunctionType
ALU = mybir.AluOpType
AX = mybir.AxisListType

D = 2048
NCH = 16
DA = 1536
DFF = 5504
NF = 43
NMEM = 256
W = 512
ALPHA = 8 ** 0.25
EPS = 1e-5
SCALE = 128 ** -0.5
NEG = -3.0e38

GEN = 1000000000
NDMA = 24
NGEN = 1
NSLOT = 6


class Prog:
    ENG = ['pe', 'act', 'dve', 'pool', 'sp']

    def __init__(self, nc, stack):
        self.nc = nc
        self.q = {e: [] for e in self.ENG}
        self.cnt = {e: 0 for e in self.ENG}
        self.known = {e: {} for e in self.ENG}
        self.res = {}
        self.sems = {}
        for e in ('pe', 'act', 'dve'):
            for g in range(NGEN):
                self.sems[(e, g)] = stack.enter_context(nc.semaphore(f"s_{e}_{g}"))
        self.sems[('pool', 0)] = stack.enter_context(nc.semaphore("s_pool_0"))
        for i in range(NDMA):
            self.sems[('dma', i)] = stack.enter_context(nc.semaphore(f"s_dma_{i}"))
        self.dma_tot = [0] * NDMA
        self.dma_rr = 0
        self.alias = {}

    def define(self, name, cells):
        self.alias[name] = list(cells)

    def _expand(self, names):
        out = []
        for n in names:
            a = self.alias.get(n)
            if a is None:
                out.append(n)
            else:
                out.extend(a)
        return out

    def _collect(self, eng, reads, writes):
        reads = self._expand(reads)
        writes = self._expand(writes)
        need = {}

        def add(dd):
            for k, v in dd.items():
                if need.get(k, 0) < v:
                    need[k] = v
        for r in reads:
            ent = self.res.get(r)
            if ent:
                add(ent[0])
        for r in writes:
            ent = self.res.get(r)
            if ent:
                add(ent[0])
                add(ent[1])
        waits = []
        kn = self.known[eng]
        for k, v in need.items():
            if eng == 'pe' and k[0] == 'pe':
                continue
            if kn.get(k, 0) < v:
                waits.append((k, v))
                kn[k] = v
        return waits

    def _update(self, key, val, reads, writes):
        reads = self._expand(reads)
        writes = self._expand(writes)
        for r in reads:
            ent = self.res.get(r)
            if ent is None:
                ent = self.res[r] = ({}, {})
            if ent[1].get(key, 0) < val:
                ent[1][key] = val
        for r in writes:
            self.res[r] = ({key: val}, {})

    @staticmethod
    def _excl(reads, writes):
        ps = [r for r in reads if isinstance(r, str) and (r.startswith('pf') or r.startswith('pb'))]
        if ps:
            reads = [r for r in reads if r not in ps]
            writes = list(writes) + ps
        return reads, writes

    def op(self, eng, fn, reads=(), writes=()):
        reads, writes = self._excl(reads, writes)
        waits = self._collect(eng, reads, writes)
        self.cnt[eng] += 1
        g, v = divmod(self.cnt[eng] - 1, GEN)
        key = (eng, g)
        sems = self.sems

        def emit(e):
            for k, val in waits:
                e.wait_ge(sems[k], val)
            fn(e).then_inc(sems[key], 1)
        self.q[eng].append(emit)
        self._update(key, v + 1, reads, writes)

    def dma(self, eng, out, in_, reads=(), writes=(), **kw):
        s = self.dma_rr
        self.dma_rr = (s + 1) % NDMA
        key = ('dma', s)
        waits = self._collect(eng, reads, writes)
        prev = self.dma_tot[s]
        if prev and self.known[eng].get(key, 0) < prev:
            waits.append((key, prev))
            self.known[eng][key] = prev
        self.dma_tot[s] = prev + 16
        sems = self.sems

        def emit(e):
            for k, val in waits:
                e.wait_ge(sems[k], val)
            e.dma_start(out=out, in_=in_, **kw).then_inc(sems[key], 16)
        self.q[eng].append(emit)
        self._update(key, prev + 16, reads, writes)

    def idma(self, out, in_, idx_ap, reads=(), writes=()):
        eng = 'pool'
        s = self.dma_rr
        self.dma_rr = (s + 1) % NDMA
        key = ('dma', s)
        waits = self._collect(eng, reads, writes)
        prev = self.dma_tot[s]
        if prev and self.known[eng].get(key, 0) < prev:
            waits.append((key, prev))
            self.known[eng][key] = prev
        self.dma_tot[s] = prev + 16
        sems = self.sems

        def emit(e):
            for k, val in waits:
                e.wait_ge(sems[k], val)
            e.indirect_dma_start(out=out, out_offset=None, in_=in_,
                                 in_offset=bass.IndirectOffsetOnAxis(ap=idx_ap, axis=0)).then_inc(sems[key], 16)
        self.q[eng].append(emit)
        self._update(key, prev + 16, reads, writes)

    def finish(self):
        waits = []
        for i in range(NDMA):
            if self.dma_tot[i]:
                waits.append((('dma', i), self.dma_tot[i]))
        for e in self.ENG:
            if e == 'sp' or self.cnt[e] == 0:
                continue
            g, v = divmod(self.cnt[e] - 1, GEN)
            waits.append(((e, g), v + 1))
        sems = self.sems

        def emit(e):
            for k, val in waits:
                e.wait_ge(sems[k], val)
        self.q['sp'].append(emit)

    def run_block(self):
        nc = self.nc
        q = self.q
        with nc.Block() as block:
            @block.sync
            def _(e):
                for f in q['sp']:
                    f(e)

            @block.tensor
            def _(e):
                for f in q['pe']:
                    f(e)

            @block.scalar
            def _(e):
                for f in q['act']:
                    f(e)

            @block.vector
            def _(e):
                for f in q['dve']:
                    f(e)

            @block.gpsimd
            def _(e):
                for f in q['pool']:
                    f(e)


class Builder:
    def __init__(self, NT=4, do_sample=True, n_layers=4, upto='all'):
        self.NT = NT
        self.T = NT * W
        self.do_sample = do_sample
        self.n_layers = n_layers
        self.upto = upto
        self.nc = bass.Bass("TRN2", target_bir_lowering=False)
        self.stack = ExitStack()
        self.ins = {}
        self.outs = {}
        self.scr = {}

    def din(self, name, shape, dt=F32):
        t = self.nc.dram_tensor(name, list(shape), dt, kind="ExternalInput").ap()
        self.ins[name] = t
        return t

    def dout(self, name, shape, dt=F32):
        t = self.nc.dram_tensor(name, list(shape), dt, kind="ExternalOutput").ap()
        self.outs[name] = t
        return t

    def dscr(self, name, shape, dt):
        t = self.nc.dram_tensor(name, list(shape), dt, kind="ExternalOutput").ap()
        self.scr[name] = t
        return t

    def sb(self, name, shape, dt):
        return self.stack.enter_context(self.nc.sbuf_tensor("sb_" + name, list(shape), dt))

    def ps(self, name, shape, dt):
        return self.stack.enter_context(self.nc.psum_tensor(name, list(shape), dt))

    def wload(self, src, shape):
        i = self.wslot
        self.wslot = (i + 1) % NSLOT
        n = 1
        for s in shape[1:]:
            n *= s
        assert n <= 2048
        flat = self.wring[0:shape[0], i, 0:n]
        if len(shape) == 3:
            view = flat.rearrange("p (a b) -> p a b", b=shape[2])
        else:
            view = flat
        name = ('ws', i)
        self.P.dma('pool', view, src, writes=[name])
        return view, name

    def mm(self, out, lhsT, rhs, start, stop, reads, writes):
        self.P.op('pe', lambda e: e.matmul(out, lhsT=lhsT, rhs=rhs, start=start, stop=stop), reads=reads, writes=writes)

    def tr(self, out, in_, ident, reads, writes):
        self.P.op('pe', lambda e: e.transpose(out, in_, ident), reads=reads, writes=writes)

    def act(self, out, in_, func, reads, writes, **kw):
        self.P.op('act', lambda e: e.activation(out=out, in_=in_, func=func, **kw), reads=reads, writes=writes)

    def tt(self, out, in0, in1, op, reads, writes, eng='dve'):
        self.P.op(eng, lambda e: e.tensor_tensor(out=out, in0=in0, in1=in1, op=op), reads=reads, writes=writes)

    def stt(self, out, in0, scalar, in1, op0, op1, reads, writes, eng='dve', **kw):
        self.P.op(eng, lambda e: e.scalar_tensor_tensor(out=out, in0=in0, scalar=scalar, in1=in1, op0=op0, op1=op1, **kw),
                  reads=reads, writes=writes)

    def ts(self, out, in0, s1, s2, op0, op1, reads, writes, eng='dve', **kw):
        if op1 is None:
            self.P.op(eng, lambda e: e.tensor_scalar(out=out, in0=in0, scalar1=s1, scalar2=None, op0=op0, **kw),
                      reads=reads, writes=writes)
        else:
            self.P.op(eng, lambda e: e.tensor_scalar(out=out, in0=in0, scalar1=s1, scalar2=s2, op0=op0, op1=op1, **kw),
                      reads=reads, writes=writes)

    def cp(self, out, in_, reads, writes, eng='dve'):
        if eng == 'act':
            self.act(out, in_, AF.Copy, reads, writes)
        else:
            self.P.op(eng, lambda e: e.tensor_copy(out, in_), reads=reads, writes=writes)

    def recip(self, out, in_, reads, writes):
        self.P.op('dve', lambda e: e.reciprocal(out, in_), reads=reads, writes=writes)

    def proj_fm(self, wsrc, nk_total, rhs_fn, rhs_res, ps, psname, ncols):
        k0 = 0
        while k0 < nk_total:
            nk = min(16, nk_total - k0)
            slot, sname = self.wload(wsrc(k0, nk), [128, nk, 128])
            for kk in range(nk):
                k = k0 + kk
                self.mm(ps[:, :ncols], slot[:, kk, :], rhs_fn(k), k == 0, k == nk_total - 1,
                        reads=[sname, rhs_res(k)], writes=[psname])
            k0 += nk

    @staticmethod
    def wcols(w3, c0):
        return lambda k0, nk: w3[k0 * 128:(k0 + nk) * 128, c0:c0 + 128].rearrange("(k p) n -> p k n", p=128)

    def layernorm(self, st, l, which):
        ncols, tag, xres, xT = st['ncols'], st['tag'], st['xres'], st['xT']
        pm, pq = self.pf[4], self.pf[5]
        for j in range(NCH):
            xb, xbn = self.lnb16[j % 2], ('lnb16', j % 2)
            sq, sqn = self.lnb16[2 + j % 2], ('lnb16', 2 + j % 2)
            self.cp(xb[:, :ncols], xres[:, j, :ncols], reads=[(tag, 'xres', j)], writes=[xbn], eng='dve')
            self.act(sq[:, :ncols], xres[:, j, :ncols], AF.Square, reads=[(tag, 'xres', j)], writes=[sqn])
            self.mm(pm[:, :ncols], self.ones_b[:, :], xb[:, :ncols], j == 0, j == NCH - 1,
                    reads=[xbn, 'const'], writes=['pf4'])
            self.mm(pq[:, :ncols], self.ones_b[:, :], sq[:, :ncols], j == 0, j == NCH - 1,
                    reads=[sqn, 'const'], writes=['pf5'])
        mean, rstd, tmp = self.lnmean, self.lnrstd, self.lntmp
        self.cp(mean[:, :ncols], pm[:, :ncols], reads=['pf4'], writes=['lnmean'], eng='act')
        self.tt(tmp[:, :ncols], mean[:, :ncols], mean[:, :ncols], ALU.mult, reads=['lnmean'], writes=['lntmp'])
        self.tt(tmp[:, :ncols], pq[:, :ncols], tmp[:, :ncols], ALU.subtract, reads=['pf5', 'lntmp'], writes=['lntmp'])
        self.ts(tmp[:, :ncols], tmp[:, :ncols], EPS, None, ALU.add, None, reads=['lntmp'], writes=['lntmp'])
        self.act(rstd[:, :ncols], tmp[:, :ncols], AF.Sqrt, reads=['lntmp'], writes=['lnrstd'])
        self.recip(rstd[:, :ncols], rstd[:, :ncols], reads=['lnrstd'], writes=['lnrstd'])
        g = self.lng[:, l, which, :]
        b = self.lnb[:, l, which, :]
        for j in range(NCH):
            t2 = self.zf[j % 2]
            self.tt(t2[:, :ncols], xres[:, j, :ncols], mean[:, :ncols], ALU.subtract,
                    reads=[(tag, 'xres', j), 'lnmean'], writes=[('zf', j % 2)])
            self.stt(t2[:, :ncols], t2[:, :ncols], g[:, j:j + 1], rstd[:, :ncols], ALU.mult, ALU.mult,
                     reads=[('zf', j % 2), 'lnrstd', 'const'], writes=[('zf', j % 2)])
            self.act(xres[:, j, :ncols], t2[:, :ncols], AF.Identity, bias=b[:, j:j + 1],
                     reads=[('zf', j % 2), 'const'], writes=[(tag, 'xres', j)])
            self.act(xT[:, j, :ncols], t2[:, :ncols], AF.Identity, bias=b[:, j:j + 1],
                     reads=[('zf', j % 2), 'const'], writes=[(tag, 'xT', j)])

    def mem_attention(self, st):
        tag, qmT, mixT = st['tag'], st['qmT'], st['mixT']
        memKT, memV, kres = st['memKT'], st['memV'], st['memres']
        u = 0
        units = []
        for h in range(4):
            for s, m in enumerate(st['subw']):
                units.append(self._mem_unit(u, h, s, m, tag, qmT, mixT, memKT, memV, kres))
                u += 1
        self.run_pipelined(units)

    def _mem_unit(self, u, h, s, m, tag, qmT, mixT, memKT, memV, kres):
        pS, pSn = self.pf[u % 2], f'pf{u % 2}'
        E, En = self.att_e[u % 2], ('att_e', u % 2)
        PT, PTn = self.att_pt[u % 2], ('att_pt', u % 2)
        pT, pTn = self.pb[u % 2], f'pb{u % 2}'
        pO, pOn = self.pf[2 + u % 2], f'pf{2 + u % 2}'
        den, denn = self.att_den[:, u % 2, :], ('att_den', u % 2)
        mo, mon = self.att_o[u % 2], ('att_o', u % 2)

        def front():
            self.mm(pS[:m, :256], qmT[:, h, s * 128:s * 128 + m], memKT[:, h, :], True, True,
                    reads=[(tag, 'qmT', h), kres], writes=[pSn])
            self.act(E[:m, :256], pS[:m, :256], AF.Exp, scale=SCALE, accum_out=den[:m, 0:1],
                     reads=[pSn], writes=[En, denn])

        def back():
            for c in range(2):
                self.tr(pT[:, c * 128:c * 128 + m], E[:m, c * 128:(c + 1) * 128], self.ident_b[:m, :m],
                        reads=[En, 'const'], writes=[pTn])
            self.cp(PT[:, 0:2, :m], pT[:, 0:256].rearrange("p (c q) -> p c q", q=128)[:, :, :m],
                    reads=[pTn], writes=[PTn], eng='act')
            for c in range(2):
                self.mm(pO[:m, :128], PT[:, c, :m], memV[:, c, h * 128:(h + 1) * 128], c == 0, c == 1,
                        reads=[PTn, kres], writes=[pOn])
            self.recip(den[:m, 1:2], den[:m, 0:1], reads=[denn], writes=[denn])
            self.ts(mo[:m, :], pO[:m, :128], den[:m, 1:2], None, ALU.mult, None, reads=[pOn, denn], writes=[mon])
            self.tr(pT[:, 512:512 + m], mo[:m, :], self.ident_b[:m, :m], reads=[mon, 'const'], writes=[pTn])
            self.cp(mixT[:, 12 + h, s * 128:s * 128 + m], pT[:, 512:512 + m], reads=[pTn], writes=[(tag, 'mixT', 12 + h)],
                    eng='dve')
        return front, back

    def layer_tail(self, st, l):
        ncols, tag = st['ncols'], st['tag']
        xres, xT, mixT, hT = st['xres'], st['xT'], st['mixT'], st['hT']
        w_o, w_up, w_down = self.ins['w_o'], self.ins['w_up'], self.ins['w_down']
        for j in range(NCH):
            ps, psn = self.pf[j % 4], f'pf{j % 4}'
            self.proj_fm(self.wcols(w_o[l], j * 128), NCH, lambda k: mixT[:, k, :ncols], lambda k: (tag, 'mixT', k), ps, psn, ncols)
            self.stt(xres[:, j, :ncols], xres[:, j, :ncols], ALPHA, ps[:, :ncols], ALU.mult, ALU.add,
                     reads=[psn, (tag, 'xres', j)], writes=[(tag, 'xres', j)])
        if self.upto == 'wo':
            return
        self.layernorm(st, l, 0)
        if self.upto == 'ln1':
            return
        cw = self.ffncw[:, l, :, :]
        carry = st['carry_ffn'][l]
        cres = (tag, 'carry_ffn', l)
        for f in range(NF):
            b0 = (2 * f) % 4
            pz, pzn = self.pf[b0], f'pf{b0}'
            pg, pgn = self.pf[b0 + 1], f'pf{b0 + 1}'
            self.proj_fm(self.wcols(w_up[l], f * 128), NCH, lambda k: xT[:, k, :ncols], lambda k: (tag, 'xT', k), pz, pzn, ncols)
            self.proj_fm(self.wcols(w_up[l], DFF + f * 128), NCH, lambda k: xT[:, k, :ncols], lambda k: (tag, 'xT', k), pg, pgn, ncols)
            zf, zfn = self.zf[f % 2], ('zf', f % 2)
            t1, t1n = self.ct[f % 2], ('ct', f % 2)
            self.cp(zf[:, 0:2], carry[:, f, :], reads=[cres], writes=[zfn], eng='dve')
            self.cp(zf[:, 2:2 + ncols], pz[:, :ncols], reads=[pzn], writes=[zfn], eng='act')
            self.cp(carry[:, f, :], zf[:, ncols:ncols + 2], reads=[zfn], writes=[cres], eng='dve')
            self.act(t1[:, :ncols], zf[:, 0:ncols], AF.Copy, scale=cw[:, f, 0:1], reads=[zfn, 'const'], writes=[t1n])
            self.stt(t1[:, :ncols], zf[:, 1:1 + ncols], cw[:, f, 1:2], t1[:, :ncols], ALU.mult, ALU.add,
                     reads=[zfn, t1n, 'const'], writes=[t1n])
            self.stt(t1[:, :ncols], zf[:, 2:2 + ncols], cw[:, f, 2:3], t1[:, :ncols], ALU.mult, ALU.add,
                     reads=[zfn, t1n, 'const'], writes=[t1n])
            self.act(t1[:, :ncols], t1[:, :ncols], AF.Silu, reads=[t1n], writes=[t1n])
            self.tt(hT[:, f, :ncols], t1[:, :ncols], pg[:, :ncols], ALU.mult, reads=[t1n, pgn], writes=[(tag, 'hT', f)])
        if self.upto == 'ffn_up':
            return
        for j in range(NCH):
            ps, psn = self.pf[j % 4], f'pf{j % 4}'
            self.proj_fm(self.wcols(w_down[l], j * 128), NF, lambda k: hT[:, k, :ncols], lambda k: (tag, 'hT', k), ps, psn, ncols)
            self.stt(xres[:, j, :ncols], xres[:, j, :ncols], ALPHA, ps[:, :ncols], ALU.mult, ALU.add,
                     reads=[psn, (tag, 'xres', j)], writes=[(tag, 'xres', j)])
        self.layernorm(st, l, 1)

    def a_mixer(self, st, l):
        ncols, tag = st['ncols'], st['tag']
        xT, mixT, qmT = st['xT'], st['mixT'], st['qmT']
        w_in = self.ins['w_in_a'][l]
        cw = self.mixcw[:, l, :, :]
        carry = st['carry_mix'][l]
        cres = (tag, 'carry_mix', l)
        xr = lambda k: xT[:, k, :ncols]
        xn = lambda k: (tag, 'xT', k)
        for j in range(12):
            base = 3 * (j % 2)
            pu, pb_, pc = self.pf[base], self.pf[base + 1], self.pf[base + 2]
            pun, pbn, pcn = f'pf{base}', f'pf{base + 1}', f'pf{base + 2}'
            for (ps, psn, off) in ((pu, pun, 0), (pb_, pbn, DA), (pc, pcn, 2 * DA)):
                self.proj_fm(self.wcols(w_in, off + j * 128), NCH, xr, xn, ps, psn, ncols)
            us, usn = self.ct[j % 2], ('ct', j % 2)
            cuf, cufn = self.zf[j % 2], ('zf', j % 2)
            self.cp(us[:, :ncols], pu[:, :ncols], reads=[pun], writes=[usn], eng='act')
            self.cp(cuf[:, 0:2], carry[:, j, :], reads=[cres], writes=[cufn], eng='dve')
            self.tt(cuf[:, 2:2 + ncols], pc[:, :ncols], us[:, :ncols], ALU.mult, reads=[pcn, usn], writes=[cufn])
            self.cp(carry[:, j, :], cuf[:, ncols:ncols + 2], reads=[cufn], writes=[cres], eng='dve')
            self.act(us[:, :ncols], cuf[:, 0:ncols], AF.Copy, scale=cw[:, j, 0:1], reads=[cufn, 'const'], writes=[usn])
            self.stt(us[:, :ncols], cuf[:, 1:1 + ncols], cw[:, j, 1:2], us[:, :ncols], ALU.mult, ALU.add,
                     reads=[cufn, usn, 'const'], writes=[usn])
            self.stt(us[:, :ncols], cuf[:, 2:2 + ncols], cw[:, j, 2:3], us[:, :ncols], ALU.mult, ALU.add,
                     reads=[cufn, usn, 'const'], writes=[usn])
            self.tt(mixT[:, j, :ncols], us[:, :ncols], pb_[:, :ncols], ALU.mult, reads=[usn, pbn], writes=[(tag, 'mixT', j)])
        for h in range(4):
            ps, psn = self.pf[h % 2], f'pf{h % 2}'
            self.proj_fm(self.wcols(w_in, 3 * DA + h * 128), NCH, xr, xn, ps, psn, ncols)
            self.cp(qmT[:, h, :ncols], ps[:, :ncols], reads=[psn], writes=[(tag, 'qmT', h)], eng='act')

    def mem_project_all(self):
        w = self.ins['w_mem_kv']
        mpT = self.U[:, 0:16 * NMEM].rearrange("p (c t) -> p c t", t=NMEM)
        P = self.P
        mpn = [('U', i) for i in range(8)]
        stg = [self.U[:, 8192 + i * 4096: 8192 + (i + 1) * 4096].bitcast(F32) for i in range(2)]
        stgn = [[('U', 16 + i * 8 + k) for k in range(8)] for i in range(2)]
        for s in range(2):
            self.load_transpose(self.ins['memp'][s * 128:(s + 1) * 128, :], 128, stg[s], stgn[s], None, mpT, s * 128,
                                None, lambda jq: mpn)
        ob = self.smallf
        import os
        mps = int(os.environ.get('MP_STAGE', '9'))
        for l in range(4 if mps >= 2 else 0):
            for c in range(2):
                pss = [(self.pf[0], 'pf0'), (self.pf[1], 'pf1')]
                for kq in range(4):
                    slot, sname = self.wload(w[l, kq * 512:(kq + 1) * 512, c * 512:(c + 1) * 512].rearrange("(k p) n -> p k n", p=128),
                                             [128, 4, 512])
                    for s in range(2):
                        for kk in range(4):
                            k = kq * 4 + kk
                            self.mm(pss[s][0][:, :], mpT[:, k, s * 128:(s + 1) * 128], slot[:, kk, :], k == 0, k == 15,
                                    reads=[sname] + mpn, writes=[pss[s][1]])
                for s in range(2):
                    self.cp(ob[:, s, :], pss[s][0][:, :], reads=[pss[s][1]], writes=[('smallf', s)], eng='act')
                    P.dma('sp', self.outs['mem_kv'][l, s * 128:(s + 1) * 128, c * 512:(c + 1) * 512], ob[:, s, :],
                          reads=[('smallf', s)])
                    if c == 1 and mps >= 3:
                        self.cp(self.smallb[:, s, :], pss[s][0][:, :], reads=[pss[s][1]], writes=[('smallb', s)], eng='dve')
                        P.dma('sp', self.memV_d[0, l, s * 128:(s + 1) * 128, :], self.smallb[:, s, :], reads=[('smallb', s)],
                              writes=[('memV_d', 0, l)])
            for h in range(4 if mps >= 4 else 0):
                ps, psn = self.pf[2 + h % 2], f'pf{2 + h % 2}'
                self.proj_fm(self.wcols(w[l], h * 128), NCH, lambda k: mpT[:, k, :], lambda k: mpn[k // 2], ps, psn, 256)
                self.cp(self.smallb[:, 2 + h % 2, 0:256], ps[:, :256], reads=[psn], writes=[('smallb', 2 + h % 2)], eng='act')
                P.dma('sp', self.memKT_d[0, l, :, h, :], self.smallb[:, 2 + h % 2, 0:256], reads=[('smallb', 2 + h % 2)],
                      writes=[('memKT_d', 0, l)])

    def load_mem(self, st, grp, l):
        P = self.P
        P.dma('sp', st['memKT'][:], self.memKT_d[grp, l], reads=[('memKT_d', grp, l)], writes=[st['memres']])
        P.dma('sp', st['memV'][:], self.memV_d[grp, l].rearrange("(c p) n -> p c n", p=128), reads=[('memV_d', grp, l)],
              writes=[st['memres']])

    def load_transpose(self, src_rows, m, stg, stgn, dstf, dstb, col0, resf, resb):
        self.P.dma('sp', stg[:m, :], src_rows, writes=stgn)
        for jq in range(4):
            pt, ptn = self.pf[jq % 2], f'pf{jq % 2}'
            for i in range(4):
                j = jq * 4 + i
                self.tr(pt[:, i * 128:i * 128 + m], stg[:m, j * 128:(j + 1) * 128], self.ident_f[:m, :m],
                        reads=stgn + ['const'], writes=[ptn])
            src = pt[:, :].rearrange("p (c q) -> p c q", q=128)[:, :, :m]
            if dstf is not None:
                self.cp(dstf[:, jq * 4:jq * 4 + 4, col0:col0 + m], src, reads=[ptn], writes=resf(jq), eng='act')
            if dstb is not None:
                self.cp(dstb[:, jq * 4:jq * 4 + 4, col0:col0 + m], src, reads=[ptn], writes=resb(jq), eng='dve')

    def build(self):
        nc = self.nc
        NT, T = self.NT, self.T
        self.din('x', [T, D])
        self.din('memp', [NMEM, D])
        self.din('w_in_a', [2, D, 3 * DA + 512])
        self.din('w_in_b', [2, D, DA + 36 + 512])
        self.din('w_o', [4, D, D])
        self.din('w_mem_kv', [4, D, 1024])
        self.din('w_up', [4, D, 2 * DFF])
        self.din('w_down', [4, DFF, D])
        self.din('w_kv', [D, 3072])
        self.din('w_cmp', [2, 32, 128, 128])
        self.din('lng', [128, 4, 2, 16])
        self.din('lnb', [128, 4, 2, 16])
        self.din('mixcw', [128, 2, 12, 3])
        self.din('ffncw', [128, 4, NF, 3])
        self.din('ident_f', [128, 128])
        self.din('ident_b', [128, 128], BF16)
        self.din('ones_b', [128, 128], BF16)
        self.dout('y', [T, D])
        self.dout('mem_kv', [4, NMEM, 1024])
        self.dout('conv_mix', [2, 128, 12, 2])
        self.dout('conv_ffn', [4, 128, NF, 2])
        self.dout('dbg', [128, 16, W])
        self.dout('kv_rows', [T, 2048])
        self.dout('win', [512, 1024])
        self.din('cmask', [128, NT * 4, 128], BF16)
        self.din('tk_mul', [128, NT * 4, 32], BF16)
        self.din('tk_add', [128, NT * 4, 32], BF16)
        self.din('tk_elig', [128, NT * 4, 32], BF16)
        self.din('tri', [128, 4, 512], BF16)
        self.din('wprev', [128, 4, 512], BF16)
        self.din('ovl', [32, 4, 32], BF16)
        self.din('posT', [128, 2, 32], BF16)
        self.din('posrep', [128, 32, 32], BF16)
        if self.do_sample:
            self.sample_decl()
        self.ksT_d = self.dscr('ksT_d', [4, 128, T], BF16)
        self.kwT_d = self.dscr('kwT_d', [4, 128, T], BF16)
        self.vs_d = self.dscr('vs_d', [T, 512], BF16)
        self.vw_d = self.dscr('vw_d', [T, 512], BF16)
        self.memKT_d = self.dscr('memKT_d', [2, 4, 128, 4, NMEM], BF16)
        self.memV_d = self.dscr('memV_d', [2, 4, NMEM, 512], BF16)

        with self.stack:
            self.P = P = Prog(nc, self.stack)
            self.pf = [self.ps(f"pf{i}", [128, 512], F32) for i in range(6)]
            self.pb = [self.ps(f"pb{i}", [128, 1024], BF16) for i in range(2)]
            self.wring = self.sb("wring", [128, NSLOT, 2048], BF16)
            self.wslot = 0
            self.ident_f = self.sb("ident_f", [128, 128], F32)
            self.ident_b = self.sb("ident_b", [128, 128], BF16)
            self.ones_b = self.sb("ones_b", [128, 128], BF16)
            self.lnb16 = [self.sb(f"lnb16_{i}", [128, W], BF16) for i in range(4)]
            self.lng = self.sb("lng", [128, 4, 2, 16], F32)
            self.lnb = self.sb("lnb", [128, 4, 2, 16], F32)
            self.mixcw = self.sb("mixcw", [128, 2, 12, 3], F32)
            self.ffncw = self.sb("ffncw", [128, 4, NF, 3], F32)
            self.cmask = self.sb("cmask", [128, NT * 4, 128], BF16)
            self.tk_mul = self.sb("tk_mul", [128, NT * 4, 32], BF16)
            self.tk_add = self.sb("tk_add", [128, NT * 4, 32], BF16)
            self.tk_elig = self.sb("tk_elig", [128, NT * 4, 32], BF16)
            self.tri = self.sb("tri", [128, 4, 512], BF16)
            self.wprev = self.sb("wprev", [128, 4, 512], BF16)
            self.ovl = self.sb("ovl", [32, 4, 32], BF16)
            self.posT = self.sb("posT", [128, 2, 32], BF16)
            self.posrep = self.sb("posrep", [128, 32, 32], BF16)
            for nm in ('ident_f', 'ident_b', 'ones_b', 'lng', 'lnb', 'mixcw', 'ffncw', 'cmask', 'tk_mul', 'tk_add', 'tk_elig',
                       'tri', 'wprev', 'ovl', 'posT', 'posrep'):
                P.dma('sp', getattr(self, nm)[:], self.ins[nm], writes=['const'])
            self.kcbuf = self.sb("kcbuf", [128, 4, 528], BF16)
            self.vcbuf = self.sb("vcbuf", [128, 4, 528], BF16)
            self.cmptmp = self.sb("cmptmp", [128, 4, 16], BF16)
            self.cmpKT = self.sb("cmpKT", [128, 4, 128], BF16)
            self.cmpV = [self.sb(f"cmpV{i}", [32, 512], BF16) for i in range(NT)]
            self.posK = self.sb("posK", [128, 1], F32)
            self.posV = self.sb("posV", [32, 128], F32)
            self.cmp_e = [self.sb(f"cmp_e{i}", [128, 128], F32) for i in range(2)]
            self.cmp_p = [self.sb(f"cmp_p{i}", [128, 128], BF16) for i in range(2)]
            self.cmp_pt = [self.sb(f"cmp_pt{i}", [32, 4, 128], BF16) for i in range(2)]
            self.cmp_den = [self.sb(f"cmp_den{i}", [128, 2], F32) for i in range(2)]
            self.sel_den = [self.sb(f"sel_den{i}", [128, 8], F32) for i in range(3)]
            self.win_den = [self.sb(f"win_den{i}", [128, 8], F32) for i in range(3)]
            self.nsa_acc = [self.sb(f"nsa_acc{i}", [128, 3, 128], F32) for i in range(2)]
            self.tk_sc = self.sb("tk_sc", [128, 32], F32)
            self.tk_sc2 = self.sb("tk_sc2", [128, 32], F32)
            self.tk_m8 = self.sb("tk_m8", [128, 16], F32)
            self.selmask = [self.sb(f"selmask{i}", [128, 32], BF16) for i in range(2)]
            self.mdiag = [self.sb(f"mdiag{i}", [128, 512], BF16) for i in range(2)]
            gsig = self.sb("gsig", [128, 4, 36], F32)
            self.lnmean = self.sb("lnmean", [128, W], F32)
            self.lnrstd = self.sb("lnrstd", [128, W], F32)
            self.lntmp = self.sb("lntmp", [128, W], F32)
            self.zf = [self.sb(f"zf{i}", [128, W + 2], F32) for i in range(2)]
            self.ct = [self.sb(f"ct{i}", [128, W], F32) for i in range(2)]
            self.att_e = [self.sb(f"att_e{i}", [128, 512], BF16) for i in range(2)]
            self.att_pt = [self.sb(f"att_pt{i}", [128, 4, 128], BF16) for i in range(2)]
            self.att_o = [self.sb(f"att_o{i}", [128, 128], BF16) for i in range(2)]
            self.att_den = self.sb("att_den", [128, 2, 4], F32)
            self.smallf = self.sb("smallf", [128, 2, 512], F32)
            self.smallb = self.sb("smallb", [128, 4, 512], BF16)
            NU = 44
            self.U = self.sb("U", [128, NU * 512], BF16)
            U = self.U
            hT = U[:, 0:NF * 512].rearrange("p (c t) -> p c t", t=512)
            mixT = U[:, 0:16 * 512].rearrange("p (c t) -> p c t", t=512)
            qmT = U[:, 16 * 512:20 * 512].rearrange("p (c t) -> p c t", t=512)
            qT = U[:, 20 * 512:32 * 512].rearrange("p (c t) -> p c t", t=512)
            for f in range(NF):
                P.define(('p', 'hT', f), [('U', f)])
            for j in range(16):
                P.define(('p', 'mixT', j), [('U', j)])
            for h in range(4):
                P.define(('p', 'qmT', h), [('U', 16 + h)])
            for h in range(12):
                P.define(('p', 'qT', h), [('U', 20 + h)])
            stg = [U[:, i * 4096:(i + 1) * 4096].bitcast(F32) for i in range(2)]
            stgn = [[('U', i * 8 + k) for k in range(8)] for i in range(2)]
            self.ksg = U[:, 32 * 512:36 * 512]
            self.vsg = U[:, 36 * 512:40 * 512].rearrange("p (c d) -> p c d", d=128)
            self.kwg = U[:, 40 * 512:42 * 512]
            self.vwg = U[:, 42 * 512:44 * 512].rearrange("p (c d) -> p c d", d=128)
            P.define('ksg', [('U', i) for i in range(32, 36)])
            P.define('vsg', [('U', i) for i in range(36, 40)])
            P.define('kwg', [('U', i) for i in range(40, 42)])
            P.define('vwg', [('U', i) for i in range(42, 44)])

            xres = self.sb("xres", [128, 16, W], F32)
            xT = self.sb("xT", [128, 16, W], BF16)
            memKT = self.sb("memKT", [128, 4, NMEM], BF16)
            memV = self.sb("memV", [128, 2, 512], BF16)
            carry_mix = self.sb("carry_mix", [128, 2, 12, 2], F32)
            carry_ffn = self.sb("carry_ffn", [128, 4, NF, 2], F32)
            P.op('dve', lambda e: e.memset(carry_mix[:], 0.0), writes=[('p', 'carry_mix', 0), ('p', 'carry_mix', 1)])
            P.op('dve', lambda e: e.memset(carry_ffn[:], 0.0), writes=[('p', 'carry_ffn', l) for l in range(4)])
            st = dict(xres=xres, xT=xT, mixT=mixT, hT=hT, qmT=qmT, qT=qT, ncols=W, tag='p', subw=[128] * 4, gsig=gsig,
                      memKT=memKT, memV=memV, memres='p_mem',
                      carry_mix=[carry_mix[:, l, :, :] for l in range(2)],
                      carry_ffn=[carry_ffn[:, l, :, :] for l in range(4)])

            stages = ['const', 'memproj', 'xload', 'mixer', 'memattn', 'wo', 'ln1', 'ffn_up', 'all']
            lvl = stages.index(self.upto)
            if lvl >= 1:
                self.mem_project_all()
            if self.n_layers > 2:
                self.cmp_pos_terms()

            for it in range(NT if lvl >= 2 else 0):
                for s in range(4):
                    self.load_transpose(self.ins['x'][it * W + s * 128: it * W + (s + 1) * 128, :], 128, stg[s % 2], stgn[s % 2],
                                        xres, xT, s * 128,
                                        lambda jq: [('p', 'xres', jq * 4 + i) for i in range(4)],
                                        lambda jq: [('p', 'xT', jq * 4 + i) for i in range(4)])
                for l in range(min(2, self.n_layers) if lvl >= 3 else 0):
                    self.load_mem(st, 0, l)
                    self.a_mixer(st, l)
                    if lvl >= 4:
                        self.mem_attention(st)
                    if lvl >= 5:
                        self.layer_tail(st, l)
                if self.n_layers > 2:
                    self.kv_project(st, it)
                    self.compress(it)
                    for jj in range(self.n_layers - 2):
                        l = 2 + jj
                        self.load_mem(st, 0, l)
                        self.b_inproj(st, jj)
                        self.nsa_prompt(st, it)
                        import os
                        if os.environ.get('DBG_MIX') and it == NT - 1 and jj == 0:
                            for j in range(12):
                                self.cp(self.zf[j % 2][:, 0:W], mixT[:, j, :], reads=[('p', 'mixT', j)], writes=[('zf', j % 2)], eng='dve')
                                P.dma('sp', self.outs['dbg'][:, j, :], self.zf[j % 2][:, 0:W], reads=[('zf', j % 2)])
                        self.mem_attention(st)
                        self.layer_tail(st, l)
                for s4 in range(4):
                    sg, sgn = stg[s4 % 2], stgn[s4 % 2]
                    for jq in range(4):
                        pt, ptn = self.pf[jq % 2], f'pf{jq % 2}'
                        for i in range(4):
                            j = jq * 4 + i
                            self.tr(pt[:, i * 128:(i + 1) * 128], xres[:, j, s4 * 128:(s4 + 1) * 128], self.ident_f[:, :],
                                    reads=[('p', 'xres', j), 'const'], writes=[ptn])
                        self.cp(sg[:, jq * 512:(jq + 1) * 512], pt[:, :], reads=[ptn], writes=sgn, eng='act')
                    P.dma('sp', self.outs['y'][it * W + s4 * 128:it * W + (s4 + 1) * 128, :], sg[:, :], reads=sgn)
            import os
            if not os.environ.get('DBG_MIX'):
                P.dma('sp', self.outs['dbg'], xres[:], reads=[('p', 'xres', j) for j in range(NCH)])
            for l in range(2):
                P.dma('sp', self.outs['conv_mix'][l], carry_mix[:, l, :, :], reads=[('p', 'carry_mix', l)])
            for l in range(4):
                P.dma('sp', self.outs['conv_ffn'][l], carry_ffn[:, l, :, :], reads=[('p', 'carry_ffn', l)])
            self.st_p = st
            if self.do_sample:
                self.sample_path()
            P.finish()
            P.run_block()
        return nc

    def kv_project(self, st, it):
        P = self.P
        xT, tag = st['xT'], st['tag']
        w = self.ins['w_kv']
        t0 = it * W
        U = self.U
        kTst = U[:, 0:4 * 512].rearrange("p (g t) -> p g t", t=512)
        kTn = [('U', i) for i in range(4)]
        kvb = U[:, 4 * 512:6 * 512].rearrange("p (a t) -> p a t", t=512)
        for nm in ('kcbuf', 'vcbuf'):
            buf = getattr(self, nm)
            if it == 0:
                P.op('dve', lambda e, buf=buf: e.memset(buf[:, :, 0:16], 0.0), writes=[nm])
            else:
                self.cp(self.cmptmp[:, :, :], buf[:, :, 512:528], reads=[nm], writes=['cmptmp'], eng='dve')
                self.cp(buf[:, :, 0:16], self.cmptmp[:, :, :], reads=['cmptmp'], writes=[nm], eng='dve')
        for c in range(6):
            for kq in range(4):
                slot, sname = self.wload(w[kq * 512:(kq + 1) * 512, c * 512:(c + 1) * 512].rearrange("(k p) n -> p k n", p=128),
                                         [128, 4, 512])
                for s in range(4):
                    for kk in range(4):
                        k = kq * 4 + kk
                        self.mm(self.pf[s][:, :], xT[:, k, s * 128:(s + 1) * 128], slot[:, kk, :], k == 0, k == 15,
                                reads=[sname, (tag, 'xT', k)], writes=[f'pf{s}'])
            for s in range(4):
                kf, kfn = self.smallf[:, s % 2, :], ('smallf', s % 2)
                self.cp(kf, self.pf[s][:, :], reads=[f'pf{s}'], writes=[kfn], eng='act')
                if c < 4:
                    P.dma('sp', self.outs['kv_rows'][t0 + s * 128:t0 + (s + 1) * 128, c * 512:(c + 1) * 512], kf, reads=[kfn])
                elif it == self.NT - 1:
                    P.dma('sp', self.outs['win'][s * 128:(s + 1) * 128, (c - 4) * 512:(c - 3) * 512], kf, reads=[kfn])
                kb, kbn = kvb[:, s % 2, :], ('U', 4 + s % 2)
                self.cp(kb, kf, reads=[kfn], writes=[kbn], eng='dve')
                if c in (3, 5):
                    dst = self.vs_d if c == 3 else self.vw_d
                    P.dma('sp', dst[t0 + s * 128:t0 + (s + 1) * 128, :], kb, reads=[kbn], writes=[('vs_d' if c == 3 else 'vw_d')])
                else:
                    pT, pTn = self.pb[s % 2], f'pb{s % 2}'
                    for g in range(4):
                        self.tr(pT[:, g * 128:(g + 1) * 128], kb[:, g * 128:(g + 1) * 128], self.ident_b[:, :],
                                reads=[kbn, 'const'], writes=[pTn])
                    src = pT[:, 0:512].rearrange("p (g t) -> p g t", t=128)
                    if c == 0:
                        self.cp(self.kcbuf[:, :, 16 + s * 128:16 + (s + 1) * 128], src, reads=[pTn], writes=['kcbuf'], eng='act')
                    elif c == 1:
                        self.cp(self.vcbuf[:, :, 16 + s * 128:16 + (s + 1) * 128], src, reads=[pTn], writes=['vcbuf'], eng='act')
                    else:
                        self.cp(kTst[:, :, s * 128:(s + 1) * 128], src, reads=[pTn], writes=kTn, eng='act')
            if c in (2, 4):
                dst = self.ksT_d if c == 2 else self.kwT_d
                P.dma('sp', dst[:, :, t0:t0 + W].rearrange("g d t -> d g t"), kTst, reads=kTn,
                      writes=[('ksT_d' if c == 2 else 'kwT_d')])

    def compress(self, it, grp=0):
        wc = self.ins['w_cmp']
        for kind in range(2):
            for half in range(2):
                slot, sname = self.wload(wc[kind, half * 16:(half + 1) * 16].rearrange("l d e -> d l e"), [128, 16, 128])
                for g in range(4):
                    for ll in range(16):
                        l = half * 16 + ll
                        if kind == 0:
                            self.mm(self.pf[g][:, 0:32], slot[:, ll, :], self.kcbuf[:, g, l:l + 16 * 31 + 1:16], l == 0, l == 31,
                                    reads=[sname, 'kcbuf'], writes=[f'pf{g}'])
                        else:
                            self.mm(self.pf[g][0:32, 0:128], self.vcbuf[:, g, l:l + 16 * 31 + 1:16], slot[:, ll, :], l == 0, l == 31,
                                    reads=[sname, 'vcbuf'], writes=[f'pf{g}'])
            for g in range(4):
                if kind == 0:
                    self.act(self.cmpKT[:, g, 32 * it:32 * it + 32], self.pf[g][:, 0:32], AF.Identity, bias=self.posK[:, 0:1],
                             reads=[f'pf{g}', 'posK'], writes=['cmpKT'])
                else:
                    self.tt(self.cmpV[it][:, g * 128:(g + 1) * 128], self.pf[g][0:32, 0:128], self.posV[:, :], ALU.add,
                            reads=[f'pf{g}', 'posV'], writes=[('cmpV', it)])

    def cmp_pos_terms(self):
        wc = self.ins['w_cmp']
        for kind in range(2):
            for half in range(2):
                slot, sname = self.wload(wc[kind, half * 16:(half + 1) * 16].rearrange("l d e -> d l e"), [128, 16, 128])
                for ll in range(16):
                    l = half * 16 + ll
                    if kind == 0:
                        self.mm(self.pf[0][:, 0:1], slot[:, ll, :], self.posT[:, 0, l:l + 1], l == 0, l == 31,
                                reads=[sname, 'const'], writes=['pf0'])
                    else:
                        self.mm(self.pf[1][0:32, 0:128], self.posrep[:, l, :], slot[:, ll, :], l == 0, l == 31,
                                reads=[sname, 'const'], writes=['pf1'])
        self.cp(self.posK[:, 0:1], self.pf[0][:, 0:1], reads=['pf0'], writes=['posK'], eng='act')
        self.cp(self.posV[:, :], self.pf[1][0:32, 0:128], reads=['pf1'], writes=['posV'], eng='act')

    def b_inproj(self, st, jj):
        ncols, tag = st['ncols'], st['tag']
        xT, qT, qmT = st['xT'], st['qT'], st['qmT']
        w_in = self.ins['w_in_b'][jj]
        xr = lambda k: xT[:, k, :ncols]
        xn = lambda k: (tag, 'xT', k)
        for h in range(12):
            ps, psn = self.pf[h % 4], f'pf{h % 4}'
            self.proj_fm(self.wcols(w_in, h * 128), NCH, xr, xn, ps, psn, ncols)
            self.cp(qT[:, h, :ncols], ps[:, :ncols], reads=[psn], writes=[(tag, 'qT', h)], eng='act')
        for h in range(4):
            ps, psn = self.pf[h % 4], f'pf{h % 4}'
            self.proj_fm(self.wcols(w_in, DA + 36 + h * 128), NCH, xr, xn, ps, psn, ncols)
            self.cp(qmT[:, h, :ncols], ps[:, :ncols], reads=[psn], writes=[(tag, 'qmT', h)], eng='act')
        slot, sname = self.wload(w_in[:, DA:DA + 36].rearrange("(k p) n -> p k n", p=128), [128, 16, 36])
        gs = st['gsig']
        for s, m in enumerate(st['subw']):
            ps, psn = self.pf[4], 'pf4'
            for k in range(NCH):
                self.mm(ps[:m, 0:36], xT[:, k, s * 128:s * 128 + m], slot[:, k, :], k == 0, k == NCH - 1,
                        reads=[sname, (tag, 'xT', k)], writes=[psn])
            self.act(gs[:m, s, :], ps[:m, 0:36], AF.Sigmoid, reads=[psn], writes=[(tag, 'gsig')])

    def attn_unit(self, u, q_lhsT, q_res, m, kT, k_res, vtok, v_res, nkeys, mask_fn, mask_res, den_ap, den_res, pO, pOn, first, last,
                  deferred=False, post=None):
        pS, pSn = self.pf[u % 2], f'pf{u % 2}'
        E, En = self.att_e[u % 2], ('att_e', u % 2)
        PT, PTn = self.att_pt[u % 2], ('att_pt', u % 2)
        pT, pTn = self.pb[u % 2], f'pb{u % 2}'
        mk = mask_fn()
        qr = q_res if isinstance(q_res, list) else [q_res]

        def front():
            self.mm(pS[:m, :nkeys], q_lhsT, kT, True, True, reads=qr + [k_res], writes=[pSn])
            self.act(E[:m, :nkeys], pS[:m, :nkeys], AF.Exp, scale=SCALE, reads=[pSn], writes=[En])
            Ev = E[:m, :nkeys] if mk.ndim == 2 else E[:m, :nkeys].rearrange("p (b k) -> p b k", k=64)
            self.stt(Ev, Ev, 1.0, mk, ALU.mult, ALU.mult, reads=[En] + mask_res, writes=[En, den_res], accum_out=den_ap)

        def back():
            nc_ = nkeys // 128
            for c in range(nc_):
                self.tr(pT[:, c * 128:c * 128 + m], E[:m, c * 128:(c + 1) * 128], self.ident_b[:m, :m], reads=[En, 'const'], writes=[pTn])
            self.cp(PT[:, 0:nc_, :m], pT[:, 0:nc_ * 128].rearrange("p (c q) -> p c q", q=128)[:, :, :m], reads=[pTn], writes=[PTn], eng='act')
            for c in range(nc_):
                self.mm(pO[:m, :128], PT[:, c, :m], vtok(c), first and c == 0, last and c == nc_ - 1, reads=[PTn, v_res], writes=[pOn])
            if post is not None:
                post()
        if deferred:
            return front, back
        front()
        back()

    @staticmethod
    def run_pipelined(units):
        pend = None
        for fr, bk in units:
            fr()
            if pend is not None:
                pend()
            pend = bk
        if pend is not None:
            pend()

    def nsa_prompt(self, st, it):
        self._u = 0
        A = []
        B = []
        for g in range(4):
            for s in range(4):
                k = g * 4 + s
                A.append(lambda g=g, s=s, k=k: self._nsa_A(st, it, g, s, k))
                B.append(lambda g=g, s=s, k=k: self._nsa_B(st, it, g, s, k))
        A[0]()
        for k in range(16):
            if k + 1 < 16:
                A[k + 1]()
            B[k]()

    def _nsa_A(self, st, it, g, s, k):
        P = self.P
        tag, qT, gs = st['tag'], st['qT'], st['gsig']
        ncmp = 32 * (it + 1)
        si = it * 4 + s
        qs = slice(s * 128, (s + 1) * 128)
        kb = k % 2
        acc = self.nsa_acc[kb]
        accn = lambda r: ('nsa_acc', kb, r)
        units = []
        for r in range(3):
            h = 3 * g + r
            u = self._u
            self._u += 1
            ub = u % 2
            pS, pSn = self.pf[ub], f'pf{ub}'
            pT, pTn = self.pb[ub], f'pb{ub}'
            pO, pOn = self.pf[2 + ub], f'pf{2 + ub}'
            Ec, Ecn = self.cmp_e[ub], ('cmp_e', ub)
            Pc, Pcn = self.cmp_p[ub], ('cmp_p', ub)
            PnT, PnTn = self.cmp_pt[ub], ('cmp_pt', ub)
            cd, cdn = self.cmp_den[ub], ('cmp_den', ub)

            def front(h=h, pS=pS, pSn=pSn, Ec=Ec, Ecn=Ecn, Pc=Pc, Pcn=Pcn, cd=cd, cdn=cdn):
                self.mm(pS[:, :ncmp], qT[:, h, qs], self.cmpKT[:, g, 0:ncmp], True, True,
                        reads=[(tag, 'qT', h), 'cmpKT'], writes=[pSn])
                self.act(Ec[:, :ncmp], pS[:, :ncmp], AF.Exp, scale=SCALE, reads=[pSn], writes=[Ecn])
                self.stt(Ec[:, :ncmp], Ec[:, :ncmp], 1.0, self.cmask[:, si, 0:ncmp], ALU.mult, ALU.mult,
                         reads=[Ecn, 'const'], writes=[Ecn, cdn], accum_out=cd[:, 0:1])
                self.ts(cd[:, 1:2], cd[:, 0:1], 1e-30, None, ALU.max, None, reads=[cdn], writes=[cdn])
                self.recip(cd[:, 1:2], cd[:, 1:2], reads=[cdn], writes=[cdn])
                self.ts(Pc[:, :ncmp], Ec[:, :ncmp], cd[:, 1:2], None, ALU.mult, None, reads=[Ecn, cdn], writes=[Pcn])

            def back(r=r, h=h, pT=pT, pTn=pTn, pO=pO, pOn=pOn, Pc=Pc, Pcn=Pcn, PnT=PnT, PnTn=PnTn):
                for i2 in range(it + 1):
                    self.tr(pT[0:32, i2 * 128:(i2 + 1) * 128], Pc[:, 32 * i2:32 * i2 + 32], self.ident_b[:, :],
                            reads=[Pcn, 'const'], writes=[pTn])
                self.cp(PnT[:, 0:it + 1, :], pT[0:32, 0:(it + 1) * 128].rearrange("p (c q) -> p c q", q=128), reads=[pTn], writes=[PnTn], eng='act')
                for i2 in range(it + 1):
                    self.mm(pO[:, :128], PnT[:, i2, :], self.cmpV[i2][:, g * 128:(g + 1) * 128], i2 == 0, i2 == it,
                            reads=[PnTn, ('cmpV', i2)], writes=[pOn])
                for i2 in range(it + 1):
                    self.mm(self.pf[4][:, 0:32], PnT[:, i2, :], self.ovl[:, i2, :], r == 0 and i2 == 0, r == 2 and i2 == it,
                            reads=[PnTn, 'const'], writes=['pf4'])
                self.ts(acc[:, r, :], pO[:, :128], gs[:, s, h:h + 1], None, ALU.mult, None, reads=[pOn, (tag, 'gsig')], writes=[accn(r)])
            units.append((front, back))
        self.run_pipelined(units)
        sc, sc2, m8 = self.tk_sc, self.tk_sc2, self.tk_m8
        selmask, smn = self.selmask[kb], ('selmask', kb)
        mdiag, mdn = self.mdiag[kb], ('mdiag', kb)
        self.tt(sc[:, :], self.pf[4][:, 0:32], self.tk_mul[:, si, :], ALU.mult, reads=['pf4', 'const'], writes=['tk_sc'])
        self.tt(sc[:, :], sc[:, :], self.tk_add[:, si, :], ALU.add, reads=['tk_sc', 'const'], writes=['tk_sc'])
        P.op('dve', lambda e: e.max(m8[:, 0:8], sc[:, :]), reads=['tk_sc'], writes=['tk_m8'])
        P.op('dve', lambda e: e.match_replace(sc2[:, :], m8[:, 0:8], sc[:, :], NEG), reads=['tk_sc', 'tk_m8'], writes=['tk_sc2'])
        P.op('dve', lambda e: e.max(m8[:, 8:16], sc2[:, :]), reads=['tk_sc2'], writes=['tk_m8'])
        self.ts(sc2[:, :], sc[:, :], m8[:, 15:16], None, ALU.is_ge, None, reads=['tk_sc', 'tk_m8'], writes=['tk_sc2'])
        self.tt(selmask[:, :], sc2[:, :], self.tk_elig[:, si, :], ALU.mult, reads=['tk_sc2', 'const'], writes=[smn])
        self.tt(mdiag[:, :].rearrange("p (b k) -> p b k", k=64), self.tri[:, s, :].rearrange("p (b k) -> p b k", k=64),
                selmask[:, 8 * it:8 * it + 8].unsqueeze(2).to_broadcast([128, 8, 64]), ALU.mult,
                reads=[smn, 'const'], writes=[mdn])

    def _nsa_B(self, st, it, g, s, k):
        P = self.P
        tag, qT, mixT, gs = st['tag'], st['qT'], st['mixT'], st['gsig']
        nk = W * (it + 1)
        w0 = max(it - 1, 0) * W
        nw = nk - w0
        qs = slice(s * 128, (s + 1) * 128)
        kb = k % 2
        acc = self.nsa_acc[kb]
        accn = lambda r: ('nsa_acc', kb, r)
        selmask, smn = self.selmask[kb], ('selmask', kb)
        mdiag, mdn = self.mdiag[kb], ('mdiag', kb)
        if s == 0:
            P.dma('sp', self.ksg[:, 0:nk], self.ksT_d[g, :, 0:nk], reads=['ksT_d'], writes=['ksg'])
            P.dma('sp', self.vsg[:, 0:4 * (it + 1), :], self.vs_d[0:nk, g * 128:(g + 1) * 128].rearrange("(c p) d -> p c d", p=128),
                  reads=['vs_d'], writes=['vsg'])
            P.dma('sp', self.kwg[:, 0:nw], self.kwT_d[g, :, w0:nk], reads=['kwT_d'], writes=['kwg'])
            P.dma('sp', self.vwg[:, 0:nw // 128, :], self.vw_d[w0:nk, g * 128:(g + 1) * 128].rearrange("(c p) d -> p c d", p=128),
                  reads=['vw_d'], writes=['vwg'])
        units = []
        for r in range(3):
            h = 3 * g + r
            pO, pOn = self.pf[2 + r % 2], f'pf{2 + r % 2}'
            dn = self.sel_den[r]
            dnn = lambda k2, r=r: ('sel_den', r, k2)

            def post_sel(r=r, h=h, pO=pO, pOn=pOn, dn=dn, dnn=dnn):
                self.cp(dn[:, 4:5], dn[:, 0:1], reads=[dnn(0)], writes=[dnn(4)], eng='dve')
                for k2 in range(1, it + 1):
                    self.tt(dn[:, 4:5], dn[:, 4:5], dn[:, k2:k2 + 1], ALU.add, reads=[dnn(4), dnn(k2)], writes=[dnn(4)])
                self.ts(dn[:, 4:5], dn[:, 4:5], 1e-30, None, ALU.max, None, reads=[dnn(4)], writes=[dnn(4)])
                self.recip(dn[:, 5:6], dn[:, 4:5], reads=[dnn(4)], writes=[dnn(5)])
                self.tt(dn[:, 5:6], dn[:, 5:6], gs[:, s, 12 + h:13 + h], ALU.mult, reads=[dnn(5), (tag, 'gsig')], writes=[dnn(5)])
                self.stt(acc[:, r, :], pO[:, :128], dn[:, 5:6], acc[:, r, :], ALU.mult, ALU.add,
                         reads=[pOn, dnn(5), accn(r)], writes=[accn(r)])
            for kg in range(it + 1):
                if kg == it:
                    mfn = lambda: mdiag[:, :]
                    mres = [mdn]
                else:
                    mfn = lambda kg=kg: selmask[:, 8 * kg:8 * kg + 8].unsqueeze(2).to_broadcast([128, 8, 64])
                    mres = [smn]
                u = self._u
                self._u += 1
                units.append(self.attn_unit(u, qT[:, h, qs], (tag, 'qT', h), 128, self.ksg[:, kg * W:(kg + 1) * W], 'ksg',
                                            lambda c, kg=kg: self.vsg[:, kg * 4 + c, :], 'vsg', W, mfn, mres,
                                            dn[:, kg:kg + 1], dnn(kg), pO, pOn, kg == 0, kg == it,
                                            deferred=True, post=post_sel if kg == it else None))
        ngrp = nw // W
        for r in range(3):
            h = 3 * g + r
            pO, pOn = self.pf[2 + (r + 1) % 2], f'pf{2 + (r + 1) % 2}'
            dn = self.win_den[r]
            dnn = lambda k2, r=r: ('win_den', r, k2)

            def post_win(r=r, h=h, pO=pO, pOn=pOn, dn=dn, dnn=dnn):
                if ngrp > 1:
                    self.tt(dn[:, 4:5], dn[:, 0:1], dn[:, 1:2], ALU.add, reads=[dnn(0), dnn(1)], writes=[dnn(4)])
                else:
                    self.cp(dn[:, 4:5], dn[:, 0:1], reads=[dnn(0)], writes=[dnn(4)], eng='dve')
                self.recip(dn[:, 5:6], dn[:, 4:5], reads=[dnn(4)], writes=[dnn(5)])
                self.tt(dn[:, 5:6], dn[:, 5:6], gs[:, s, 24 + h:25 + h], ALU.mult, reads=[dnn(5), (tag, 'gsig')], writes=[dnn(5)])
                mo, mon = self.att_o[r % 2], ('att_o', r % 2)
                self.stt(mo[:, :], pO[:, :128], dn[:, 5:6], acc[:, r, :], ALU.mult, ALU.add,
                         reads=[pOn, dnn(5), accn(r)], writes=[mon])
                pT, pTn = self.pb[r % 2], f'pb{r % 2}'
                self.tr(pT[:, 512:640], mo[:, :], self.ident_b[:, :], reads=[mon, 'const'], writes=[pTn])
                self.cp(mixT[:, h, qs], pT[:, 512:640], reads=[pTn], writes=[(tag, 'mixT', h)], eng='dve')
            for kg in range(ngrp):
                if kg == ngrp - 1:
                    mfn = lambda: self.tri[:, s, :]
                else:
                    mfn = lambda: self.wprev[:, s, :]
                u = self._u
                self._u += 1
                units.append(self.attn_unit(u, qT[:, h, qs], (tag, 'qT', h), 128, self.kwg[:, kg * W:(kg + 1) * W], 'kwg',
                                            lambda c, kg=kg: self.vwg[:, kg * 4 + c, :], 'vwg', W, mfn, ['const'],
                                            dn[:, kg:kg + 1], dnn(kg), pO, pOn, kg == 0, kg == ngrp - 1,
                                            deferred=True, post=post_win if kg == ngrp - 1 else None))
        self.run_pipelined(units)

    def sample_decl(self):
        self.din('xs', [128, 16])
        self.din('cache_kv', [1280 * 128, 2048])
        self.din('page_tab', [1, 128], I32)
        self.din('riota', [128, 1])
        self.din('cache_win', [512, 1024])
        self.din('cache_mem', [4, NMEM, 1024])
        self.din('st_mix', [128, 2, 12, 2])
        self.din('st_ffn', [128, 4, NF, 2])
        self.din('cmask_s', [3, 1024], BF16)
        self.din('tk_s', [3, 2, 264])
        self.din('wmask_s', [3, 512], BF16)
        self.din('ovl_s', [128, 8, 264], BF16)
        self.dout('ys', [128, 16])
        self.dout('kv_rows_s', [1, 2048])
        self.dout('win_s', [512, 1024])
        self.dout('conv_mix_s', [2, 128, 12, 2])
        self.dout('conv_ffn_s', [4, 128, NF, 2])
        self.ksT_s_d = self.dscr('ksT_s_d', [4, 128, 16384], BF16)
        self.vs_s_d = self.dscr('vs_s_d', [16384, 512], BF16)
        self.kwT_s_d = self.dscr('kwT_s_d', [4, 128, 512], BF16)
        self.vw_s_d = self.dscr('vw_s_d', [512, 512], BF16)
        self.cmpV_s_d = self.dscr('cmpV_s_d', [1024, 512], BF16)

    def sample_mem_setup(self):
        P = self.P
        cm = self.ins['cache_mem']
        mb = self.U[:, 0:2 * 1024].rearrange("p (c n) -> p c n", n=1024)
        mbn = [('U', i) for i in range(4)]
        kt = self.smallb
        for l in range(4):
            P.dma('pool', mb, cm[l].rearrange("(c p) n -> p c n", p=128), writes=mbn)
            P.dma('sp', self.memV_d[1, l].rearrange("(c p) n -> p c n", p=128), mb[:, :, 512:1024], reads=mbn, writes=[('memV_d', 1, l)])
            for h in range(4):
                pT, pTn = self.pb[h % 2], f'pb{h % 2}'
                for c in range(2):
                    self.tr(pT[:, c * 128:(c + 1) * 128], mb[:, c, h * 128:(h + 1) * 128], self.ident_b[:, :], reads=mbn + ['const'], writes=[pTn])
                self.cp(kt[:, h, 0:256], pT[:, 0:256], reads=[pTn], writes=[('smallb', h)], eng='act')
            P.dma('sp', self.memKT_d[1, l], kt[:, :, 0:256], reads=[('smallb', h) for h in range(4)], writes=[('memKT_d', 1, l)])

    def sample_gather_pass(self):
        P = self.P
        U = self.U
        ckv = self.ins['cache_kv']
        ptb = self.s_ptb
        P.dma('sp', ptb[:, :], self.ins['page_tab'].partition_broadcast(128), writes=['s_ptb'])
        P.dma('sp', self.s_riota[:, :], self.ins['riota'], writes=['s_riota'])
        self.ts(self.s_idx[:, :], ptb[:, :], 128.0, self.s_riota[:, 0:1], ALU.mult, ALU.add, reads=['s_ptb', 's_riota'], writes=['s_idx'])
        gat = [U[:, i * 4096:(i + 1) * 4096].bitcast(F32) for i in range(2)]
        gatn = [[('U', i * 8 + k) for k in range(8)] for i in range(2)]
        gb = [U[:, (16 + 4 * i) * 512:(20 + 4 * i) * 512] for i in range(2)]
        gbn = [[('U', 16 + 4 * i + k) for k in range(4)] for i in range(2)]
        kTst = U[:, 24 * 512:28 * 512].rearrange("p (g t) -> p g t", t=512)
        kTn = [('U', 24 + i) for i in range(4)]
        wres = U[:, 28 * 512:44 * 512].rearrange("p (k l e) -> p k l e", k=2, l=32)
        wresn = [('U', 28 + i) for i in range(16)]
        wc = self.ins['w_cmp']
        for kind in range(2):
            for half in range(2):
                P.dma('pool', wres[:, kind, half * 16:(half + 1) * 16, :], wc[kind, half * 16:(half + 1) * 16].rearrange("l d e -> d l e"),
                      writes=wresn)
        for i in range(32):
            for nm in ('kcbuf', 'vcbuf'):
                buf = getattr(self, nm)
                if i == 0:
                    P.op('dve', lambda e, buf=buf: e.memset(buf[:, :, 0:16], 0.0), writes=[nm])
                else:
                    self.cp(self.cmptmp[:, :, :], buf[:, :, 512:528], reads=[nm], writes=['cmptmp'], eng='dve')
                    self.cp(buf[:, :, 0:16], self.cmptmp[:, :, :], reads=['cmptmp'], writes=[nm], eng='dve')
            for pgl in range(4):
                pg = 4 * i + pgl
                G, Gn = gat[pg % 2], gatn[pg % 2]
                B, Bn = gb[pg % 2], gbn[pg % 2]
                P.idma(G[:, :], ckv, self.s_idx[:, pg:pg + 1], reads=['s_idx'], writes=Gn)
                self.cp(B[:, 0:1024], G[:, 0:1024], reads=Gn, writes=Bn, eng='dve')
                self.cp(B[:, 1024:2048], G[:, 1024:2048], reads=Gn, writes=Bn, eng='act')
                P.dma('sp', self.vs_s_d[pg * 128:(pg + 1) * 128, :], B[:, 1536:2048], reads=Bn, writes=['vs_s_d'])
                for kind in range(3):
                    pT, pTn = self.pb[kind % 2], f'pb{kind % 2}'
                    for g in range(4):
                        self.tr(pT[:, g * 128:(g + 1) * 128], B[:, kind * 512 + g * 128:kind * 512 + (g + 1) * 128], self.ident_b[:, :],
                                reads=Bn + ['const'], writes=[pTn])
                    src = pT[:, 0:512].rearrange("p (g t) -> p g t", t=128)
                    if kind == 0:
                        self.cp(self.kcbuf[:, :, 16 + pgl * 128:16 + (pgl + 1) * 128], src, reads=[pTn], writes=['kcbuf'], eng='act')
                    elif kind == 1:
                        self.cp(self.vcbuf[:, :, 16 + pgl * 128:16 + (pgl + 1) * 128], src, reads=[pTn], writes=['vcbuf'], eng='dve')
                    else:
                        self.cp(kTst[:, :, pgl * 128:(pgl + 1) * 128], src, reads=[pTn], writes=kTn, eng='act')
            P.dma('sp', self.ksT_s_d[:, :, i * 512:(i + 1) * 512].rearrange("g d t -> d g t"), kTst, reads=kTn, writes=['ksT_s_d'])
            for kind in range(2):
                for g in range(4):
                    for l in range(32):
                        if kind == 0:
                            self.mm(self.pf[g][:, 0:32], wres[:, 0, l, :], self.kcbuf[:, g, l:l + 16 * 31 + 1:16], l == 0, l == 31,
                                    reads=wresn + ['kcbuf'], writes=[f'pf{g}'])
                        else:
                            self.mm(self.pf[g][0:32, 0:128], self.vcbuf[:, g, l:l + 16 * 31 + 1:16], wres[:, 1, l, :], l == 0, l == 31,
                                    reads=wresn + ['vcbuf'], writes=[f'pf{g}'])
                for g in range(4):
                    if kind == 0:
                        self.act(self.cmpKT_s[:, g, 32 * i:32 * i + 32], self.pf[g][:, 0:32], AF.Identity, bias=self.posK[:, 0:1],
                                 reads=[f'pf{g}', 'posK'], writes=['cmpKT_s'])
                    else:
                        self.tt(self.cvst[i % 2][:, g * 128:(g + 1) * 128], self.pf[g][0:32, 0:128], self.posV[:, :], ALU.add,
                                reads=[f'pf{g}', 'posV'], writes=[('cvst', i % 2)])
            P.dma('sp', self.cmpV_s_d[32 * i:32 * i + 32, :], self.cvst[i % 2][:, :], reads=[('cvst', i % 2)], writes=['cmpV_s_d'])

    def sample_win_setup(self):
        P = self.P
        cw = self.ins['cache_win']
        wb = self.U[:, 0:4 * 1024].rearrange("p (c n) -> p c n", n=1024)
        wbn = [('U', i) for i in range(8)]
        kTst = self.U[:, 8 * 512:12 * 512].rearrange("p (g t) -> p g t", t=512)
        kTn = [('U', 8 + i) for i in range(4)]
        P.dma('pool', wb, cw.rearrange("(c p) n -> p c n", p=128), writes=wbn)
        P.dma('sp', self.vw_s_d.rearrange("(c p) n -> p c n", p=128), wb[:, :, 512:1024], reads=wbn, writes=['vw_s_d'])
        for c in range(4):
            pT, pTn = self.pb[c % 2], f'pb{c % 2}'
            for g in range(4):
                self.tr(pT[:, g * 128:(g + 1) * 128], wb[:, c, g * 128:(g + 1) * 128], self.ident_b[:, :], reads=wbn + ['const'], writes=[pTn])
            self.cp(kTst[:, :, c * 128:(c + 1) * 128], pT[:, 0:512].rearrange("p (g t) -> p g t", t=128), reads=[pTn], writes=kTn, eng='act')
        P.dma('sp', self.kwT_s_d.rearrange("g d t -> d g t"), kTst, reads=kTn, writes=['kwT_s_d'])
        P.dma('sp', self.outs['win_s'][0:511, :], cw[1:512, :])

    def sample_kv(self, st):
        P = self.P
        xT, tag = st['xT'], st['tag']
        w = self.ins['w_kv']
        kvf, kvb = self.s_kvf, self.s_kvb
        for c in range(6):
            ps, psn = self.pf[c % 2], f'pf{c % 2}'
            for kq in range(4):
                slot, sname = self.wload(w[kq * 512:(kq + 1) * 512, c * 512:(c + 1) * 512].rearrange("(k p) n -> p k n", p=128), [128, 4, 512])
                for kk in range(4):
                    k = kq * 4 + kk
                    self.mm(ps[0:1, :], xT[:, k, 0:1], slot[:, kk, :], k == 0, k == 15, reads=[sname, (tag, 'xT', k)], writes=[psn])
            self.cp(kvf[0:1, c * 512:(c + 1) * 512], ps[0:1, :], reads=[psn], writes=['s_kvf'], eng='act')
        self.cp(kvb[0:1, :], kvf[0:1, :], reads=['s_kvf'], writes=['s_kvb'], eng='dve')
        P.dma('sp', self.outs['kv_rows_s'], kvf[0:1, 0:2048], reads=['s_kvf'])
        P.dma('sp', self.outs['win_s'][511:512, :], kvf[0:1, 2048:3072], reads=['s_kvf'])
        pT, pTn = self.pb[0], 'pb0'
        for i, c0 in enumerate([1024 + g * 128 for g in range(4)] + [2048 + g * 128 for g in range(4)]):
            self.tr(pT[:, 2 * i:2 * i + 1], kvb[0:1, c0:c0 + 128], self.ident_b[0:1, 0:1], reads=['s_kvb', 'const'], writes=[pTn])
        self.cp(self.s_knT[:, 0:8], pT[:, 0:16:2], reads=[pTn], writes=['s_knT'], eng='act')

    def new_key_unit(self, u, q3, q_res, knT_col, vrow, mask_ap, mask_res, den_ap, den_res, pO, pOn, first, last):
        pS, pSn = self.pf[u % 2], f'pf{u % 2}'
        pT, pTn = self.pb[u % 2], f'pb{u % 2}'
        e1 = self.s_e1
        self.mm(pS[:3, 0:1], q3, knT_col, True, True, reads=[q_res, 's_knT'], writes=[pSn])
        self.act(e1[:3, 0:1], pS[:3, 0:1], AF.Exp, scale=SCALE, reads=[pSn], writes=['s_e1'])
        if mask_ap is not None:
            self.tt(e1[:3, 0:1], e1[:3, 0:1], mask_ap, ALU.mult, reads=['s_e1'] + mask_res, writes=['s_e1'])
        self.cp(den_ap, e1[:3, 0:1], reads=['s_e1'], writes=[den_res], eng='dve')
        self.cp(e1[:3, 1:2], e1[:3, 0:1], reads=['s_e1'], writes=['s_e1b'], eng='dve')
        self.cp(self.s_e1b[:3, 0:1], e1[:3, 0:1], reads=['s_e1'], writes=['s_e1b'], eng='dve')
        self.tr(pT[0:1, 0:3], self.s_e1b[:3, 0:1], self.ident_b[:3, :3], reads=['s_e1b', 'const'], writes=[pTn])
        self.cp(self.s_pt1[0:1, 0:3], pT[0:1, 0:3], reads=[pTn], writes=['s_pt1'], eng='act')
        self.mm(pO[:3, :128], self.s_pt1[0:1, 0:3], vrow, first, last, reads=['s_pt1', 's_kvb'], writes=[pOn])

    def nsa_sample(self, st, jj):
        P = self.P
        tag, qT, mixT, xT = st['tag'], st['qT'], st['mixT'], st['xT']
        w_in = self.ins['w_in_b'][jj]
        cmpV = self.U[:, 0:8 * 512].rearrange("p (c n) -> p c n", n=512)
        cvn = [('U', i) for i in range(8)]
        P.dma('sp', cmpV, self.cmpV_s_d.rearrange("(c p) n -> p c n", p=128), reads=['cmpV_s_d'], writes=cvn)
        gslot, gname = self.wload(w_in[:, DA:DA + 36].rearrange("(k p) n -> p k n", p=128), [128, 16, 36])
        u = 0
        for g in range(4):
            q3 = qT[:, 3 * g:3 * g + 3, 0:1].rearrange("p h o -> p (h o)")
            qres = (tag, 'qT', 3 * g)
            qresl = [(tag, 'qT', 3 * g + r) for r in range(3)]
            pg_, pgn = self.pf[4], 'pf4'
            for b in range(3):
                for k in range(NCH):
                    self.mm(pg_[0:3, b:b + 1], gslot[:, k, 12 * b + 3 * g:12 * b + 3 * g + 3], xT[:, k, 0:1], k == 0, k == NCH - 1,
                            reads=[gname, (tag, 'xT', k)], writes=[pgn])
            gs = self.s_gs
            self.act(gs[0:3, 0:3], pg_[0:3, 0:3], AF.Sigmoid, reads=[pgn], writes=['s_gs'])
            Ec, Pc = self.s_ec, self.s_pc
            for kg in range(2):
                pS, pSn = self.pf[u % 2], f'pf{u % 2}'
                u += 1
                self.mm(pS[:3, :], q3, self.cmpKT_s[:, g, kg * 512:(kg + 1) * 512], True, True, reads=qresl + ['cmpKT_s'], writes=[pSn])
                self.act(Ec[:3, kg * 512:(kg + 1) * 512], pS[:3, :], AF.Exp, scale=SCALE, reads=[pSn], writes=['s_ec'])
            dn = self.s_den
            self.stt(Ec[:3, :], Ec[:3, :], 1.0, self.cmask_s[:3, :], ALU.mult, ALU.mult, reads=['s_ec', 'sconst'], writes=['s_ec', 's_den'],
                     accum_out=dn[:3, 0:1])
            self.recip(dn[:3, 1:2], dn[:3, 0:1], reads=['s_den'], writes=['s_den'])
            self.ts(Pc[:3, :], Ec[:3, :], dn[:3, 1:2], None, ALU.mult, None, reads=['s_ec', 's_den'], writes=['s_pc'])
            pT, pTn = self.pb[0], 'pb0'
            for c in range(8):
                self.tr(pT[:, c * 4:c * 4 + 3], Pc[:3, c * 128:(c + 1) * 128], self.ident_b[:3, :3], reads=['s_pc', 'const'], writes=[pTn])
            PT8 = self.s_pt8
            self.cp(PT8[:, :, 0:3], pT[:, 0:32].rearrange("p (c q) -> p c q", q=4)[:, :, 0:3], reads=[pTn], writes=['s_pt8'], eng='act')
            pO, pOn = self.pf[2], 'pf2'
            for c in range(8):
                self.mm(pO[:3, :128], PT8[:, c, 0:3], cmpV[:, c, g * 128:(g + 1) * 128], c == 0, c == 7, reads=['s_pt8'] + cvn, writes=[pOn])
            acc = self.s_acc
            self.ts(acc[:3, :], pO[:3, :128], gs[0:3, 0:1], None, ALU.mult, None, reads=[pOn, 's_gs'], writes=['s_acc'])
            psum3 = self.s_psum
            self.tt(psum3[:, :], PT8[:, :, 0], PT8[:, :, 1], ALU.add, reads=['s_pt8'], writes=['s_psum'])
            self.tt(psum3[:, :], psum3[:, :], PT8[:, :, 2], ALU.add, reads=['s_pt8', 's_psum'], writes=['s_psum'])
            prep = self.s_prep
            self.cp(prep[:, :, :], psum3[:, :].unsqueeze(2).to_broadcast([128, 8, 3]), reads=['s_psum'], writes=['s_prep'], eng='dve')
            pI, pIn = self.pf[3], 'pf3'
            for c in range(8):
                self.mm(pI[:3, 0:264], prep[:, c, :], self.ovl_s[:, c, :], c == 0, c == 7, reads=['s_prep', 'sconst'], writes=[pIn])
            sc, sc2, m8 = self.s_sc, self.s_sc2, self.s_m8
            self.tt(sc[:3, :], pI[:3, 0:264], self.tk_s[:3, 0, :], ALU.mult, reads=[pIn, 'sconst'], writes=['s_sc'])
            self.tt(sc[:3, :], sc[:3, :], self.tk_s[:3, 1, :], ALU.add, reads=['s_sc', 'sconst'], writes=['s_sc'])
            P.op('dve', lambda e: e.max(m8[:3, 0:8], sc[:3, :]), reads=['s_sc'], writes=['s_m8'])
            P.op('dve', lambda e: e.match_replace(sc2[:3, :], m8[:3, 0:8], sc[:3, :], NEG), reads=['s_sc', 's_m8'], writes=['s_sc2'])
            P.op('dve', lambda e: e.max(m8[:3, 8:16], sc2[:3, :]), reads=['s_sc2'], writes=['s_m8'])
            selm = self.s_selm
            self.ts(selm[:3, :], sc[:3, :], m8[:3, 15:16], None, ALU.is_ge, None, reads=['s_sc', 's_m8'], writes=['s_selm'])
            pO, pOn = self.pf[2], 'pf2'
            dsel = self.s_dsel
            for kk in range(8):
                P.dma('sp', self.ksg[:, :], self.ksT_s_d[g, :, kk * 2048:(kk + 1) * 2048], reads=['ksT_s_d'], writes=['ksg'])
                P.dma('sp', self.vsg[:, :, :], self.vs_s_d[kk * 2048:(kk + 1) * 2048, g * 128:(g + 1) * 128].rearrange("(c p) d -> p c d", p=128),
                      reads=['vs_s_d'], writes=['vsg'])
                units = []
                for k4 in range(4):
                    kg = kk * 4 + k4
                    mfn = lambda kg=kg: selm[:3, 8 * kg:8 * kg + 8].unsqueeze(2).to_broadcast([3, 8, 64])
                    units.append(self.attn_unit(u, q3, qresl, 3, self.ksg[:, k4 * W:(k4 + 1) * W], 'ksg',
                                                lambda c, k4=k4: self.vsg[:, k4 * 4 + c, :], 'vsg', W, mfn, ['s_selm'],
                                                dsel[:3, kg:kg + 1], ('s_dsel', kg), pO, pOn, kg == 0, False, deferred=True))
                    u += 1
                self.run_pipelined(units)
            self.new_key_unit(u, q3, qres, self.s_knT[:, g:g + 1], self.s_kvb[0:1, 1536 + g * 128:1536 + (g + 1) * 128],
                              selm[:3, 256:257], ['s_selm'], dsel[:3, 32:33], ('s_dsel', 32), pO, pOn, False, True)
            u += 1
            self.act(self.s_sc2[:3, 0:33], dsel[:3, 0:33], AF.Copy, reads=[('s_dsel', k) for k in range(33)] + ['s_sc2'], writes=['s_sc2', 's_dsel2'], accum_out=dsel[:3, 36:37])
            self.recip(dsel[:3, 37:38], dsel[:3, 36:37], reads=['s_dsel2'], writes=['s_dsel2'])
            self.tt(dsel[:3, 37:38], dsel[:3, 37:38], gs[0:3, 1:2], ALU.mult, reads=['s_dsel2', 's_gs'], writes=['s_dsel2'])
            self.stt(acc[:3, :], pO[:3, :128], dsel[:3, 37:38], acc[:3, :], ALU.mult, ALU.add, reads=[pOn, 's_dsel2', 's_acc'], writes=['s_acc'])
            pO, pOn = self.pf[3], 'pf3'
            dw = self.s_dwin
            P.dma('sp', self.kwg[:, 0:512], self.kwT_s_d[g, :, :], reads=['kwT_s_d'], writes=['kwg'])
            P.dma('sp', self.vwg[:, 0:4, :], self.vw_s_d[:, g * 128:(g + 1) * 128].rearrange("(c p) d -> p c d", p=128), reads=['vw_s_d'], writes=['vwg'])
            self.attn_unit(u, q3, qres, 3, self.kwg[:, 0:W], 'kwg', lambda c: self.vwg[:, c, :], 'vwg', W,
                           lambda: self.wmask_s[:3, :], ['sconst'], dw[:3, 0:1], 's_dwin', pO, pOn, True, False)
            u += 1
            self.new_key_unit(u, q3, qres, self.s_knT[:, 4 + g:5 + g], self.s_kvb[0:1, 2560 + g * 128:2560 + (g + 1) * 128],
                              None, [], dw[:3, 1:2], 's_dwin', pO, pOn, False, True)
            u += 1
            self.tt(dw[:3, 2:3], dw[:3, 0:1], dw[:3, 1:2], ALU.add, reads=['s_dwin'], writes=['s_dwin2'])
            self.recip(dw[:3, 3:4], dw[:3, 2:3], reads=['s_dwin2'], writes=['s_dwin2'])
            self.tt(dw[:3, 3:4], dw[:3, 3:4], gs[0:3, 2:3], ALU.mult, reads=['s_dwin2', 's_gs'], writes=['s_dwin2'])
            mo = self.s_mo
            self.stt(mo[:3, :], pO[:3, :128], dw[:3, 3:4], acc[:3, :], ALU.mult, ALU.add, reads=[pOn, 's_dwin2', 's_acc'], writes=['s_mo'])
            pT, pTn = self.pb[1], 'pb1'
            self.tr(pT[:, 0:3], mo[:3, :], self.ident_b[:3, :3], reads=['s_mo', 'const'], writes=[pTn])
            self.cp(mixT[:, 3 * g:3 * g + 3, 0:1].rearrange("p h o -> p (h o)"), pT[:, 0:3], reads=[pTn],
                    writes=[(tag, 'mixT', 3 * g + r) for r in range(3)], eng='dve')

    def sample_path(self):
        P = self.P
        XR = self.st_p['xres']
        XT = self.st_p['xT']
        xrn = lambda a, b: [('p', 'xres', j) for j in range(a, b)]
        xtn = lambda a, b: [('p', 'xT', j) for j in range(a, b)]
        self.s_ptb = self.sb("s_ptb", [128, 128], I32)
        self.s_riota = self.sb("s_riota", [128, 1], F32)
        self.s_idx = self.sb("s_idx", [128, 128], I32)
        self.cmpKT_s = XT[:, 0:8, :].rearrange("p (g a) t -> p g (a t)", g=4)
        P.define('cmpKT_s', xtn(0, 8))
        self.s_kvb = XT[0:1, 8:14, :].rearrange("p c t -> p (c t)")
        P.define('s_kvb', xtn(8, 14))
        self.s_pc = XT[0:3, 14:16, :].rearrange("p c t -> p (c t)")
        P.define('s_pc', xtn(14, 16))
        self.s_kvf = XR[0:1, 0:6, :].rearrange("p c t -> p (c t)")
        P.define('s_kvf', xrn(0, 6))
        self.s_ec = XR[0:3, 6:8, :].rearrange("p c t -> p (c t)")
        P.define('s_ec', xrn(6, 8))
        self.ovl_s = XR[:, 8:11, :].rearrange("p c t -> p (c t)").bitcast(BF16)[:, 0:8 * 264].rearrange("p (c j) -> p c j", j=264)
        self.tk_s = XR[0:3, 11:13, :].rearrange("p c t -> p (c t)")[:, 0:528].rearrange("p (a j) -> p a j", j=264)
        self.cmask_s = XR[0:3, 13, :].bitcast(BF16)
        self.wmask_s = XR[0:3, 14, 0:256].bitcast(BF16)
        P.define('sconst', xrn(8, 15))
        self.cvst = [self.sb(f"cvst{i}", [32, 512], BF16) for i in range(2)]
        self.s_knT = self.sb("s_knT", [128, 8], BF16)
        self.s_e1 = self.sb("s_e1", [3, 2], F32)
        self.s_e1b = self.sb("s_e1b", [3, 2], BF16)
        self.s_pt1 = self.sb("s_pt1", [1, 4], BF16)
        self.s_gs = self.sb("s_gs", [3, 4], F32)
        self.s_den = self.sb("s_den", [3, 4], F32)
        self.s_pt8 = self.sb("s_pt8", [128, 8, 4], BF16)
        self.s_acc = self.sb("s_acc", [3, 128], F32)
        self.s_psum = self.sb("s_psum", [128, 8], F32)
        self.s_prep = self.sb("s_prep", [128, 8, 3], BF16)
        self.s_sc = self.sb("s_sc", [3, 264], F32)
        self.s_sc2 = self.sb("s_sc2", [3, 264], F32)
        self.s_m8 = self.sb("s_m8", [3, 16], F32)
        self.s_selm = self.sb("s_selm", [3, 264], BF16)
        self.s_dsel = self.sb("s_dsel", [3, 40], F32)
        self.s_dwin = self.sb("s_dwin", [3, 4], F32)
        self.s_mo = self.sb("s_mo", [3, 128], BF16)
        for nm in ('cmask_s', 'tk_s', 'wmask_s', 'ovl_s'):
            P.dma('sp', getattr(self, nm), self.ins[nm], writes=['sconst'])
        xres = self.sb("xres_s", [128, 16, 1], F32)
        xT = self.sb("xT_s", [128, 16, 1], BF16)
        mixT = self.sb("mixT_s", [128, 16, 1], BF16)
        hT = self.sb("hT_s", [128, NF, 1], BF16)
        qmT = self.sb("qmT_s", [128, 4, 1], BF16)
        qT = self.sb("qT_s", [128, 12, 1], BF16)
        gsig = self.sb("gsig_s", [1, 1, 36], F32)
        carry_mix = self.sb("carry_mix_s", [128, 2, 12, 2], F32)
        carry_ffn = self.sb("carry_ffn_s", [128, 4, NF, 2], F32)
        P.dma('sp', carry_mix[:], self.ins['st_mix'], writes=[('s', 'carry_mix', 0), ('s', 'carry_mix', 1)])
        P.dma('sp', carry_ffn[:], self.ins['st_ffn'], writes=[('s', 'carry_ffn', l) for l in range(4)])
        st = dict(xres=xres, xT=xT, mixT=mixT, hT=hT, qmT=qmT, qT=qT, ncols=1, tag='s', subw=[1], gsig=gsig,
                  memKT=self.st_p['memKT'], memV=self.st_p['memV'], memres='p_mem',
                  carry_mix=[carry_mix[:, l, :, :] for l in range(2)],
                  carry_ffn=[carry_ffn[:, l, :, :] for l in range(4)])
        P.dma('sp', xres[:, :, 0:1].rearrange("p c o -> p (c o)"), self.ins['xs'], writes=[('s', 'xres', j) for j in range(NCH)])
        self.cp(xT[:, :, :], xres[:, :, :], reads=[('s', 'xres', j) for j in range(NCH)], writes=[('s', 'xT', j) for j in range(NCH)], eng='dve')
        self.sample_mem_setup()
        self.sample_win_setup()
        self.sample_gather_pass()
        for l in range(2):
            self.load_mem(st, 1, l)
            self.a_mixer(st, l)
            self.mem_attention(st)
            self.layer_tail(st, l)
        self.sample_kv(st)
        for jj in range(2):
            l = 2 + jj
            self.load_mem(st, 1, l)
            self.b_inproj(st, jj)
            self.nsa_sample(st, jj)
            self.mem_attention(st)
            self.layer_tail(st, l)
        P.dma('sp', self.outs['ys'], xres[:, :, 0:1].rearrange("p c o -> p (c o)"), reads=[('s', 'xres', j) for j in range(NCH)])
        for l in range(2):
            P.dma('sp', self.outs['conv_mix_s'][l], carry_mix[:, l, :, :], reads=[('s', 'carry_mix', l)])
        for l in range(4):
            P.dma('sp', self.outs['conv_ffn_s'][l], carry_ffn[:, l, :, :], reads=[('s', 'carry_ffn', l)])


def _bf16(a):
    return np.asarray(a, dtype=np.float32).astype(ml_dtypes.bfloat16)


def make_consts():
    c = {}
    c['ident_f'] = np.eye(128, dtype=np.float32)
    c['ident_b'] = _bf16(np.eye(128, dtype=np.float32))
    c['ones_b'] = _bf16(np.full((128, 128), 1.0 / D, dtype=np.float32))
    return c


def make_inmap(inp, b, c, NT, do_sample=False):
    T = NT * W
    m = {}
    m['x'] = np.ascontiguousarray(inp['x_prompt'][b, :T])
    m['memp'] = np.ascontiguousarray(inp['mem_prompt'][b])
    for k in ('w_in_a', 'w_in_b', 'w_o', 'w_mem_kv', 'w_up', 'w_down', 'w_cmp'):
        m[k] = np.ascontiguousarray(inp[k])
    m['w_kv'] = np.ascontiguousarray(inp['w_kv_shared'])
    m['lng'] = np.ascontiguousarray(inp['ln_g'].reshape(4, 2, 16, 128).transpose(3, 0, 1, 2))
    m['lnb'] = np.ascontiguousarray(inp['ln_b'].reshape(4, 2, 16, 128).transpose(3, 0, 1, 2))
    m['mixcw'] = np.ascontiguousarray(inp['conv_a_w'].reshape(2, 3, 12, 128).transpose(3, 0, 2, 1))
    m['ffncw'] = np.ascontiguousarray(inp['ffn_conv_w'].reshape(4, 3, NF, 128).transpose(3, 0, 2, 1))
    m.update(make_consts())
    m.update(make_masks(NT))
    m.update(make_pos(inp))
    if do_sample:
        m.update(make_sample(inp, c))
    return m


def make_masks(NT):
    c = {}
    T = NT * W
    p = np.arange(128)
    nsl = T // 64
    cm = np.zeros((128, NT * 4, 128), np.float32)
    mul = np.zeros((128, NT * 4, 32), np.float32)
    add = np.zeros((128, NT * 4, 32), np.float32)
    elig = np.zeros((128, NT * 4, 32), np.float32)
    for it in range(NT):
        for s in range(4):
            qpos = it * W + s * 128 + p
            npr = np.arange(128)
            n = npr - 1
            valid = (npr[None, :] >= 1) & (16 * n[None, :] + 31 <= qpos[:, None]) & (npr[None, :] < 32 * (it + 1))
            cm[:, it * 4 + s, :] = valid
            blk = np.arange(32)
            cur = qpos // 64
            el = (blk[None, :] * 64 <= qpos[:, None]) & (blk[None, :] < nsl)
            forced = ((blk[None, :] == 0) | (blk[None, :] == cur[:, None]) | (blk[None, :] == cur[:, None] - 1)) & (blk[None, :] < nsl)
            mul[:, it * 4 + s, :] = (el & ~forced)
            add[:, it * 4 + s, :] = np.where(forced, 1e9, np.where(el, 0.0, -1e9))
            elig[:, it * 4 + s, :] = el
    c['cmask'] = _bf16(cm)
    c['tk_mul'] = _bf16(mul)
    c['tk_add'] = _bf16(add)
    c['tk_elig'] = _bf16(elig)
    col = np.arange(512)
    tri = np.zeros((128, 4, 512), np.float32)
    for s in range(4):
        tri[:, s, :] = col[None, :] <= (128 * s + p)[:, None]
    c['tri'] = _bf16(tri)
    c['wprev'] = _bf16(1.0 - tri)
    ovl = np.zeros((32, 4, 32), np.float32)
    for i2 in range(4):
        for m in range(32):
            n = 32 * i2 + m - 1
            if n < 0:
                continue
            for j in range(32):
                if (16 * n <= 64 * j + 63) and (16 * n + 31 >= 64 * j):
                    ovl[m, i2, j] = 1.0
    c['ovl'] = _bf16(ovl)
    return c


def make_pos(inp):
    c = {}
    cp = np.asarray(inp['cmp_pos'], np.float32)
    c['posT'] = _bf16(cp.transpose(2, 0, 1))
    c['posrep'] = _bf16(np.repeat(cp[1].T[:, :, None], 32, axis=2))
    return c


def make_sample(inp, c):
    m = {}
    m['xs'] = np.ascontiguousarray(inp['x_sample'][c, 0].reshape(16, 128).T)
    m['cache_kv'] = inp['cache_kv'].reshape(1280 * 128, 2048)
    m['page_tab'] = np.ascontiguousarray(inp['page_table'][c:c + 1]).astype(np.int32)
    m['riota'] = np.arange(128, dtype=np.float32).reshape(128, 1)
    m['cache_win'] = np.ascontiguousarray(inp['cache_win'][c].reshape(512, 1024))
    m['cache_mem'] = np.ascontiguousarray(inp['cache_mem'][:, c].reshape(4, NMEM, 1024))
    m['st_mix'] = np.ascontiguousarray(inp['state_conv_mix'][:, c].reshape(2, 2, 12, 128).transpose(3, 0, 2, 1))
    m['st_ffn'] = np.ascontiguousarray(inp['state_conv_ffn'][:, c].reshape(4, 2, NF, 128).transpose(3, 0, 2, 1))
    cm = np.ones((3, 1024), np.float32)
    cm[:, 0] = 0
    m['cmask_s'] = _bf16(cm)
    tk = np.zeros((3, 2, 264), np.float32)
    j = np.arange(264)
    forced = (j == 0) | (j == 255) | (j == 256)
    real = j <= 256
    tk[:, 0, :] = (real & ~forced)
    tk[:, 1, :] = np.where(forced, 1e9, np.where(real, 0.0, -1e9))
    m['tk_s'] = tk
    wm = np.ones((3, 512), np.float32)
    wm[:, 0] = 0
    m['wmask_s'] = _bf16(wm)
    ov = np.zeros((128, 8, 264), np.float32)
    npr = (np.arange(8)[None, :] * 128 + np.arange(128)[:, None])
    n = npr - 1
    jj = np.arange(264)
    o = (16 * n[:, :, None] <= 64 * jj[None, None, :] + 63) & (16 * n[:, :, None] + 31 >= 64 * jj[None, None, :]) \
        & (n[:, :, None] >= 0) & (jj[None, None, :] <= 256)
    m['ovl_s'] = _bf16(o.astype(np.float32))
    return m


_NC_CACHE = {}


def kernel(**inputs):
    inp = {k: np.asarray(v) for k, v in inputs.items()}
    NT = 4
    if 'nc' not in _NC_CACHE:
        bld = Builder(NT=NT, n_layers=4, upto='all', do_sample=True)
        _NC_CACHE['nc'] = bld.build()
    nc = _NC_CACHE['nc']
    in_maps = [make_inmap(inp, c // 2, c, NT, do_sample=True) for c in range(8)]
    res = run_bass_kernel_spmd(nc, in_maps, core_ids=list(range(8)))
    R = res.results
    f32 = np.float32

    def fm(a, n):
        return np.asarray(a).transpose(2, 1, 0).reshape(2, n * 128)
    y_prompt = np.stack([R[2 * b]['y'] for b in range(4)]).astype(f32)
    y_sample = np.stack([R[c]['ys'].T.reshape(1, D) for c in range(8)]).astype(f32)
    kv_rows_prompt = np.stack([R[2 * b]['kv_rows'].reshape(NT * W, 4, 4, 128) for b in range(4)]).astype(f32)
    win_prompt = np.stack([R[2 * b]['win'].reshape(512, 2, 4, 128) for b in range(4)]).astype(f32)
    mem_kv = np.stack([R[2 * b]['mem_kv'].reshape(4, NMEM, 2, 4, 128) for b in range(4)], axis=1).astype(f32)
    conv_mix_p = np.stack([np.stack([fm(R[2 * b]['conv_mix'][l], 12) for b in range(4)]) for l in range(2)]).astype(f32)
    conv_ffn_p = np.stack([np.stack([fm(R[2 * b]['conv_ffn'][l], NF) for b in range(4)]) for l in range(4)]).astype(f32)
    kv_rows_s = np.stack([R[c]['kv_rows_s'].reshape(1, 4, 4, 128) for c in range(8)]).astype(f32)
    win_s = np.stack([R[c]['win_s'].reshape(512, 2, 4, 128) for c in range(8)]).astype(f32)
    conv_mix_s = np.stack([np.stack([fm(R[c]['conv_mix_s'][l], 12) for c in range(8)]) for l in range(2)]).astype(f32)
    conv_ffn_s = np.stack([np.stack([fm(R[c]['conv_ffn_s'][l], NF) for c in range(8)]) for l in range(4)]).astype(f32)
    return (y_prompt, y_sample, kv_rows_prompt, win_prompt, mem_kv, conv_mix_p, conv_ffn_p,
            kv_rows_s, win_s, conv_mix_s, conv_ffn_s)
```
